# Optimizing a Trainium2 kernel written in Bass

```python
import jax, jax.numpy as jnp
from jax import lax
import numpy as np

D_MODEL = 2048
BATCH = 2
SEQ = 8192
DEPTH = 1
DEC_BATCH = 32
DEC_SEQ = 4
PAST_LEN = 16384
PAGE_SIZE = 128

HEAD_DIM = 128
N_HEADS = 8
GROUPS = ((128, 1), (512, 4), (2048, 16))
N_GROUPS = 3
N_OFFSETS = 129
MAX_OFFSET = N_OFFSETS - 1
BAND = 128
D_ATT = N_HEADS * HEAD_DIM
D_CONV = D_MODEL
CONV_WIDTH = 3
ROT_DIM = HEAD_DIM // 4
ROPE_THETA = 500000.0
EPS = 1e-6
SCALE = HEAD_DIM ** -0.5
NEG_INF = -1e30

QKV_COLS = N_GROUPS * 3 * D_ATT
OFF_AGATE = QKV_COLS
OFF_CONV = OFF_AGATE + D_ATT
OFF_CGATE = OFF_CONV + 3 * D_CONV
OFF_MERGE = OFF_CGATE + D_CONV
D_IN_TOTAL = OFF_MERGE + 2 * D_MODEL

kernel_name = "dilated_swa_shortconv_gated_hybrid_step"


def _rmsnorm(x, g):
    x32 = x.astype(jnp.float32)
    y = x32 * lax.rsqrt(jnp.mean(x32 * x32, axis=-1, keepdims=True) + EPS)
    return (y * g.astype(jnp.float32)).astype(x.dtype)


def _rope(x, pos):
    half = ROT_DIM // 2
    inv = ROPE_THETA ** (-jnp.arange(half, dtype=jnp.float32) * (2.0 / ROT_DIM))
    ang = pos[:, None] * inv[None, :]
    cos = jnp.cos(ang)[None, :, None, :]
    sin = jnp.sin(ang)[None, :, None, :]
    x32 = x.astype(jnp.float32)
    x1, x2, rest = x32[..., :half], x32[..., half:ROT_DIM], x32[..., ROT_DIM:]
    return jnp.concatenate([x1 * cos - x2 * sin, x2 * cos + x1 * sin, rest], axis=-1).astype(x.dtype)


def _softmax_stats(s):
    m = jnp.max(s, axis=-1, keepdims=True)
    p = jnp.exp(s - m)
    l = jnp.sum(p, axis=-1, keepdims=True)
    return p / l, (m + jnp.log(l))[..., 0]


def _mixer_inputs(x, pos, norm_w, w_in, q_norm_w, k_norm_w):
    b, s, _ = x.shape
    z = _rmsnorm(x, norm_w) @ w_in
    qkv = z[..., :QKV_COLS].reshape(b, s, N_GROUPS, 3, N_HEADS, HEAD_DIM)
    q = _rmsnorm(qkv[:, :, :, 0], q_norm_w[:, None, :]).reshape(b, s, N_GROUPS * N_HEADS, HEAD_DIM)
    k = _rmsnorm(qkv[:, :, :, 1], k_norm_w[:, None, :]).reshape(b, s, N_GROUPS * N_HEADS, HEAD_DIM)
    q, k = _rope(q, pos), _rope(k, pos)
    v = qkv[:, :, :, 2]
    qs = [q[:, :, g * N_HEADS:(g + 1) * N_HEADS] for g in range(N_GROUPS)]
    ks = [k[:, :, g * N_HEADS:(g + 1) * N_HEADS] for g in range(N_GROUPS)]
    vs = [v[:, :, g] for g in range(N_GROUPS)]
    a_gate = z[..., OFF_AGATE:OFF_CONV]
    h_c = z[..., OFF_CONV:OFF_CONV + D_CONV]
    b_c = z[..., OFF_CONV + D_CONV:OFF_CONV + 2 * D_CONV]
    c_c = z[..., OFF_CONV + 2 * D_CONV:OFF_CGATE]
    c_gate = z[..., OFF_CGATE:OFF_MERGE]
    merge_logits = z[..., OFF_MERGE:]
    return qs, ks, vs, a_gate, h_c, b_c, c_c, c_gate, merge_logits


def _dilated_band_attention(q, k, v, dilation):
    b, s, h, dh = q.shape
    L = s // dilation
    nb = -(-L // BAND)
    lp = nb * BAND
    n = b * dilation

    def to_sub(x):
        x = x.reshape(b, L, dilation, h, dh).transpose(0, 2, 1, 3, 4).reshape(n, L, h, dh)
        return jnp.pad(x, ((0, 0), (0, lp - L), (0, 0), (0, 0)))

    def band_keys(x):
        xp = jnp.pad(to_sub(x), ((0, 0), (BAND, 0), (0, 0), (0, 0)))
        prev = xp[:, :lp].reshape(n, nb, BAND, h, dh)
        cur = xp[:, BAND:].reshape(n, nb, BAND, h, dh)
        return jnp.concatenate([prev, cur], axis=2)

    qb = to_sub(q).reshape(n, nb, BAND, h, dh)
    kb, vb = band_keys(k), band_keys(v)
    sc = jnp.einsum('nbqhd,nbkhd->nbhqk', qb, kb).astype(jnp.float32) * SCALE
    qi = jnp.arange(BAND)[:, None]
    kj = jnp.arange(2 * BAND)[None, :]
    dist = qi + BAND - kj
    key_idx = jnp.arange(nb)[:, None, None] * BAND - BAND + kj[None]
    mask = ((dist >= 0) & (dist <= MAX_OFFSET))[None] & (key_idx >= 0)
    sc = jnp.where(mask[None, :, None], sc, NEG_INF)
    p, lse = _softmax_stats(sc)
    o = jnp.einsum('nbhqk,nbkhd->nbqhd', p.astype(v.dtype), vb)
    o = o.reshape(n, lp, h, dh)[:, :L].reshape(b, dilation, L, h, dh).transpose(0, 2, 1, 3, 4).reshape(b, s, h, dh)
    lse = lse.transpose(0, 1, 3, 2).reshape(n, lp, h)[:, :L].reshape(b, dilation, L, h).transpose(0, 2, 1, 3).reshape(b, s, h)
    return o, lse


def _dilated_gather_attention(q, k_ext, v_ext, dilation):
    t = q.shape[1]
    lb = k_ext.shape[1] - t
    idx = lb + jnp.arange(t)[:, None] - jnp.arange(N_OFFSETS)[None, :] * dilation
    valid = idx >= 0
    idx = jnp.maximum(idx, 0)
    kg = k_ext[:, idx]
    vg = v_ext[:, idx]
    sc = jnp.einsum('bthd,btjhd->bthj', q, kg).astype(jnp.float32) * SCALE
    sc = jnp.where(valid[None, :, None, :], sc, NEG_INF)
    p, lse = _softmax_stats(sc)
    o = jnp.einsum('bthj,btjhd->bthd', p.astype(v_ext.dtype), vg)
    return o, lse


def _combine_groups(outs, lses):
    w = jax.nn.softmax(jnp.stack(lses, axis=0), axis=0)
    return jnp.einsum('gbsh,gbshd->bshd', w.astype(outs[0].dtype), jnp.stack(outs, axis=0))


def _short_conv(u_ext, w):
    L = u_ext.shape[1] - (CONV_WIDTH - 1)
    return sum(w[i] * u_ext[:, i:i + L] for i in range(CONV_WIDTH))


def _merge(x, o_att, a_gate, conv_y, c_gate, merge_logits, w_att_proj, w_conv_proj, w_out):
    b, s, _ = x.shape
    a = (o_att.reshape(b, s, D_ATT) * jax.nn.silu(a_gate)) @ w_att_proj
    c = (conv_y * jax.nn.silu(c_gate)) @ w_conv_proj
    m = jax.nn.sigmoid(merge_logits[..., :D_MODEL]) * a + jax.nn.sigmoid(merge_logits[..., D_MODEL:]) * c
    return x + m @ w_out


def setup_inputs(seed: int = 0) -> dict:
    key = jax.random.key(seed)
    ks = jax.random.split(key, 16)
    f32 = jnp.float32

    def nrm(k, shape, scale):
        return jax.random.normal(k, shape, f32) * scale

    def kv_cache(k, window):
        return nrm(k, (DEPTH, DEC_BATCH, min(window, PAST_LEN), 2, N_HEADS, HEAD_DIM), 1.0)

    return {
        "x_prompt": nrm(ks[0], (BATCH, SEQ, D_MODEL), 1.0),
        "x_sample": nrm(ks[1], (DEC_BATCH, DEC_SEQ, D_MODEL), 1.0),
        "cache_kv_w128": kv_cache(ks[2], GROUPS[0][0]),
        "cache_kv_w512": kv_cache(ks[3], GROUPS[1][0]),
        "cache_kv_w2048": kv_cache(ks[4], GROUPS[2][0]),
        "state_conv": nrm(ks[5], (DEPTH, DEC_BATCH, CONV_WIDTH - 1, D_CONV), 1.0),
        "norm_w": 1.0 + nrm(ks[6], (DEPTH, D_MODEL), 0.1),
        "w_in": nrm(ks[7], (DEPTH, D_MODEL, D_IN_TOTAL), D_MODEL ** -0.5),
        "q_norm_w": 1.0 + nrm(ks[8], (DEPTH, N_GROUPS, HEAD_DIM), 0.1),
        "k_norm_w": 1.0 + nrm(ks[9], (DEPTH, N_GROUPS, HEAD_DIM), 0.1),
        "conv_w": nrm(ks[10], (DEPTH, CONV_WIDTH, D_CONV), CONV_WIDTH ** -0.5),
        "w_att_proj": nrm(ks[11], (DEPTH, D_ATT, D_MODEL), D_ATT ** -0.5),
        "w_conv_proj": nrm(ks[12], (DEPTH, D_CONV, D_MODEL), D_CONV ** -0.5),
        "w_out": nrm(ks[13], (DEPTH, D_MODEL, D_MODEL), D_MODEL ** -0.5),
    }


def reference(x_prompt, x_sample, cache_kv_w128, cache_kv_w512, cache_kv_w2048, state_conv,
              norm_w, w_in, q_norm_w, k_norm_w, conv_w, w_att_proj, w_conv_proj, w_out):
    caches = (cache_kv_w128, cache_kv_w512, cache_kv_w2048)
    s_p = x_prompt.shape[1]
    t_s = x_sample.shape[1]
    pos_p = jnp.arange(s_p, dtype=jnp.float32)
    pos_s = PAST_LEN + jnp.arange(t_s, dtype=jnp.float32)

    yp, ys = x_prompt, x_sample
    kv_p = [[] for _ in GROUPS]
    kv_s = [[] for _ in GROUPS]
    conv_p, conv_s = [], []
    for l in range(DEPTH):
        lw = (norm_w[l], w_in[l], q_norm_w[l], k_norm_w[l])
        pw = (w_att_proj[l], w_conv_proj[l], w_out[l])

        qs, ks, vs, a_gate, h_c, b_c, c_c, c_gate, mlog = _mixer_inputs(yp, pos_p, *lw)
        outs, lses = [], []
        for g, (window, dil) in enumerate(GROUPS):
            o, lse = _dilated_band_attention(qs[g], ks[g], vs[g], dil)
            outs.append(o)
            lses.append(lse)
            keep = min(window, s_p)
            kv_p[g].append(jnp.stack([ks[g][:, -keep:], vs[g][:, -keep:]], axis=2))
        o_att = _combine_groups(outs, lses)
        u_ext = jnp.pad(c_c * h_c, ((0, 0), (CONV_WIDTH - 1, 0), (0, 0)))
        conv_y = b_c * _short_conv(u_ext, conv_w[l])
        conv_p.append(u_ext[:, -(CONV_WIDTH - 1):])
        yp_next = _merge(yp, o_att, a_gate, conv_y, c_gate, mlog, *pw)

        qs, ks, vs, a_gate, h_c, b_c, c_c, c_gate, mlog = _mixer_inputs(ys, pos_s, *lw)
        outs, lses = [], []
        for g, (window, dil) in enumerate(GROUPS):
            buf = caches[g][l]
            k_ext = jnp.concatenate([buf[:, :, 0], ks[g]], axis=1)
            v_ext = jnp.concatenate([buf[:, :, 1], vs[g]], axis=1)
            o, lse = _dilated_gather_attention(qs[g], k_ext, v_ext, dil)
            outs.append(o)
            lses.append(lse)
            keep = min(window, buf.shape[1] + t_s)
            kv_s[g].append(jnp.stack([k_ext[:, -keep:], v_ext[:, -keep:]], axis=2))
        o_att = _combine_groups(outs, lses)
        u_ext = jnp.concatenate([state_conv[l], c_c * h_c], axis=1)
        conv_y = b_c * _short_conv(u_ext, conv_w[l])
        conv_s.append(u_ext[:, -(CONV_WIDTH - 1):])
        ys_next = _merge(ys, o_att, a_gate, conv_y, c_gate, mlog, *pw)

        yp, ys = yp_next, ys_next

    new_kv_w128_prompt = jnp.stack(kv_p[0], axis=0)
    new_kv_w512_prompt = jnp.stack(kv_p[1], axis=0)
    new_kv_w2048_prompt = jnp.stack(kv_p[2], axis=0)
    new_conv_prompt = jnp.stack(conv_p, axis=0)
    new_kv_w128_sample = jnp.stack(kv_s[0], axis=0)
    new_kv_w512_sample = jnp.stack(kv_s[1], axis=0)
    new_kv_w2048_sample = jnp.stack(kv_s[2], axis=0)
    new_conv_sample = jnp.stack(conv_s, axis=0)
    return (yp, ys, new_kv_w128_prompt, new_kv_w512_prompt, new_kv_w2048_prompt, new_conv_prompt,
            new_kv_w128_sample, new_kv_w512_sample, new_kv_w2048_sample, new_conv_sample)
```

```python
import numpy as np
from contextlib import ExitStack
import concourse.bass as bass
import concourse.mybir as mybir
from concourse.bass_utils import run_bass_kernel_spmd

F32 = mybir.dt.float32
BF16 = mybir.dt.bfloat16
AF = mybir.ActivationFunctionType
ALU = mybir.AluOpType

NCORES = 8
D_MODEL = 2048
SEQ = 8192
PAST_LEN = 16384
HD = 128
NH = 8
DILS = (1, 4, 16)
D_ATT = 1024
QKV_COLS = 9216
OFF_AGATE = QKV_COLS
OFF_CONV = OFF_AGATE + D_ATT
OFF_CGATE = OFF_CONV + 3 * 2048
OFF_MERGE = OFF_CGATE + 2048
EPS = 1e-6
SCALE = HD ** -0.5
TOK = 2048
NSAMP = 16
ROPE_BASE = (0, 17, 37)
ROPE_SAMPLE = 69
CACHE_LEN = (128, 512, 2048)


class Buf:
    __slots__ = ("name", "w", "r", "wsem", "wcnt", "rsem", "rcnt")

    def __init__(self, name):
        self.name = name
        self.w = None
        self.r = {}
        self.wsem = None
        self.wcnt = 0
        self.rsem = None
        self.rcnt = 0


class Trk:
    def __init__(self, nc, stack):
        self.nc = nc
        self.stack = stack
        self.eng = {"pe": nc.tensor, "act": nc.scalar, "dve": nc.vector, "pool": nc.gpsimd, "sp": nc.sync}
        self.done = {}
        self.cnt = {}
        self.sems = {}
        for k in ("pe", "act", "dve", "pool"):
            self.done[k] = stack.enter_context(nc.semaphore("done_" + k))
            self.cnt[k] = 0
            self.sems[("done", k)] = self.done[k]
        self.seen = {k: {} for k in self.eng}
        self.nsem = 0
        self.bufs = []
        self.semmax = {}

    def buf(self, name):
        b = Buf(name)
        self.bufs.append(b)
        return b

    def bufs_n(self, name, n):
        return [self.buf("%s%d" % (name, i)) for i in range(n)]

    def _newsem(self, name):
        self.nsem += 1
        h = self.stack.enter_context(self.nc.semaphore("%s_%d" % (name, self.nsem)))
        key = ("dma", self.nsem)
        self.sems[key] = h
        return key

    def _wait(self, e, key, val):
        if self.seen[e].get(key, 0) >= val:
            return
        self.seen[e][key] = val
        self.eng[e].wait_ge(self.sems[key], val)

    def _deps(self, e, reads, writes):
        for b in reads:
            if b.w is not None:
                self._wait(e, b.w[0], b.w[1])
        for b in writes:
            if b.w is not None:
                k, v, we = b.w
                if we != e:
                    self._wait(e, k, v)
            for k, (v, re) in b.r.items():
                if re != e:
                    self._wait(e, k, v)

    def op(self, e, fn, reads=(), writes=()):
        if _MUTE[0]:
            return None
        self._deps(e, reads, writes)
        ins = fn(self.eng[e])
        self.cnt[e] += 1
        ins.then_inc(self.done[e], 1)
        key = ("done", e)
        ev = self.cnt[e]
        for b in reads:
            b.r[key] = (ev, e)
        for b in writes:
            b.w = (key, ev, e)
            b.r = {}
        return ins

    def dma(self, q, out, in_, reads=(), writes=(), nobarrier=False):
        if _MUTE[0]:
            return None
        self._deps("dq_" + q if False else q, reads, writes)
        if writes:
            b = writes[0]
            if b.wsem is None:
                b.wsem = self._newsem("w")
            b.wcnt += 1
            key, val = b.wsem, 16 * b.wcnt
        else:
            b = reads[0]
            if b.rsem is None:
                b.rsem = self._newsem("r")
            b.rcnt += 1
            key, val = b.rsem, 16 * b.rcnt
        ins = self.eng[q].dma_start(out=out, in_=in_)
        ins.then_inc(self.sems[key], 16)
        if not nobarrier:
            self.semmax[key] = max(self.semmax.get(key, 0), val)
        tag = "dma"
        for b2 in reads:
            b2.r[key] = (val, tag)
        for b2 in writes:
            b2.w = (key, val, tag)
            b2.r = {}
        return ins

    def barrier(self):
        if _MUTE[0]:
            return
        tg = [(("done", k), self.cnt[k]) for k in self.cnt if self.cnt[k] > 0] + list(self.semmax.items())
        for e in self.eng:
            for key, val in tg:
                if not (key == ("done", e)):
                    self._wait(e, key, val)

    def finish(self, e="sp"):
        for b in self.bufs:
            if b.w is not None:
                self._wait(e, b.w[0], b.w[1])
            for k, (v, _) in b.r.items():
                self._wait(e, k, v)


class _Stop(Exception):
    pass


_STOP = [None]


_MUTE = [False]
_DEBUG = [False]


def _stage(name):
    if _STOP[0] == name:
        _MUTE[0] = True


def bmid(ap, n):
    a = ap.ap
    return bass.AP(ap.tensor, ap.offset, [list(a[0]), [0, n]] + [list(x) for x in a[1:]])


def build_program():
    _MUTE[0] = False
    nc = bass.Bass("TRN2", target_bir_lowering=False)

    def din(name, shape, dt=F32):
        return nc.dram_tensor(name, list(shape), dt, kind="ExternalInput").ap()

    def dout(name, shape, dt=F32):
        return nc.dram_tensor(name, list(shape), dt, kind="ExternalOutput").ap()

    def dscr(name, shape, dt):
        return nc.dram_tensor(name, list(shape), dt).ap()

    x = din("x", [4096 + NSAMP, D_MODEL])
    ck = [din("ck%d" % g, [4, CACHE_LEN[g], 2, 1024]) for g in range(3)]
    scT = din("scT", [128, 16, 4, 2])
    wqkv = din("wqkv", [NH, 3, 128, 16, 384])
    wag = din("wag", [NH, 128, 16, 128])
    wcv = din("wcv", [16, 4, 128, 16, 128])
    wml = din("wml", [16, 2, 128, 16, 128])
    watt = din("watt", [16, 128, 8, 128])
    wcp = din("wcp", [16, 128, 16, 128])
    wout = din("wout", [4, 128, 16, 512])
    nw_d = din("nw", [128, D_MODEL])
    qkw_d = din("qkw", [128, 3, 2, 128])
    cw_d = din("cw", [128, 16, 3])
    rope_d = din("rope", [128, 70, 48])
    masks_d = din("masks", [128, 2, 256])
    smask_d = din("smask", [128, 36, 16])
    nmask_d = din("nmask", [16, 3, 16])
    ident_d = din("ident", [128, 128])

    y_o = dout("y", [TOK + NSAMP, D_MODEL])
    kv_o = [dout("kv%d" % g, [CACHE_LEN[g], 2, 1024]) for g in range(3)]
    ncv_o = dout("ncv", [128, 16, 2])
    skv_o = [dout("skv%d" % g, [4, CACHE_LEN[g], 2, 1024]) for g in range(3)]
    sncv_o = dout("sncv", [128, 16, 4, 2])

    hK = (dout if _DEBUG[0] else dscr)("hK", [3, NH, 128, 2048], BF16)
    hV = (dout if _DEBUG[0] else dscr)("hV", [3, NH, 128, 16, 128], BF16)
    t2s = (dout if _DEBUG[0] else dscr)("t2s", [16, 128, TOK + NSAMP], BF16)

    with ExitStack() as st:
        T = Trk(nc, st)
        sE = None
        try:

            def sb(name, shape, dt=F32):
                return st.enter_context(nc.sbuf_tensor("s_s_" + name, list(shape), dt))

            PS = [st.enter_context(nc.psum_tensor("ps%d" % i, [128, 512], F32)) for i in range(8)]
            PSB = T.bufs_n("ps", 8)
            bank_rr = [0]

            def nextbank():
                i = bank_rr[0] % 8
                bank_rr[0] += 1
                return PS[i], PSB[i]

            xnT = sb("xnT", [128, 16, TOK], BF16); b_xnT = T.buf("xnT")
            xnS = sb("xnS", [128, 16, 18], BF16); b_xnS = T.buf("xnS")
            b_agT = T.buf("agT")

            cw = sb("cw", [128, 16, 3]); b_cw = T.buf("cw")
            scTs = sb("scTs", [128, 16, 4, 2]); b_scT = T.buf("scT")
            sE = ExitStack()

            def sbe(name, shape, dt=F32):
                return sE.enter_context(nc.sbuf_tensor("s_e_" + name, list(shape), dt))

            qkw = sbe("qkw", [128, 3, 2, 128]); b_qkw = T.buf("qkw")
            rope = sbe("rope", [128, 70, 48]); b_rope = T.buf("rope")
            masks = sbe("masks", [128, 2, 256], BF16); b_masks = T.buf("masks")
            smask = sbe("smask", [128, 36, 16], BF16); b_smask = T.buf("smask")
            nmask = sbe("nmask", [16, 3, 16], BF16); b_nmask = T.buf("nmask")
            identb = sbe("identb", [128, 128], BF16); b_identb = T.buf("identb")
            identf = sbe("identf", [128, 128], F32); b_identf = T.buf("identf")
            ones = sbe("ones", [128, 128], BF16); b_ones = T.buf("ones")
            T.dma("pool", qkw[:], qkw_d, writes=[b_qkw])
            T.dma("pool", cw[:], cw_d, writes=[b_cw])
            T.dma("pool", rope[:], rope_d, writes=[b_rope])
            T.dma("pool", masks[:], masks_d, writes=[b_masks])
            T.dma("pool", smask[:], smask_d, writes=[b_smask])
            T.dma("pool", nmask[:], nmask_d, writes=[b_nmask])
            T.dma("pool", identb[:], ident_d, writes=[b_identb])
            T.dma("pool", identf[:], ident_d, writes=[b_identf])
            T.dma("pool", scTs[:], scT, writes=[b_scT])
            T.op("dve", lambda e: e.memset(ones[:], 1.0), writes=[b_ones])

            def post_block(W, pz, bpz, rows, g, tile, has_q, kT_dst, q_dst, v_dst, kv_out):
                i = W["i"]; W["i"] += 1
                s = i % WD
                qk, bqk = W["qk"][s], W["bqk"][s]
                junk, bjunk = W["junk"], W["bjunk"]
                ss, bss = W["ss"][s], W["bss"][s]
                tmp, btmp = W["tmp"][s], W["btmp"][s]
                qkb, bqkb = W["qkb"][s], W["bqkb"][s]
                vf, bvf = W["vf"][s], W["bvf"][s]
                ptr, bptr = W["ptr"], W["bptr"]
                o0 = 0 if has_q else -128
                slots = ([0] if has_q else []) + [1]
                for sl in slots:
                    c0 = o0 + sl * 128
                    T.op("act", lambda e: e.activation(out=junk[:rows, :], in_=pz[:rows, c0:c0 + 128], func=AF.Square,
                                                       accum_out=ss[:rows, sl:sl + 1]),
                         reads=[bpz], writes=[bjunk, bss])
                if not has_q:
                    T.op("dve", lambda e: e.memset(ss[:rows, 0:1], 1.0), writes=[bss])
                T.op("act", lambda e: e.activation(out=ss[:rows, :], in_=ss[:rows, :], func=AF.Ln, scale=1.0 / HD, bias=W["eps"][:rows, 0:1]),
                     reads=[bss, W["beps"]], writes=[bss])
                T.op("act", lambda e: e.activation(out=ss[:rows, :], in_=ss[:rows, :], func=AF.Exp, scale=-0.5),
                     reads=[bss], writes=[bss])
                for sl in slots:
                    c0 = o0 + sl * 128
                    T.op("dve", lambda e: e.scalar_tensor_tensor(out=qk[:rows, sl, :], in0=pz[:rows, c0:c0 + 128],
                                                                 scalar=ss[:rows, sl:sl + 1], in1=qkw[:rows, g, sl, :],
                                                                 op0=ALU.mult, op1=ALU.mult),
                         reads=[bpz, bss, b_qkw], writes=[bqk])
                if not has_q:
                    T.op("dve", lambda e: e.memset(qk[:rows, 0, :], 0.0), writes=[bqk])
                X = qk[:rows, :, 0:32]
                CS = bmid(rope[:rows, tile, 0:32], 2)
                SC = bmid(rope[:rows, tile, 16:48], 2)
                T.op("dve", lambda e: e.tensor_tensor(out=tmp[:rows, 0], in0=X, in1=CS, op=ALU.mult),
                     reads=[bqk, b_rope], writes=[btmp])
                T.op("dve", lambda e: e.tensor_tensor(out=tmp[:rows, 1], in0=X, in1=SC, op=ALU.mult),
                     reads=[bqk, b_rope], writes=[btmp])
                T.op("dve", lambda e: e.tensor_tensor(out=qk[:rows, :, 0:16], in0=tmp[:rows, 0, :, 0:16], in1=tmp[:rows, 0, :, 16:32],
                                                      op=ALU.subtract), reads=[btmp], writes=[bqk])
                T.op("dve", lambda e: e.tensor_tensor(out=qk[:rows, :, 16:32], in0=tmp[:rows, 1, :, 0:16], in1=tmp[:rows, 1, :, 16:32],
                                                      op=ALU.add), reads=[btmp], writes=[bqk])
                T.op("act", lambda e: e.activation(out=qkb[:rows], in_=qk[:rows], func=AF.Copy), reads=[bqk], writes=[bqkb])
                vc = o0 + 256
                T.op("dve", lambda e: e.tensor_copy(out=v_dst, in_=pz[:rows, vc:vc + 128]), reads=[bpz], writes=[W["bv_dst"]])
                if kv_out is not None:
                    T.op("act", lambda e: e.activation(out=vf[:rows, :], in_=pz[:rows, vc:vc + 128], func=AF.Copy),
                         reads=[bpz], writes=[bvf])
                    for (kd, vd, p0, p1) in kv_out:
                        T.dma("sp", kd, qk[p0:p1, 1, :], reads=[bqk])
                        T.dma("sp", vd, vf[p0:p1, :], reads=[bvf])
                for sl in slots:
                    T.op("pe", lambda e: e.transpose(out=ptr[:, sl * 128: sl * 128 + rows], in_=qkb[:rows, sl, :],
                                                     identity=identb[:rows, :rows]),
                         reads=[bqkb, b_identb], writes=[bptr])
                if has_q:
                    T.op("act", lambda e: e.activation(out=q_dst, in_=ptr[:, 0:rows], func=AF.Copy), reads=[bptr], writes=[W["bq_dst"]])
                T.op("act", lambda e: e.activation(out=kT_dst, in_=ptr[:, 128:128 + rows], func=AF.Copy), reads=[bptr], writes=[W["bk_dst"]])

            with ExitStack() as s1:
                def sb1(name, shape, dt=F32):
                    return s1.enter_context(nc.sbuf_tensor("s_s_" + name, list(shape), dt))

                WD = 3
                W = {"i": 0}
                W["qk"] = [sbe("qk%d" % i, [128, 2, 128]) for i in range(WD)]; W["bqk"] = T.bufs_n("qk", WD)
                W["junk"] = sbe("junk", [128, 128], BF16); W["bjunk"] = T.buf("junk")
                W["ss"] = [sbe("ss%d" % i, [128, 2]) for i in range(WD)]; W["bss"] = T.bufs_n("ss", WD)
                W["tmp"] = [sbe("tmp%d" % i, [128, 2, 2, 32]) for i in range(WD)]; W["btmp"] = T.bufs_n("tmp", WD)
                W["qkb"] = [sbe("qkb%d" % i, [128, 2, 128], BF16) for i in range(WD)]; W["bqkb"] = T.bufs_n("qkb", WD)
                W["vf"] = [sbe("vf%d" % i, [128, 128]) for i in range(WD)]; W["bvf"] = T.bufs_n("vf", WD)
                W["eps"] = sbe("epsc", [128, 1]); W["beps"] = T.buf("eps")
                T.op("dve", lambda e: e.memset(W["eps"][:], EPS), writes=[W["beps"]])
                W["bkvout"] = T.buf("kvout")
                ptr_ap = PS[2][:].bitcast(BF16)
                W["ptr"] = ptr_ap
                W["bptr"] = PSB[2]

                with ExitStack() as sA:
                    xnH = s1.enter_context(nc.sbuf_tensor("s_xnH", [128, 16, 2048], BF16)); b_xnH = T.buf("xnH")
                    xt = [sA.enter_context(nc.sbuf_tensor("s_xt%d" % i, [128, D_MODEL], F32)) for i in range(2)]
                    bxt = T.bufs_n("xt", 2)
                    xnb = [sA.enter_context(nc.sbuf_tensor("s_xnb%d" % i, [128, D_MODEL], BF16)) for i in range(2)]
                    bxnb = T.bufs_n("xnb", 2)
                    nw = sA.enter_context(nc.sbuf_tensor("s_nw", [128, D_MODEL], F32)); b_nw = T.buf("nw")
                    T.dma("pool", nw[:], nw_d, writes=[b_nw])
                    junkA = sA.enter_context(nc.sbuf_tensor("s_junkA", [128, D_MODEL], BF16)); bjunkA = T.buf("junkA")
                    ssA = [sA.enter_context(nc.sbuf_tensor("s_ssA%d" % i, [128, 1], F32)) for i in range(2)]
                    bssA = T.bufs_n("ssA", 2)
                    trA = [PS[0][:].bitcast(BF16), PS[1][:].bitcast(BF16)]
                    btrA = [PSB[0], PSB[1]]
                    for ti in range(33):
                        rows = 128 if ti < 32 else NSAMP
                        s = ti % 2
                        T.dma("sp", xt[s][:rows, :], x[ti * 128: ti * 128 + rows, :], writes=[bxt[s]])
                        T.op("act", lambda e: e.activation(out=junkA[:rows, :], in_=xt[s][:rows, :], func=AF.Square,
                                                           accum_out=ssA[s][:rows, 0:1]),
                             reads=[bxt[s]], writes=[bjunkA, bssA[s]])
                        T.op("act", lambda e: e.activation(out=ssA[s][:rows, :], in_=ssA[s][:rows, :], func=AF.Ln,
                                                           scale=1.0 / D_MODEL, bias=W["eps"][:rows, 0:1]),
                             reads=[bssA[s], W["beps"]], writes=[bssA[s]])
                        T.op("act", lambda e: e.activation(out=ssA[s][:rows, :], in_=ssA[s][:rows, :], func=AF.Exp, scale=-0.5),
                             reads=[bssA[s]], writes=[bssA[s]])
                        T.op("dve", lambda e: e.scalar_tensor_tensor(out=xnb[s][:rows, :], in0=xt[s][:rows, :],
                                                                     scalar=ssA[s][:rows, 0:1], in1=nw[:rows, :],
                                                                     op0=ALU.mult, op1=ALU.mult),
                             reads=[bxt[s], bssA[s], b_nw], writes=[bxnb[s]])
                        for half in range(2):
                            tr, btr = trA[half], btrA[half]
                            for k in range(8):
                                kc = half * 8 + k
                                T.op("pe", lambda e: e.transpose(out=tr[:, k * 128: k * 128 + rows],
                                                                 in_=xnb[s][:rows, kc * 128:(kc + 1) * 128],
                                                                 identity=identb[:rows, :rows]),
                                     reads=[bxnb[s], b_identb], writes=[btr])
                            src = tr.rearrange("p (k t) -> p k t", k=8)[:, :, 0:rows]
                            eng = "act" if half == 0 else "dve"
                            if ti < 16:
                                dst, bd = xnH[:, half * 8:(half + 1) * 8, ti * 128: ti * 128 + rows], b_xnH
                            elif ti < 32:
                                dst, bd = xnT[:, half * 8:(half + 1) * 8, (ti - 16) * 128:(ti - 16) * 128 + rows], b_xnT
                            else:
                                dst, bd = xnS[:, half * 8:(half + 1) * 8, 2:18], b_xnS
                            if eng == "act":
                                T.op("act", lambda e: e.activation(out=dst, in_=src, func=AF.Copy), reads=[btr], writes=[bd])
                            else:
                                T.op("dve", lambda e: e.tensor_copy(out=dst, in_=src), reads=[btr], writes=[bd])
                    for g in range(3):
                        L = CACHE_LEN[g]
                        T.dma("act", skv_o[g][:, 0:L - 4], ck[g][:, 4:L], writes=[T.buf("skvc%d" % g)], nobarrier=True)
                    T.op("dve", lambda e: e.tensor_copy(out=xnS[:, :, 0:2], in_=xnH[:, :, 2046:2048]), reads=[b_xnH], writes=[b_xnS])

                T.barrier()
                _stage("A")
                PZ = [(PS[0], PSB[0]), (PS[1], PSB[1]), (PS[7], PSB[7])]
                pzc = [0]
                LA = 2
                _bhk = T.buf("hK"); b_hK = [[_bhk] * NH for g in range(3)]
                _bhv = T.buf("hV"); b_hV = [[_bhv] * NH for g in range(3)]
                with ExitStack() as s0:
                    wkv = [s0.enter_context(nc.sbuf_tensor("s_wkv%d" % i, [128, 16, 256], BF16)) for i in range(2)]
                    bwkv = T.bufs_n("wkv", 2)
                    kst = [s0.enter_context(nc.sbuf_tensor("s_kst%d" % i, [128, 2048], BF16)) for i in range(2)]
                    bkst = T.bufs_n("kst", 2)
                    vst = [s0.enter_context(nc.sbuf_tensor("s_vst%d" % i, [128, 16, 128], BF16)) for i in range(2)]
                    bvst = T.bufs_n("vst", 2)
                    u = 0
                    for h in range(NH):
                        for g in range(3):
                            d = DILS[g]
                            s = u % 2
                            u += 1
                            T.dma("pool", wkv[s][:], wqkv[h, g, :, :, 128:384], writes=[bwkv[s]])
                            W["bk_dst"] = bkst[s]; W["bv_dst"] = bvst[s]; W["bq_dst"] = None
                            def s0_proj(r):
                                pz, bpz = PZ[pzc[0] % 3]
                                pzc[0] += 1
                                start = 2048 - 128 * d + r
                                for kc in range(16):
                                    T.op("pe", lambda e: e.matmul(pz[:, 0:256], xnH[:, kc, start:start + 127 * d + 1:d],
                                                                  wkv[s][:, kc, :], start=(kc == 0), stop=(kc == 15)),
                                         reads=[b_xnH, bwkv[s]], writes=[bpz])
                                return pz, bpz
                            pend = {}
                            for n in range(d + LA):
                                if n < d:
                                    pend[n] = s0_proj(n)
                                if n >= LA:
                                    r = n - LA
                                    pz, bpz = pend.pop(r)
                                    post_block(W, pz, bpz, 128, g, ROPE_BASE[g] + r * (16 // d + 1), False,
                                               kst[s][:, r * 128:(r + 1) * 128], None, vst[s][:, r, :], None)
                            T.dma("sp", hK[g, h, :, 0:d * 128], kst[s][:, 0:d * 128], reads=[bkst[s]], writes=[b_hK[g][h]])
                            T.dma("sp", hV[g, h, :, 0:d, :], vst[s][:, 0:d, :], reads=[bvst[s]], writes=[b_hV[g][h]])

                T.barrier()
                _stage("S0")
                s1.close()
                agT = sbe("agT", [128, NH, TOK + NSAMP], BF16)
                wq = [sb1("wq0", [128, 16, 384], BF16)] * 2; bwq = [T.buf("wq")] * 2
                wg = [sb1("wg0", [128, 16, 128], BF16)] * 2; bwg = [T.buf("wg")] * 2
                QT = [sb1("QT0", [128, 2048], BF16)] * 2; bQT = [T.buf("QT")] * 2
                KT = [sb1("KT0", [128, 20 * 128], BF16)] * 2; bKT = [T.buf("KT")] * 2
                VV = [sb1("VV0", [128, 20, 128], BF16)] * 2; bVV = [T.buf("VV")] * 2
                OL = sb1("OL", [128, 2, TOK]); bOL = T.buf("OL")
                EX = [sb1("EX%d" % i, [128, 256], BF16) for i in range(2)]; bEX = T.bufs_n("EX", 2)
                PP = [sb1("PP%d" % i, [128, 256], BF16) for i in range(2)]; bPP = T.bufs_n("PP", 2)
                sgt = [sb1("sgt%d" % i, [128, 512]) for i in range(2)]; bsgt = T.bufs_n("sgt", 2)
                ot = [sb1("ot%d" % i, [128, 512]) for i in range(2)]; bot = T.bufs_n("ot", 2)
                QsT = sb1("QsT", [128, 3, NH, NSAMP], BF16); bQsT = T.buf("QsT")
                KsT = sb1("KsT", [128, 3, NH, NSAMP], BF16); bKsT = T.buf("KsT")
                Vs = sb1("Vs", [NSAMP, 3, NH, 128], BF16); bVs = T.buf("Vs")
                sgS = sb1("sgS", [128, NH, NSAMP]); bsgS = T.buf("sgS")

                units = []
                for h in range(NH):
                    for g in range(3):
                        d = DILS[g]
                        parts = 2 if g == 2 else 1
                        for p in range(parts):
                            rs = list(range(d)) if parts == 1 else list(range(8 * p, 8 * p + 8))
                            units.append((h, g, p, rs))

                def load_unit_w(ui):
                    h, g, p, rs = units[ui]
                    if p == 0:
                        slot = (h * 3 + g) % 2
                        T.dma("pool", wq[slot][:], wqkv[h, g], writes=[bwq[slot]])

                load_unit_w(0)
                first_in_head = True
                for ui, (h, g, p, rs) in enumerate(units):
                    d = DILS[g]
                    nkb = 16 // d + 1
                    us = ui % 2
                    slot = (h * 3 + g) % 2
                    if g == 0 and p == 0:
                        T.dma("pool", wg[h % 2][:], wag[h], writes=[bwg[h % 2]])
                    W["bk_dst"] = bKT[us]; W["bv_dst"] = bVV[us]; W["bq_dst"] = bQT[us]
                    nr = len(rs)
                    kdst = KT[us][:, 0:nr * nkb * 128].rearrange("p (r k c) -> p r k c", r=nr, k=nkb)[:, :, 0, :]
                    T.dma("pool", kdst, hK[g, h, :, rs[0] * 128:(rs[0] + nr) * 128].rearrange("p (r c) -> p r c", r=nr),
                          reads=[b_hK[g][h]], writes=[bKT[us]])
                    vdst = VV[us][:, 0:nr * nkb, :].rearrange("p (r k) c -> p r k c", r=nr)[:, :, 0, :]
                    T.dma("pool", vdst, hV[g, h, :, rs[0]:rs[0] + nr, :], reads=[b_hV[g][h]], writes=[bVV[us]])
                    blist = []
                    for rl, r in enumerate(rs):
                        for kb in range(1, nkb):
                            blist.append((rl, r, kb))
                    if p == 0:
                        blist.append(None)

                    def s1_proj(item):
                        pz, bpz = PZ[pzc[0] % 3]
                        pzc[0] += 1
                        if item is None:
                            for kc in range(16):
                                T.op("pe", lambda e: e.matmul(pz[:NSAMP, 0:384], xnS[:, kc, 2:18], wq[slot][:, kc, :],
                                                              start=(kc == 0), stop=(kc == 15)),
                                     reads=[b_xnS, bwq[slot]], writes=[bpz])
                        else:
                            rl, r, kb = item
                            start = (kb - 1) * 128 * d + r
                            for kc in range(16):
                                T.op("pe", lambda e: e.matmul(pz[:, 0:384], xnT[:, kc, start:start + 127 * d + 1:d],
                                                              wq[slot][:, kc, :], start=(kc == 0), stop=(kc == 15)),
                                     reads=[b_xnT, bwq[slot]], writes=[bpz])
                        return pz, bpz

                    def s1_post(item, pz, bpz):
                        if item is None:
                            L = CACHE_LEN[g]
                            kv_out = []
                            for b in range(4):
                                kv_out.append((skv_o[g][b, L - 4:L, 0, h * 128:(h + 1) * 128],
                                               skv_o[g][b, L - 4:L, 1, h * 128:(h + 1) * 128], 4 * b, 4 * b + 4))
                            W["bk_dst"] = bKsT; W["bv_dst"] = bVs; W["bq_dst"] = bQsT
                            post_block(W, pz, bpz, NSAMP, g, ROPE_SAMPLE, True, KsT[:, g, h, :], QsT[:, g, h, :], Vs[:, g, h, :], kv_out)
                            W["bk_dst"] = bKT[us]; W["bv_dst"] = bVV[us]; W["bq_dst"] = bQT[us]
                            return
                        rl, r, kb = item
                        bi = rl * nkb + kb
                        qi = rl * (nkb - 1) + (kb - 1)
                        kv_out = None
                        lo_tok = (kb - 1) * 128 * d + r
                        keep0 = TOK - CACHE_LEN[g]
                        if lo_tok >= keep0:
                            row0 = lo_tok - keep0
                            kd = kv_o[g][row0:row0 + 127 * d + 1:d, 0, h * 128:(h + 1) * 128]
                            vd = kv_o[g][row0:row0 + 127 * d + 1:d, 1, h * 128:(h + 1) * 128]
                            kv_out = [(kd, vd, 0, 128)]
                        post_block(W, pz, bpz, 128, g, ROPE_BASE[g] + r * nkb + kb, True,
                                   KT[us][:, bi * 128:(bi + 1) * 128], QT[us][:, qi * 128:(qi + 1) * 128],
                                   VV[us][:, bi, :], kv_out)

                    pend = {}
                    NBk = len(blist)
                    for n in range(NBk + LA):
                        if n < NBk:
                            pend[n] = s1_proj(blist[n])
                        if n >= LA:
                            pz, bpz = pend.pop(n - LA)
                            s1_post(blist[n - LA], pz, bpz)
                    if ui + 1 < len(units):
                        load_unit_w(ui + 1)
                    if ui + 1 < len(units):
                        load_unit_w(ui + 1)
                    qlist = []
                    for rl, r in enumerate(rs):
                        for kq in range(1, nkb):
                            qlist.append((rl, r, kq))

                    def att_scores(n):
                        rl, r, kq = qlist[n]
                        bi = rl * nkb + kq
                        qi = rl * (nkb - 1) + (kq - 1)
                        a = n % 2
                        pS, bpS = PS[3 + a], PSB[3 + a]
                        T.op("pe", lambda e: e.matmul(pS[:, 0:128], KT[us][:, (bi - 1) * 128: bi * 128], QT[us][:, qi * 128:(qi + 1) * 128],
                                                      start=True, stop=True), reads=[bKT[us], bQT[us]], writes=[bpS])
                        T.op("pe", lambda e: e.matmul(pS[:, 128:256], KT[us][:, bi * 128:(bi + 1) * 128], QT[us][:, qi * 128:(qi + 1) * 128],
                                                      start=True, stop=True), reads=[bKT[us], bQT[us]], writes=[bpS])
                        T.op("act", lambda e: e.activation(out=EX[a][:], in_=pS[:, 0:256], func=AF.Exp, scale=SCALE),
                             reads=[bpS], writes=[bEX[a]])
                        mk = masks[:, 1, :] if kq == 1 else masks[:, 0, :]
                        T.op("dve", lambda e: e.tensor_tensor(out=PP[a][:], in0=EX[a][:], in1=mk, op=ALU.mult),
                             reads=[bEX[a], b_masks], writes=[bPP[a]])

                    def att_pv(n):
                        rl, r, kq = qlist[n]
                        bi = rl * nkb + kq
                        a = n % 2
                        pO, bpO = PS[5 + a], PSB[5 + a]
                        T.op("pe", lambda e: e.matmul(pO[:, 0:128], VV[us][:, bi - 1, :], PP[a][:, 0:128], start=True, stop=False),
                             reads=[bVV[us], bPP[a]], writes=[bpO])
                        T.op("pe", lambda e: e.matmul(pO[:, 0:128], VV[us][:, bi, :], PP[a][:, 128:256], start=False, stop=True),
                             reads=[bVV[us], bPP[a]], writes=[bpO])
                        T.op("pe", lambda e: e.matmul(pO[:, 128:256], ones[:], PP[a][:, 0:128], start=True, stop=False),
                             reads=[b_ones, bPP[a]], writes=[bpO])
                        T.op("pe", lambda e: e.matmul(pO[:, 128:256], ones[:], PP[a][:, 128:256], start=False, stop=True),
                             reads=[b_ones, bPP[a]], writes=[bpO])
                        t0 = (kq - 1) * 128 * d + r
                        dst = OL[:, :, t0:t0 + 127 * d + 1:d]
                        src_ = pO[:, 0:256].rearrange("p (a b) -> p a b", a=2)
                        if first_in_head:
                            T.op("act", lambda e: e.activation(out=dst, in_=src_, func=AF.Copy), reads=[bpO], writes=[bOL])
                        else:
                            T.op("dve", lambda e: e.tensor_tensor(out=dst, in0=src_, in1=dst, op=ALU.add), reads=[bpO, bOL], writes=[bOL])

                    NQ = len(qlist)
                    for n in range(NQ + 1):
                        if n < NQ:
                            att_scores(n)
                        if n >= 1:
                            att_pv(n - 1)
                    if g == 0:
                        first_in_head = False
                    if g == 2 and p == 1:
                        first_in_head = True
                        if _DEBUG[0] and h == 0:
                            dbgOL = dout("dbgOL", [128, 2, TOK])
                            T.dma("sp", dbgOL, OL[:], reads=[bOL])
                        T.op("dve", lambda e: e.reciprocal(out=OL[:, 1, :], in_=OL[:, 1, :]), reads=[bOL], writes=[bOL])
                        gs = h % 2
                        blocks = [(tb * 512, 512, xnT, b_xnT, tb * 512) for tb in range(4)] + [(TOK, NSAMP, xnS, b_xnS, 2)]
                        pbs = [(PS[0], PSB[0]), (PS[1], PSB[1]), (PS[3], PSB[3]), (PS[4], PSB[4]), (PS[7], PSB[7])]
                        for kc in range(16):
                            for bi_, (c0, n, src_t, src_b, sc0) in enumerate(blocks):
                                pb, bpb = pbs[bi_]
                                T.op("pe", lambda e: e.matmul(pb[:, 0:n], wg[gs][:, kc, :], src_t[:, kc, sc0:sc0 + n],
                                                              start=(kc == 0), stop=(kc == 15)),
                                     reads=[bwg[gs], src_b], writes=[bpb])
                        for bi_, (c0, n, src_t, src_b, sc0) in enumerate(blocks):
                            pb, bpb = pbs[bi_]
                            if bi_ < 4:
                                a = bi_ % 2
                                T.op("act", lambda e: e.activation(out=sgt[a][:], in_=pb[:, 0:512], func=AF.Silu), reads=[bpb], writes=[bsgt[a]])
                                if _DEBUG[0] and h == 0 and bi_ == 0:
                                    dbgsg = dout("dbgsg", [128, 512])
                                    T.dma("sp", dbgsg, sgt[a][:], reads=[bsgt[a]])
                                T.op("dve", lambda e: e.tensor_tensor(out=ot[a][:], in0=OL[:, 0, c0:c0 + 512], in1=OL[:, 1, c0:c0 + 512], op=ALU.mult),
                                     reads=[bOL], writes=[bot[a]])
                                T.op("dve", lambda e: e.tensor_tensor(out=agT[:, h, c0:c0 + 512], in0=ot[a][:], in1=sgt[a][:], op=ALU.mult),
                                     reads=[bot[a], bsgt[a]], writes=[b_agT])
                            else:
                                T.op("act", lambda e: e.activation(out=sgS[:, h, :], in_=pb[:, 0:NSAMP], func=AF.Silu), reads=[bpb], writes=[bsgS])

                T.barrier()
                _stage("S1")
                with ExitStack() as ss_:
                    kt_ = [ss_.enter_context(nc.sbuf_tensor("s_skt%d" % i, [128, 1024], BF16)) for i in range(2)]; bkt_ = T.bufs_n("skt", 2)
                    vt_ = [ss_.enter_context(nc.sbuf_tensor("s_svt%d" % i, [128, 1024], BF16)) for i in range(2)]; bvt_ = T.bufs_n("svt", 2)
                    ktT = [ss_.enter_context(nc.sbuf_tensor("s_sktT%d" % i, [128, 1024], BF16)) for i in range(2)]; bktT = T.bufs_n("sktT", 2)
                    pe_ = [ss_.enter_context(nc.sbuf_tensor("s_spe%d" % i, [128, NH, NSAMP], BF16)) for i in range(2)]; bpe_ = T.bufs_n("spe", 2)
                    pp_ = [ss_.enter_context(nc.sbuf_tensor("s_spp%d" % i, [128, NH, NSAMP], BF16)) for i in range(2)]; bpp_ = T.bufs_n("spp", 2)
                    osb = ss_.enter_context(nc.sbuf_tensor("s_osb", [NSAMP, 1024], F32)); bosb = T.buf("osb")
                    lsb = ss_.enter_context(nc.sbuf_tensor("s_lsb", [NSAMP, NH], F32)); blsb = T.buf("lsb")
                    accO = [PS[5], PS[6]]; baccO = [PSB[5], PSB[6]]
                    accL, baccL = PS[7], PSB[7]
                    trp = PS[2][:].bitcast(BF16); btrp = PSB[2]
                    tiles = []
                    for b in range(4):
                        tiles.append((0, b, 0, 0))
                        for g in (1, 2):
                            for t in range(4):
                                tiles.append((g, b, t, 1 + (g - 1) * 4 + t))
                    for ti, (g, b, t, mi) in enumerate(tiles):
                        d = DILS[g]
                        s = ti % 2
                        T.dma("pool", kt_[s][:], ck[g][b, t:t + 127 * d + 1:d, 0, :], writes=[bkt_[s]])
                        T.dma("pool", vt_[s][:], ck[g][b, t:t + 127 * d + 1:d, 1, :], writes=[bvt_[s]])
                        for h in range(NH):
                            T.op("pe", lambda e: e.transpose(out=trp[:, h * 128:(h + 1) * 128], in_=kt_[s][:, h * 128:(h + 1) * 128],
                                                             identity=identb[:]), reads=[bkt_[s], b_identb], writes=[btrp])
                        T.op("act", lambda e: e.activation(out=ktT[s][:], in_=trp[:, 0:1024], func=AF.Copy), reads=[btrp], writes=[bktT[s]])
                        pS, bpS = PS[3 + s], PSB[3 + s]
                        for h in range(NH):
                            T.op("pe", lambda e: e.matmul(pS[:, h * NSAMP:(h + 1) * NSAMP], ktT[s][:, h * 128:(h + 1) * 128], QsT[:, g, h, :],
                                                          start=True, stop=True), reads=[bktT[s], bQsT], writes=[bpS])
                        T.op("act", lambda e: e.activation(out=pe_[s][:], in_=pS[:, 0:NH * NSAMP].rearrange("p (h t) -> p h t", h=NH),
                                                           func=AF.Exp, scale=SCALE), reads=[bpS], writes=[bpe_[s]])
                        T.op("dve", lambda e: e.tensor_tensor(out=pp_[s][:], in0=pe_[s][:], in1=bmid(smask[:, b * 9 + mi, :], NH), op=ALU.mult),
                             reads=[bpe_[s], b_smask], writes=[bpp_[s]])
                        for h in range(NH):
                            T.op("pe", lambda e: e.matmul(accO[h // 4][:NSAMP, (h % 4) * 128:(h % 4 + 1) * 128], pp_[s][:, h, :],
                                                          vt_[s][:, h * 128:(h + 1) * 128], start=(ti == 0 and h % 4 == 0), stop=False),
                                 reads=[bpp_[s], bvt_[s]], writes=[baccO[h // 4]])
                            T.op("pe", lambda e: e.matmul(accL[:NSAMP, h:h + 1], pp_[s][:, h, :], ones[:, 0:1], start=(ti == 0 and h == 0), stop=False),
                                 reads=[bpp_[s], b_ones], writes=[baccL])
                    ne_ = ss_.enter_context(nc.sbuf_tensor("s_sne", [NSAMP, NH, NSAMP], BF16)); bne_ = T.buf("sne")
                    np_ = ss_.enter_context(nc.sbuf_tensor("s_snp", [NSAMP, NH, NSAMP], BF16)); bnp_ = T.buf("snp")
                    for g in range(3):
                        pS, bpS = PS[3 + g % 2], PSB[3 + g % 2]
                        for h in range(NH):
                            T.op("pe", lambda e: e.matmul(pS[:NSAMP, h * NSAMP:(h + 1) * NSAMP], KsT[:, g, h, :], QsT[:, g, h, :],
                                                          start=True, stop=True), reads=[bKsT, bQsT], writes=[bpS])
                        T.op("act", lambda e: e.activation(out=ne_[:], in_=pS[:NSAMP, 0:NH * NSAMP].rearrange("p (h t) -> p h t", h=NH),
                                                           func=AF.Exp, scale=SCALE), reads=[bpS], writes=[bne_])
                        T.op("dve", lambda e: e.tensor_tensor(out=np_[:], in0=ne_[:], in1=bmid(nmask[:, g, :], NH), op=ALU.mult),
                             reads=[bne_, b_nmask], writes=[bnp_])
                        for h in range(NH):
                            last = (g == 2)
                            T.op("pe", lambda e: e.matmul(accO[h // 4][:NSAMP, (h % 4) * 128:(h % 4 + 1) * 128], np_[:, h, :],
                                                          Vs[:, g, h, :], start=False, stop=last),
                                 reads=[bnp_, bVs], writes=[baccO[h // 4]])
                            T.op("pe", lambda e: e.matmul(accL[:NSAMP, h:h + 1], np_[:, h, :], ones[:NSAMP, 0:1], start=False, stop=last),
                                 reads=[bnp_, b_ones], writes=[baccL])
                    T.op("dve", lambda e: e.reciprocal(out=lsb[:], in_=accL[:NSAMP, 0:NH]), reads=[baccL], writes=[blsb])
                    for hh in range(2):
                        la = lsb[:, hh * 4:(hh + 1) * 4]
                        lb_ = bass.AP(la.tensor, la.offset, [list(la.ap[0]), [1, 4], [0, 128]])
                        T.op("dve", lambda e: e.tensor_tensor(out=osb[:, hh * 512:(hh + 1) * 512].rearrange("p (h c) -> p h c", h=4),
                                                              in0=accO[hh][:NSAMP, :].rearrange("p (h c) -> p h c", h=4),
                                                              in1=lb_, op=ALU.mult),
                             reads=[baccO[hh], blsb], writes=[bosb])
                    pT, bpT = PS[0], PSB[0]
                    for h in range(NH):
                        T.op("pe", lambda e: e.transpose(out=pT[:, h * NSAMP:(h + 1) * NSAMP], in_=osb[:, h * 128:(h + 1) * 128],
                                                         identity=identf[:NSAMP, :NSAMP]), reads=[bosb, b_identf], writes=[bpT])
                    T.op("dve", lambda e: e.tensor_tensor(out=agT[:, :, TOK:TOK + NSAMP],
                                                          in0=pT[:, 0:NH * NSAMP].rearrange("p (h t) -> p h t", h=NH),
                                                          in1=sgS[:], op=ALU.mult), reads=[bpT, bsgS], writes=[b_agT])

                _stage("S1s")
                ags = (dout if _DEBUG[0] else dscr)("ags", [128, NH, TOK + NSAMP], BF16); b_ags = T.buf("ags")
                T.dma("sp", ags, agT[:], reads=[b_agT], writes=[b_ags])
            sE.close()

            T.barrier()
            with ExitStack() as s2:
                def sb2(name, shape, dt=F32):
                    return s2.enter_context(nc.sbuf_tensor("s_s_" + name, list(shape), dt))

                cyT = sb2("cyT", [128, 16, TOK + NSAMP], BF16); b_cyT = T.buf("cyT")
                wsl = [sb2("wsl%d" % i, [128, 16, 128], BF16) for i in range(6)]; bwsl = T.bufs_n("wsl", 6)
                wrr = [0]

                def load_slab(src, nkc=16):
                    i = wrr[0] % 6
                    wrr[0] += 1
                    T.dma("pool", wsl[i][:, 0:nkc, :], src, writes=[bwsl[i]])
                    return wsl[i], bwsl[i]

                OWN = [(tb * 512, 512) for tb in range(4)]

                PASSES = [[0, 1, 4], [2, 3]]

                def proj_fm(slab, bslab, nkc, act_own, b_own, act_s, b_s, s0, sn, which):
                    outs = {bi_: nextbank() for bi_ in which}
                    for kc in range(nkc):
                        for bi_ in which:
                            pb, bpb = outs[bi_]
                            if bi_ < 4:
                                c0, n = OWN[bi_]
                                T.op("pe", lambda e: e.matmul(pb[:, 0:n], slab[:, kc, :], act_own[:, kc, c0:c0 + n],
                                                              start=(kc == 0), stop=(kc == nkc - 1)),
                                     reads=[bslab, b_own], writes=[bpb])
                            else:
                                T.op("pe", lambda e: e.matmul(pb[:, 0:sn], slab[:, kc, :], act_s[:, kc, s0:s0 + sn],
                                                              start=(kc == 0), stop=(kc == nkc - 1)),
                                     reads=[bslab, b_s], writes=[bpb])
                    return outs

                with ExitStack() as sc:
                    def sbc(name, shape, dt=F32):
                        return sc.enter_context(nc.sbuf_tensor("s_s_" + name, list(shape), dt))
                    uext = [sbc("uext%d" % i, [128, TOK + 2]) for i in range(2)]; buext = T.bufs_n("uext", 2)
                    usx = [sbc("usx%d" % i, [128, 4, 6]) for i in range(2)]; busx = T.bufs_n("usx", 2)
                    hS = [sbc("hS%d" % i, [128, 512]) for i in range(4)]; bhS = T.bufs_n("hS", 4)
                    hs_s = sbc("hs_s", [128, 18]); bhs_s = T.buf("hs_s")
                    us_s = sbc("us_s", [128, 18]); bus_s = T.buf("us_s")
                    acc = [sbc("acc%d" % i, [128, 512]) for i in range(2)]; bacc = T.bufs_n("acc", 2)
                    yv = [sbc("yv%d" % i, [128, 512]) for i in range(2)]; byv = T.bufs_n("yv", 2)
                    sg2 = [sbc("sg2%d" % i, [128, 512]) for i in range(2)]; bsg2 = T.bufs_n("sg2", 2)
                    accs = sbc("accs", [128, 4, 4]); baccs = T.buf("accs")
                    ys = sbc("ys", [128, 4, 4]); bys = T.buf("ys")
                    sgs2 = sbc("sgs2", [128, 4, 4]); bsgs2 = T.buf("sgs2")
                    ncvS = sbc("ncvS", [128, 16, 2]); bncvS = T.buf("ncvS")
                    sncvS = sbc("sncvS", [128, 16, 4, 2]); bsncvS = T.buf("sncvS")
                    for j in range(16):
                        ue, bue = uext[j % 2], buext[j % 2]
                        ux, bux = usx[j % 2], busx[j % 2]
                        sl_h = load_slab(wcv[j, 0]); sl_c = load_slab(wcv[j, 1])
                        sl_b = load_slab(wcv[j, 2]); sl_g = load_slab(wcv[j, 3])
                        for ps_ in PASSES:
                            ph = proj_fm(sl_h[0], sl_h[1], 16, xnT, b_xnT, xnS, b_xnS, 0, 18, ps_)
                            pc = proj_fm(sl_c[0], sl_c[1], 16, xnT, b_xnT, xnS, b_xnS, 0, 18, ps_)
                            for bi_ in ps_:
                                pb, bpb = ph[bi_]
                                pb2, bpb2 = pc[bi_]
                                if bi_ < 4:
                                    c0, n = OWN[bi_]
                                    T.op("act", lambda e: e.activation(out=hS[bi_][:], in_=pb[:, 0:512], func=AF.Copy), reads=[bpb], writes=[bhS[bi_]])
                                    T.op("dve", lambda e: e.tensor_tensor(out=ue[:, 2 + c0:2 + c0 + n], in0=pb2[:, 0:n], in1=hS[bi_][:], op=ALU.mult),
                                         reads=[bpb2, bhS[bi_]], writes=[bue])
                                else:
                                    T.op("act", lambda e: e.activation(out=hs_s[:], in_=pb[:, 0:18], func=AF.Copy), reads=[bpb], writes=[bhs_s])
                                    T.op("dve", lambda e: e.tensor_tensor(out=us_s[:], in0=pb2[:, 0:18], in1=hs_s[:], op=ALU.mult),
                                         reads=[bpb2, bhs_s], writes=[bus_s])
                                    T.op("dve", lambda e: e.tensor_copy(out=ue[:, 0:2], in_=us_s[:, 0:2]), reads=[bus_s], writes=[bue])
                                    T.op("dve", lambda e: e.tensor_copy(out=ux[:, :, 2:6], in_=us_s[:, 2:18].rearrange("p (b t) -> p b t", b=4)),
                                         reads=[bus_s], writes=[bux])
                                    T.op("dve", lambda e: e.tensor_copy(out=ux[:, :, 0:2], in_=scTs[:, j, :, :]), reads=[b_scT], writes=[bux])
                        T.op("dve", lambda e: e.tensor_copy(out=ncvS[:, j, :], in_=ue[:, TOK:TOK + 2]), reads=[bue], writes=[bncvS])
                        T.op("dve", lambda e: e.tensor_copy(out=sncvS[:, j, :, :], in_=ux[:, :, 4:6]), reads=[bux], writes=[bsncvS])
                        for ps_ in PASSES:
                            pbb = proj_fm(sl_b[0], sl_b[1], 16, xnT, b_xnT, xnS, b_xnS, 0, 18, ps_)
                            pgg = proj_fm(sl_g[0], sl_g[1], 16, xnT, b_xnT, xnS, b_xnS, 0, 18, ps_)
                            for bi_ in ps_:
                                pb, bpb = pbb[bi_]
                                pg, bpg = pgg[bi_]
                                if bi_ < 4:
                                    c0, n = OWN[bi_]
                                    a = bi_ % 2
                                    T.op("act", lambda e: e.activation(out=acc[a][:], in_=ue[:, 2 + c0:2 + c0 + n], func=AF.Copy, scale=cw[:, j, 2:3]),
                                         reads=[bue, b_cw], writes=[bacc[a]])
                                    T.op("dve", lambda e: e.scalar_tensor_tensor(out=acc[a][:], in0=ue[:, 1 + c0:1 + c0 + n], scalar=cw[:, j, 1:2],
                                                                                 in1=acc[a][:], op0=ALU.mult, op1=ALU.add),
                                         reads=[bue, b_cw, bacc[a]], writes=[bacc[a]])
                                    T.op("dve", lambda e: e.scalar_tensor_tensor(out=acc[a][:], in0=ue[:, c0:c0 + n], scalar=cw[:, j, 0:1],
                                                                                 in1=acc[a][:], op0=ALU.mult, op1=ALU.add),
                                         reads=[bue, b_cw, bacc[a]], writes=[bacc[a]])
                                    T.op("dve", lambda e: e.tensor_tensor(out=yv[a][:], in0=pb[:, 0:n], in1=acc[a][:], op=ALU.mult),
                                         reads=[bpb, bacc[a]], writes=[byv[a]])
                                    T.op("act", lambda e: e.activation(out=sg2[a][:], in_=pg[:, 0:n], func=AF.Silu), reads=[bpg], writes=[bsg2[a]])
                                    T.op("dve", lambda e: e.tensor_tensor(out=cyT[:, j, c0:c0 + n], in0=yv[a][:], in1=sg2[a][:], op=ALU.mult),
                                         reads=[byv[a], bsg2[a]], writes=[b_cyT])
                                else:
                                    T.op("act", lambda e: e.activation(out=accs[:], in_=ux[:, :, 2:6], func=AF.Copy, scale=cw[:, j, 2:3]),
                                         reads=[bux, b_cw], writes=[baccs])
                                    T.op("dve", lambda e: e.scalar_tensor_tensor(out=accs[:], in0=ux[:, :, 1:5], scalar=cw[:, j, 1:2], in1=accs[:],
                                                                                 op0=ALU.mult, op1=ALU.add), reads=[bux, b_cw, baccs], writes=[baccs])
                                    T.op("dve", lambda e: e.scalar_tensor_tensor(out=accs[:], in0=ux[:, :, 0:4], scalar=cw[:, j, 0:1], in1=accs[:],
                                                                                 op0=ALU.mult, op1=ALU.add), reads=[bux, b_cw, baccs], writes=[baccs])
                                    T.op("dve", lambda e: e.tensor_tensor(out=ys[:], in0=pb[:, 2:18].rearrange("p (b t) -> p b t", b=4), in1=accs[:], op=ALU.mult),
                                         reads=[bpb, baccs], writes=[bys])
                                    T.op("act", lambda e: e.activation(out=sgs2[:], in_=pg[:, 2:18].rearrange("p (b t) -> p b t", b=4), func=AF.Silu),
                                         reads=[bpg], writes=[bsgs2])
                                    T.op("dve", lambda e: e.tensor_tensor(out=cyT[:, j, TOK:TOK + NSAMP].rearrange("p (b t) -> p b t", b=4), in0=ys[:], in1=sgs2[:],
                                                                          op=ALU.mult), reads=[bys, bsgs2], writes=[b_cyT])
                    T.dma("sp", ncv_o, ncvS[:], reads=[bncvS])
                    T.dma("sp", sncv_o, sncvS[:], reads=[bsncvS])

                T.barrier()
                _stage("S2")
                _bt2 = T.buf("t2s"); b_t2s = [_bt2] * 16
                with ExitStack() as sc:
                    def sbc(name, shape, dt=F32):
                        return sc.enter_context(nc.sbuf_tensor("s_s_" + name, list(shape), dt))
                    sgm = [sbc("sgm%d" % i, [128, 512]) for i in range(2)]; bsgm = T.bufs_n("sgm", 2)
                    t2o = [sbc("t2o%d" % i, [128, TOK + NSAMP], BF16) for i in range(2)]; bt2o = T.bufs_n("t2o", 2)
                    for i in range(16):
                        sl_m = load_slab(wml[i, 1]); sl_p = load_slab(wcp[i])
                        to, bto = t2o[i % 2], bt2o[i % 2]
                        BL = OWN + [(TOK, NSAMP)]
                        for ps_ in PASSES:
                            pm = proj_fm(sl_m[0], sl_m[1], 16, xnT, b_xnT, xnS, b_xnS, 2, NSAMP, ps_)
                            pp2 = proj_fm(sl_p[0], sl_p[1], 16, cyT, b_cyT, cyT, b_cyT, TOK, NSAMP, ps_)
                            for bi_ in ps_:
                                c0, n = BL[bi_]
                                a = bi_ % 2
                                T.op("act", lambda e: e.activation(out=sgm[a][:, 0:n], in_=pm[bi_][0][:, 0:n], func=AF.Sigmoid),
                                     reads=[pm[bi_][1]], writes=[bsgm[a]])
                                T.op("dve", lambda e: e.tensor_tensor(out=to[:, c0:c0 + n], in0=pp2[bi_][0][:, 0:n], in1=sgm[a][:, 0:n], op=ALU.mult),
                                     reads=[pp2[bi_][1], bsgm[a]], writes=[bto])
                        T.dma("sp", t2s[i], to[:], reads=[bto], writes=[b_t2s[i]])

            T.barrier()
            _stage("S3b")
            with ExitStack() as s3:
                def sb3(name, shape, dt=F32):
                    return s3.enter_context(nc.sbuf_tensor("s_s_" + name, list(shape), dt))
                mT = sb3("mT", [128, 16, TOK + NSAMP], BF16); b_mT = T.buf("mT")
                agT = sb3("agT2", [128, NH, TOK + NSAMP], BF16); b_agT = T.buf("agT2")
                T.dma("pool", agT[:], ags, reads=[b_ags], writes=[b_agT])
                s3t = ExitStack()

                def sb3t(name, shape, dt=F32):
                    return s3t.enter_context(nc.sbuf_tensor("s_t_" + name, list(shape), dt))
                wsl = [sb3t("wsm%d" % i, [128, 16, 128], BF16) for i in range(4)]; bwsl = T.bufs_n("wsm", 4)
                wrr = [0]

                def load_slab3(src, nkc=16):
                    i = wrr[0] % 4
                    wrr[0] += 1
                    T.dma("pool", wsl[i][:, 0:nkc, :], src, writes=[bwsl[i]])
                    return wsl[i], bwsl[i]

                OWN = [(tb * 512, 512) for tb in range(4)]
                PASSES = [[0, 1, 4], [2, 3]]

                def proj_fm3(slab, bslab, nkc, act_own, b_own, act_s, b_s, s0, sn, which):
                    outs = {bi_: nextbank() for bi_ in which}
                    for kc in range(nkc):
                        for bi_ in which:
                            pb, bpb = outs[bi_]
                            if bi_ < 4:
                                c0, n = OWN[bi_]
                                T.op("pe", lambda e: e.matmul(pb[:, 0:n], slab[:, kc, :], act_own[:, kc, c0:c0 + n],
                                                              start=(kc == 0), stop=(kc == nkc - 1)),
                                     reads=[bslab, b_own], writes=[bpb])
                            else:
                                T.op("pe", lambda e: e.matmul(pb[:, 0:sn], slab[:, kc, :], act_s[:, kc, s0:s0 + sn],
                                                              start=(kc == 0), stop=(kc == nkc - 1)),
                                     reads=[bslab, b_s], writes=[bpb])
                    return outs

                sgm = [sb3t("sgn%d" % i, [128, 512]) for i in range(2)]; bsgm = T.bufs_n("sgn", 2)
                t1 = [sb3t("t1%d" % i, [128, 512]) for i in range(2)]; bt1 = T.bufs_n("t1", 2)
                t2i = [sb3t("t2i%d" % i, [128, TOK + NSAMP], BF16) for i in range(2)]; bt2i = T.bufs_n("t2i", 2)
                for i in range(16):
                    sl_m = load_slab3(wml[i, 0]); sl_a = load_slab3(watt[i], 8)
                    T.dma("pool", t2i[i % 2][:], t2s[i], reads=[b_t2s[i]], writes=[bt2i[i % 2]])
                    BL = OWN + [(TOK, NSAMP)]
                    for ps_ in PASSES:
                        pm = proj_fm3(sl_m[0], sl_m[1], 16, xnT, b_xnT, xnS, b_xnS, 2, NSAMP, ps_)
                        pa = proj_fm3(sl_a[0], sl_a[1], 8, agT, b_agT, agT, b_agT, TOK, NSAMP, ps_)
                        for bi_ in ps_:
                            c0, n = BL[bi_]
                            a = bi_ % 2
                            T.op("act", lambda e: e.activation(out=sgm[a][:, 0:n], in_=pm[bi_][0][:, 0:n], func=AF.Sigmoid),
                                 reads=[pm[bi_][1]], writes=[bsgm[a]])
                            T.op("dve", lambda e: e.tensor_tensor(out=t1[a][:, 0:n], in0=pa[bi_][0][:, 0:n], in1=sgm[a][:, 0:n], op=ALU.mult),
                                 reads=[pa[bi_][1], bsgm[a]], writes=[bt1[a]])
                            T.op("dve", lambda e: e.tensor_tensor(out=mT[:, i, c0:c0 + n], in0=t1[a][:, 0:n], in1=t2i[i % 2][:, c0:c0 + n], op=ALU.add),
                                 reads=[bt1[a], bt2i[i % 2]], writes=[b_mT])
                s3t.close()
                T.barrier()
                _stage("S3a")
                wo = [sb3("wo%d" % i, [128, 16, 512], BF16) for i in range(2)]; bwo = T.bufs_n("wo", 2)
                xs = [sb3("xs%d" % i, [128, 512]) for i in range(2)]; bxs = T.bufs_n("xs", 2)
                yo = [sb3("yo%d" % i, [128, 512]) for i in range(2)]; byo = T.bufs_n("yo", 2)
                b_y = T.buf("y_o")
                n4 = 0
                for cb in range(4):
                    T.dma("pool", wo[cb % 2][:], wout[cb], writes=[bwo[cb % 2]])
                    for tt in range(17):
                        rows = 128 if tt < 16 else NSAMP
                        a = n4 % 2
                        n4 += 1
                        T.dma("pool", xs[a][:rows, :], x[2048 + tt * 128: 2048 + tt * 128 + rows, cb * 512:(cb + 1) * 512], writes=[bxs[a]])
                        pb, bpb = nextbank()
                        for kc in range(16):
                            T.op("pe", lambda e: e.matmul(pb[:rows, :], mT[:, kc, tt * 128: tt * 128 + rows], wo[cb % 2][:, kc, :],
                                                          start=(kc == 0), stop=(kc == 15)), reads=[b_mT, bwo[cb % 2]], writes=[bpb])
                        T.op("dve", lambda e: e.tensor_tensor(out=yo[a][:rows, :], in0=pb[:rows, :], in1=xs[a][:rows, :], op=ALU.add),
                             reads=[bpb, bxs[a]], writes=[byo[a]])
                        T.dma("sp", y_o[tt * 128: tt * 128 + rows, cb * 512:(cb + 1) * 512], yo[a][:rows, :], reads=[byo[a]])

        except _Stop:
            if sE is not None:
                sE.close()
        T.finish("sp")
    return nc


def _slabs(w, cols, nkc=16):
    return np.ascontiguousarray(w[:, cols].reshape(nkc, 128, len(cols)).transpose(1, 0, 2))


def _rope_tables(c0):
    half = 16
    inv = (np.float32(500000.0) ** (-np.arange(half, dtype=np.float32) * np.float32(2.0 / 32))).astype(np.float32)
    tab = np.zeros((128, 70, 48), np.float32)
    i = np.arange(128)
    for g, d in enumerate(DILS):
        nkb = 16 // d + 1
        for r in range(d):
            for kb in range(nkb):
                pos = (c0 - 128 * d + (kb * 128 + i) * d + r).astype(np.float32)
                ang = pos[:, None] * inv[None, :]
                c, s = np.cos(ang).astype(np.float32), np.sin(ang).astype(np.float32)
                t = ROPE_BASE[g] + r * nkb + kb
                tab[:, t, 0:16] = c; tab[:, t, 16:32] = s; tab[:, t, 32:48] = c
    pos = (PAST_LEN + (np.arange(NSAMP) % 4)).astype(np.float32)
    ang = pos[:, None] * inv[None, :]
    tab[:NSAMP, ROPE_SAMPLE, 0:16] = np.cos(ang); tab[:NSAMP, ROPE_SAMPLE, 16:32] = np.sin(ang); tab[:NSAMP, ROPE_SAMPLE, 32:48] = np.cos(ang)
    return tab


def _const_masks(halo_valid):
    k = np.arange(128)[:, None]
    q = np.arange(128)[None, :]
    prev = (k >= q).astype(np.float32)
    cur = (k <= q).astype(np.float32)
    masks = np.zeros((128, 2, 256), np.float32)
    masks[:, 0, :128] = prev; masks[:, 0, 128:] = cur
    masks[:, 1, :128] = prev * halo_valid; masks[:, 1, 128:] = cur
    smask = np.zeros((128, 36, 16), np.float32)
    m = np.arange(128)
    for b in range(4):
        for t in range(4):
            smask[:, b * 9 + 0, b * 4 + t] = (m >= t)
            for gi in range(2):
                smask[:, b * 9 + 1 + gi * 4 + t, b * 4 + t] = 1.0
    nmask = np.zeros((16, 3, 16), np.float32)
    for b in range(4):
        for tk in range(4):
            for tq in range(4):
                nmask[b * 4 + tk, 0, b * 4 + tq] = float(tk <= tq)
                nmask[b * 4 + tk, 1, b * 4 + tq] = float(tk == tq)
                nmask[b * 4 + tk, 2, b * 4 + tq] = float(tk == tq)
    return masks, smask, nmask


_NC_CACHE = {}


def _prepare(x_prompt, x_sample, cache_kv_w128, cache_kv_w512, cache_kv_w2048, state_conv,
             norm_w, w_in, q_norm_w, k_norm_w, conv_w, w_att_proj, w_conv_proj, w_out):
    f = np.float32
    x_prompt = np.asarray(x_prompt, f); x_sample = np.asarray(x_sample, f)
    caches = [np.asarray(c, f)[0] for c in (cache_kv_w128, cache_kv_w512, cache_kv_w2048)]
    state_conv = np.asarray(state_conv, f)[0]
    w_in = np.asarray(w_in, f)[0]; w_att = np.asarray(w_att_proj, f)[0]
    w_cp = np.asarray(w_conv_proj, f)[0]; w_o = np.asarray(w_out, f)[0]
    norm_w = np.asarray(norm_w, f)[0]; qn = np.asarray(q_norm_w, f)[0]; kn = np.asarray(k_norm_w, f)[0]
    conv_w = np.asarray(conv_w, f)[0]

    ar = np.arange(128)
    wqkv = np.empty((NH, 3, 128, 16, 384), f)
    for h in range(NH):
        for g in range(3):
            cols = np.concatenate([g * 3072 + s * 1024 + h * 128 + ar for s in range(3)])
            wqkv[h, g] = _slabs(w_in, cols)
    wag = np.stack([_slabs(w_in, OFF_AGATE + h * 128 + ar) for h in range(NH)])
    wcv = np.empty((16, 4, 128, 16, 128), f)
    for j in range(16):
        wcv[j, 0] = _slabs(w_in, OFF_CONV + j * 128 + ar)
        wcv[j, 1] = _slabs(w_in, OFF_CONV + 4096 + j * 128 + ar)
        wcv[j, 2] = _slabs(w_in, OFF_CONV + 2048 + j * 128 + ar)
        wcv[j, 3] = _slabs(w_in, OFF_CGATE + j * 128 + ar)
    wml = np.empty((16, 2, 128, 16, 128), f)
    for i in range(16):
        wml[i, 0] = _slabs(w_in, OFF_MERGE + i * 128 + ar)
        wml[i, 1] = _slabs(w_in, OFF_MERGE + 2048 + i * 128 + ar)
    watt = np.stack([_slabs(w_att, i * 128 + ar, 8) for i in range(16)])
    wcp = np.stack([_slabs(w_cp, i * 128 + ar) for i in range(16)])
    wout = np.stack([_slabs(w_o, cb * 512 + np.arange(512)) for cb in range(4)])
    nw = np.ascontiguousarray(np.broadcast_to(norm_w[None, :], (128, D_MODEL)))
    qkw = np.ascontiguousarray(np.broadcast_to(np.stack([qn, kn], axis=1)[None], (128, 3, 2, 128)))
    cw = np.ascontiguousarray(conv_w.reshape(3, 16, 128).transpose(2, 1, 0))
    ident = np.eye(128, dtype=f)

    in_maps = []
    for c in range(NCORES):
        b, q = c // 4, c % 4
        c0 = q * TOK
        xe = np.zeros((4096 + NSAMP, D_MODEL), f)
        if q > 0:
            xe[0:2048] = x_prompt[b, c0 - 2048:c0]
        xe[2048:4096] = x_prompt[b, c0:c0 + TOK]
        xe[4096:] = x_sample[4 * c:4 * c + 4].reshape(NSAMP, D_MODEL)
        masks, smask, nmask = _const_masks(1.0 if q > 0 else 0.0)
        sc = state_conv[4 * c:4 * c + 4]
        scT = np.ascontiguousarray(sc.reshape(4, 2, 16, 128).transpose(3, 2, 0, 1))
        m = {"x": xe, "scT": scT, "wqkv": wqkv, "wag": wag, "wcv": wcv, "wml": wml, "watt": watt, "wcp": wcp,
             "wout": wout, "nw": nw, "qkw": qkw, "cw": cw, "rope": _rope_tables(c0), "masks": masks,
             "smask": smask, "nmask": nmask, "ident": ident}
        for g in range(3):
            m["ck%d" % g] = np.ascontiguousarray(caches[g][4 * c:4 * c + 4].reshape(4, CACHE_LEN[g], 2, 1024))
        in_maps.append(m)

    return in_maps


def _assemble(R):
    f = np.float32
    y_p = np.empty((2, SEQ, D_MODEL), f)
    y_s = np.empty((32, 4, D_MODEL), f)
    for c in range(NCORES):
        b, q = c // 4, c % 4
        y_p[b, q * TOK:(q + 1) * TOK] = R[c]["y"][:TOK]
        y_s[4 * c:4 * c + 4] = R[c]["y"][TOK:].reshape(4, 4, D_MODEL)
    kvp = []
    for g in range(3):
        L = CACHE_LEN[g]
        kvp.append(np.stack([R[3]["kv%d" % g], R[7]["kv%d" % g]]).reshape(1, 2, L, 2, NH, HD))
    ncp = np.stack([R[3]["ncv"], R[7]["ncv"]])
    ncp = np.ascontiguousarray(ncp.transpose(0, 3, 2, 1)).reshape(1, 2, 2, D_MODEL)
    kvs = []
    for g in range(3):
        L = CACHE_LEN[g]
        kvs.append(np.concatenate([R[c]["skv%d" % g] for c in range(NCORES)], axis=0).reshape(1, 32, L, 2, NH, HD))
    ncs = np.concatenate([np.ascontiguousarray(R[c]["sncv"].transpose(2, 3, 1, 0)).reshape(4, 2, D_MODEL) for c in range(NCORES)],
                         axis=0).reshape(1, 32, 2, D_MODEL)
    return (y_p, y_s, kvp[0], kvp[1], kvp[2], ncp, kvs[0], kvs[1], kvs[2], ncs)


def kernel(**inputs):
    in_maps = _prepare(**inputs)
    if "nc" not in _NC_CACHE:
        _NC_CACHE["nc"] = build_program()
    res = run_bass_kernel_spmd(_NC_CACHE["nc"], in_maps, core_ids=list(range(NCORES)))
    return _assemble(res.results)
```

```python
import numpy as np
from contextlib import ExitStack
import concourse.bass as bass
import concourse.mybir as mybir
from concourse.bass_utils import run_bass_kernel_spmd

F32 = mybir.dt.float32
BF16 = mybir.dt.bfloat16
AF = mybir.ActivationFunctionType
ALU = mybir.AluOpType

NCORES = 8
D_MODEL = 2048
SEQ = 8192
PAST_LEN = 16384
HD = 128
NH = 8
DILS = (1, 4, 16)
D_ATT = 1024
QKV_COLS = 9216
OFF_AGATE = QKV_COLS
OFF_CONV = OFF_AGATE + D_ATT
OFF_CGATE = OFF_CONV + 3 * 2048
OFF_MERGE = OFF_CGATE + 2048
EPS = 1e-6
SCALE = HD ** -0.5
TOK = 2048
NSAMP = 16
ROPE_BASE = (0, 17, 37)
ROPE_SAMPLE = 69
CACHE_LEN = (128, 512, 2048)


class Buf:
    __slots__ = ("name", "w", "r", "wsem", "wcnt", "rsem", "rcnt")

    def __init__(self, name):
        self.name = name
        self.w = None
        self.r = {}
        self.wsem = None
        self.wcnt = 0
        self.rsem = None
        self.rcnt = 0


class Trk:
    def __init__(self, nc, stack):
        self.nc = nc
        self.stack = stack
        self.eng = {"pe": nc.tensor, "act": nc.scalar, "dve": nc.vector, "pool": nc.gpsimd, "sp": nc.sync}
        self.done = {}
        self.cnt = {}
        self.sems = {}
        for k in ("pe", "act", "dve", "pool"):
            self.done[k] = stack.enter_context(nc.semaphore("done_" + k))
            self.cnt[k] = 0
            self.sems[("done", k)] = self.done[k]
        self.seen = {k: {} for k in self.eng}
        self.nsem = 0
        self.bufs = []
        self.semmax = {}

    def buf(self, name):
        b = Buf(name)
        self.bufs.append(b)
        return b

    def bufs_n(self, name, n):
        return [self.buf("%s%d" % (name, i)) for i in range(n)]

    def _newsem(self, name):
        self.nsem += 1
        h = self.stack.enter_context(self.nc.semaphore("%s_%d" % (name, self.nsem)))
        key = ("dma", self.nsem)
        self.sems[key] = h
        return key

    def _wait(self, e, key, val):
        if self.seen[e].get(key, 0) >= val:
            return
        self.seen[e][key] = val
        self.eng[e].wait_ge(self.sems[key], val)

    def _deps(self, e, reads, writes):
        for b in reads:
            if b.w is not None:
                self._wait(e, b.w[0], b.w[1])
        for b in writes:
            if b.w is not None:
                k, v, we = b.w
                if we != e:
                    self._wait(e, k, v)
            for k, (v, re) in b.r.items():
                if re != e:
                    self._wait(e, k, v)

    def op(self, e, fn, reads=(), writes=()):
        if _MUTE[0]:
            return None
        self._deps(e, reads, writes)
        ins = fn(self.eng[e])
        self.cnt[e] += 1
        ins.then_inc(self.done[e], 1)
        key = ("done", e)
        ev = self.cnt[e]
        for b in reads:
            b.r[key] = (ev, e)
        for b in writes:
            b.w = (key, ev, e)
            b.r = {}
        return ins

    def dma(self, q, out, in_, reads=(), writes=(), nobarrier=False, free=False):
        if _MUTE[0]:
            return None
        if not free:
            self._deps(q, reads, writes)
        if writes:
            b = writes[0]
            if b.wsem is None:
                b.wsem = self._newsem("w")
            b.wcnt += 1
            key, val = b.wsem, 16 * b.wcnt
        else:
            b = reads[0]
            if b.rsem is None:
                b.rsem = self._newsem("r")
            b.rcnt += 1
            key, val = b.rsem, 16 * b.rcnt
        ins = self.eng[q].dma_start(out=out, in_=in_)
        ins.then_inc(self.sems[key], 16)
        if not nobarrier:
            self.semmax[key] = max(self.semmax.get(key, 0), val)
        tag = "dma"
        for b2 in reads:
            b2.r[key] = (val, tag)
        for b2 in writes:
            b2.w = (key, val, tag)
            b2.r = {}
        return ins

    def barrier(self):
        if _MUTE[0]:
            return
        tg = [(("done", k), self.cnt[k]) for k in self.cnt if self.cnt[k] > 0] + list(self.semmax.items())
        for e in self.eng:
            for key, val in tg:
                if not (key == ("done", e)):
                    self._wait(e, key, val)

    def finish(self, e="sp"):
        for b in self.bufs:
            if b.w is not None:
                self._wait(e, b.w[0], b.w[1])
            for k, (v, _) in b.r.items():
                self._wait(e, k, v)


class _Stop(Exception):
    pass


_STOP = [None]


_MUTE = [False]
_DEBUG = [False]


def _stage(name):
    if _STOP[0] == name:
        _MUTE[0] = True


def bmid(ap, n):
    a = ap.ap
    return bass.AP(ap.tensor, ap.offset, [list(a[0]), [0, n]] + [list(x) for x in a[1:]])


def build_program():
    _MUTE[0] = False
    nc = bass.Bass("TRN2", target_bir_lowering=False)

    def din(name, shape, dt=F32):
        return nc.dram_tensor(name, list(shape), dt, kind="ExternalInput").ap()

    def dout(name, shape, dt=F32):
        return nc.dram_tensor(name, list(shape), dt, kind="ExternalOutput").ap()

    def dscr(name, shape, dt):
        return nc.dram_tensor(name, list(shape), dt).ap()

    x = din("x", [4096 + NSAMP, D_MODEL])
    ck = [din("ck%d" % g, [4, CACHE_LEN[g], 2, 1024]) for g in range(3)]
    scT = din("scT", [128, 16, 4, 2])
    wqkv = din("wqkv", [NH, 3, 128, 16, 384])
    wag = din("wag", [NH, 128, 16, 128])
    wcv = din("wcv", [16, 4, 128, 16, 128])
    wml = din("wml", [16, 2, 128, 16, 128])
    watt = din("watt", [16, 128, 8, 128])
    wcp = din("wcp", [16, 128, 16, 128])
    wout = din("wout", [4, 128, 16, 512])
    nw_d = din("nw", [128, D_MODEL])
    qkw_d = din("qkw", [128, 3, 2, 128])
    cw_d = din("cw", [128, 16, 3])
    rope_d = din("rope", [128, 70, 48])
    masks_d = din("masks", [128, 2, 256])
    smask_d = din("smask", [128, 36, 16])
    nmask_d = din("nmask", [16, 3, 16])
    ident_d = din("ident", [128, 128])

    y_o = dout("y", [TOK + NSAMP, D_MODEL])
    kv_o = [dout("kv%d" % g, [CACHE_LEN[g], 2, 1024]) for g in range(3)]
    ncv_o = dout("ncv", [128, 16, 2])
    skv_o = [dout("skv%d" % g, [4, CACHE_LEN[g], 2, 1024]) for g in range(3)]
    sncv_o = dout("sncv", [128, 16, 4, 2])

    hK = (dout if _DEBUG[0] else dscr)("hK", [3, NH, 128, 2048], BF16)
    hV = (dout if _DEBUG[0] else dscr)("hV", [3, NH, 128, 16, 128], BF16)
    t2s = (dout if _DEBUG[0] else dscr)("t2s", [16, 128, TOK + NSAMP], BF16)

    with ExitStack() as st:
        T = Trk(nc, st)
        sE = None
        try:

            def sb(name, shape, dt=F32):
                return st.enter_context(nc.sbuf_tensor("s_s_" + name, list(shape), dt))

            PS = [st.enter_context(nc.psum_tensor("ps%d" % i, [128, 512], F32)) for i in range(8)]
            PSB = T.bufs_n("ps", 8)
            bank_rr = [0]

            def nextbank():
                i = bank_rr[0] % 8
                bank_rr[0] += 1
                return PS[i], PSB[i]

            xnT = sb("xnT", [128, 16, TOK], BF16); b_xnT = T.buf("xnT")
            xnS = sb("xnS", [128, 16, 18], BF16); b_xnS = T.buf("xnS")
            b_agT = T.buf("agT")

            cw = sb("cw", [128, 16, 3]); b_cw = T.buf("cw")
            scTs = sb("scTs", [128, 16, 4, 2]); b_scT = T.buf("scT")
            sE = ExitStack()

            def sbe(name, shape, dt=F32):
                return sE.enter_context(nc.sbuf_tensor("s_e_" + name, list(shape), dt))

            qkw = sbe("qkw", [128, 3, 2, 128]); b_qkw = T.buf("qkw")
            rope = sbe("rope", [128, 70, 48]); b_rope = T.buf("rope")
            masks = sbe("masks", [128, 2, 256], BF16); b_masks = T.buf("masks")
            smask = sbe("smask", [128, 36, 16], BF16); b_smask = T.buf("smask")
            nmask = sbe("nmask", [16, 3, 16], BF16); b_nmask = T.buf("nmask")
            identb = sbe("identb", [128, 128], BF16); b_identb = T.buf("identb")
            identf = sbe("identf", [128, 128], F32); b_identf = T.buf("identf")
            ones = sbe("ones", [128, 128], BF16); b_ones = T.buf("ones")
            T.dma("pool", qkw[:], qkw_d, writes=[b_qkw])
            T.dma("pool", cw[:], cw_d, writes=[b_cw])
            T.dma("pool", rope[:], rope_d, writes=[b_rope])
            T.dma("pool", masks[:], masks_d, writes=[b_masks])
            T.dma("pool", smask[:], smask_d, writes=[b_smask])
            T.dma("pool", nmask[:], nmask_d, writes=[b_nmask])
            T.dma("pool", identb[:], ident_d, writes=[b_identb])
            T.dma("pool", identf[:], ident_d, writes=[b_identf])
            T.dma("pool", scTs[:], scT, writes=[b_scT])
            T.op("dve", lambda e: e.memset(ones[:], 1.0), writes=[b_ones])

            def post_block(W, pz, bpz, rows, g, tile, has_q, kT_dst, q_dst, v_dst, kv_out):
                i = W["i"]; W["i"] += 1
                s = i % WD
                qk, bqk = W["qk"][s], W["bqk"][s]
                junk, bjunk = W["junk"], W["bjunk"]
                ss, bss = W["ss"][s], W["bss"][s]
                tmp, btmp = W["tmp"][s], W["btmp"][s]
                qkb, bqkb = W["qkb"][s], W["bqkb"][s]
                vf, bvf = W["vf"][s], W["bvf"][s]
                ptr, bptr = W["ptr"], W["bptr"]
                o0 = 0 if has_q else -128
                slots = ([0] if has_q else []) + [1]
                for sl in slots:
                    c0 = o0 + sl * 128
                    T.op("act", lambda e: e.activation(out=junk[:rows, :], in_=pz[:rows, c0:c0 + 128], func=AF.Square,
                                                       accum_out=ss[:rows, sl:sl + 1]),
                         reads=[bpz], writes=[bjunk, bss])
                if not has_q:
                    T.op("dve", lambda e: e.memset(ss[:rows, 0:1], 1.0), writes=[bss])
                T.op("act", lambda e: e.activation(out=ss[:rows, :], in_=ss[:rows, :], func=AF.Ln, scale=1.0 / HD, bias=W["eps"][:rows, 0:1]),
                     reads=[bss, W["beps"]], writes=[bss])
                T.op("act", lambda e: e.activation(out=ss[:rows, :], in_=ss[:rows, :], func=AF.Exp, scale=-0.5),
                     reads=[bss], writes=[bss])
                for sl in slots:
                    c0 = o0 + sl * 128
                    T.op("dve", lambda e: e.scalar_tensor_tensor(out=qk[:rows, sl, :], in0=pz[:rows, c0:c0 + 128],
                                                                 scalar=ss[:rows, sl:sl + 1], in1=qkw[:rows, g, sl, :],
                                                                 op0=ALU.mult, op1=ALU.mult),
                         reads=[bpz, bss, b_qkw], writes=[bqk])
                if not has_q:
                    T.op("dve", lambda e: e.memset(qk[:rows, 0, :], 0.0), writes=[bqk])
                X = qk[:rows, :, 0:32]
                CS = bmid(rope[:rows, tile, 0:32], 2)
                SC = bmid(rope[:rows, tile, 16:48], 2)
                T.op("dve", lambda e: e.tensor_tensor(out=tmp[:rows, 0], in0=X, in1=CS, op=ALU.mult),
                     reads=[bqk, b_rope], writes=[btmp])
                T.op("dve", lambda e: e.tensor_tensor(out=tmp[:rows, 1], in0=X, in1=SC, op=ALU.mult),
                     reads=[bqk, b_rope], writes=[btmp])
                T.op("dve", lambda e: e.tensor_tensor(out=qk[:rows, :, 0:16], in0=tmp[:rows, 0, :, 0:16], in1=tmp[:rows, 0, :, 16:32],
                                                      op=ALU.subtract), reads=[btmp], writes=[bqk])
                T.op("dve", lambda e: e.tensor_tensor(out=qk[:rows, :, 16:32], in0=tmp[:rows, 1, :, 0:16], in1=tmp[:rows, 1, :, 16:32],
                                                      op=ALU.add), reads=[btmp], writes=[bqk])
                T.op("act", lambda e: e.activation(out=qkb[:rows], in_=qk[:rows], func=AF.Copy), reads=[bqk], writes=[bqkb])
                vc = o0 + 256
                T.op("dve", lambda e: e.tensor_copy(out=v_dst, in_=pz[:rows, vc:vc + 128]), reads=[bpz], writes=[W["bv_dst"]])
                if kv_out is not None:
                    T.op("act", lambda e: e.activation(out=vf[:rows, :], in_=pz[:rows, vc:vc + 128], func=AF.Copy),
                         reads=[bpz], writes=[bvf])
                    for (kd, vd, p0, p1) in kv_out:
                        T.dma("sp", kd, qk[p0:p1, 1, :], reads=[bqk])
                        T.dma("sp", vd, vf[p0:p1, :], reads=[bvf])
                for sl in slots:
                    T.op("pe", lambda e: e.transpose(out=ptr[:, sl * 128: sl * 128 + rows], in_=qkb[:rows, sl, :],
                                                     identity=identb[:rows, :rows]),
                         reads=[bqkb, b_identb], writes=[bptr])
                if has_q:
                    T.op("act", lambda e: e.activation(out=q_dst, in_=ptr[:, 0:rows], func=AF.Copy), reads=[bptr], writes=[W["bq_dst"]])
                T.op("act", lambda e: e.activation(out=kT_dst, in_=ptr[:, 128:128 + rows], func=AF.Copy), reads=[bptr], writes=[W["bk_dst"]])

            with ExitStack() as s1:
                def sb1(name, shape, dt=F32):
                    return s1.enter_context(nc.sbuf_tensor("s_s_" + name, list(shape), dt))

                WD = 3
                W = {"i": 0}
                W["qk"] = [sbe("qk%d" % i, [128, 2, 128]) for i in range(WD)]; W["bqk"] = T.bufs_n("qk", WD)
                W["junk"] = sbe("junk", [128, 128], BF16); W["bjunk"] = T.buf("junk")
                W["ss"] = [sbe("ss%d" % i, [128, 2]) for i in range(WD)]; W["bss"] = T.bufs_n("ss", WD)
                W["tmp"] = [sbe("tmp%d" % i, [128, 2, 2, 32]) for i in range(WD)]; W["btmp"] = T.bufs_n("tmp", WD)
                W["qkb"] = [sbe("qkb%d" % i, [128, 2, 128], BF16) for i in range(WD)]; W["bqkb"] = T.bufs_n("qkb", WD)
                W["vf"] = [sbe("vf%d" % i, [128, 128]) for i in range(WD)]; W["bvf"] = T.bufs_n("vf", WD)
                W["eps"] = sbe("epsc", [128, 1]); W["beps"] = T.buf("eps")
                T.op("dve", lambda e: e.memset(W["eps"][:], EPS), writes=[W["beps"]])
                W["bkvout"] = T.buf("kvout")
                ptr_ap = PS[2][:].bitcast(BF16)
                W["ptr"] = ptr_ap
                W["bptr"] = PSB[2]

                with ExitStack() as sA:
                    xnH = s1.enter_context(nc.sbuf_tensor("s_xnH", [128, 16, 2048], BF16)); b_xnH = T.buf("xnH")
                    xt = [sA.enter_context(nc.sbuf_tensor("s_xt%d" % i, [128, D_MODEL], F32)) for i in range(2)]
                    bxt = T.bufs_n("xt", 2)
                    xnb = [sA.enter_context(nc.sbuf_tensor("s_xnb%d" % i, [128, D_MODEL], BF16)) for i in range(2)]
                    bxnb = T.bufs_n("xnb", 2)
                    nw = sA.enter_context(nc.sbuf_tensor("s_nw", [128, D_MODEL], F32)); b_nw = T.buf("nw")
                    T.dma("pool", nw[:], nw_d, writes=[b_nw])
                    junkA = sA.enter_context(nc.sbuf_tensor("s_junkA", [128, D_MODEL], BF16)); bjunkA = T.buf("junkA")
                    ssA = [sA.enter_context(nc.sbuf_tensor("s_ssA%d" % i, [128, 1], F32)) for i in range(2)]
                    bssA = T.bufs_n("ssA", 2)
                    trA = [PS[0][:].bitcast(BF16), PS[1][:].bitcast(BF16)]
                    btrA = [PSB[0], PSB[1]]
                    for ti in range(33):
                        rows = 128 if ti < 32 else NSAMP
                        s = ti % 2
                        T.dma("sp", xt[s][:rows, :], x[ti * 128: ti * 128 + rows, :], writes=[bxt[s]])
                        T.op("act", lambda e: e.activation(out=junkA[:rows, :], in_=xt[s][:rows, :], func=AF.Square,
                                                           accum_out=ssA[s][:rows, 0:1]),
                             reads=[bxt[s]], writes=[bjunkA, bssA[s]])
                        T.op("act", lambda e: e.activation(out=ssA[s][:rows, :], in_=ssA[s][:rows, :], func=AF.Ln,
                                                           scale=1.0 / D_MODEL, bias=W["eps"][:rows, 0:1]),
                             reads=[bssA[s], W["beps"]], writes=[bssA[s]])
                        T.op("act", lambda e: e.activation(out=ssA[s][:rows, :], in_=ssA[s][:rows, :], func=AF.Exp, scale=-0.5),
                             reads=[bssA[s]], writes=[bssA[s]])
                        T.op("dve", lambda e: e.scalar_tensor_tensor(out=xnb[s][:rows, :], in0=xt[s][:rows, :],
                                                                     scalar=ssA[s][:rows, 0:1], in1=nw[:rows, :],
                                                                     op0=ALU.mult, op1=ALU.mult),
                             reads=[bxt[s], bssA[s], b_nw], writes=[bxnb[s]])
                        for half in range(2):
                            tr, btr = trA[half], btrA[half]
                            for k in range(8):
                                kc = half * 8 + k
                                T.op("pe", lambda e: e.transpose(out=tr[:, k * 128: k * 128 + rows],
                                                                 in_=xnb[s][:rows, kc * 128:(kc + 1) * 128],
                                                                 identity=identb[:rows, :rows]),
                                     reads=[bxnb[s], b_identb], writes=[btr])
                            src = tr.rearrange("p (k t) -> p k t", k=8)[:, :, 0:rows]
                            eng = "act" if half == 0 else "dve"
                            if ti < 16:
                                dst, bd = xnH[:, half * 8:(half + 1) * 8, ti * 128: ti * 128 + rows], b_xnH
                            elif ti < 32:
                                dst, bd = xnT[:, half * 8:(half + 1) * 8, (ti - 16) * 128:(ti - 16) * 128 + rows], b_xnT
                            else:
                                dst, bd = xnS[:, half * 8:(half + 1) * 8, 2:18], b_xnS
                            if eng == "act":
                                T.op("act", lambda e: e.activation(out=dst, in_=src, func=AF.Copy), reads=[btr], writes=[bd])
                            else:
                                T.op("dve", lambda e: e.tensor_copy(out=dst, in_=src), reads=[btr], writes=[bd])
                    T.op("dve", lambda e: e.tensor_copy(out=xnS[:, :, 0:2], in_=xnH[:, :, 2046:2048]), reads=[b_xnH], writes=[b_xnS])

                T.barrier()
                _stage("A")
                PZ = [(PS[0], PSB[0]), (PS[1], PSB[1]), (PS[7], PSB[7])]
                pzc = [0]
                LA = 2
                _bhk = T.buf("hK"); b_hK = [[_bhk] * NH for g in range(3)]
                _bhv = T.buf("hV"); b_hV = [[_bhv] * NH for g in range(3)]
                with ExitStack() as s0:
                    wkv = [s0.enter_context(nc.sbuf_tensor("s_wkv%d" % i, [128, 16, 256], BF16)) for i in range(2)]
                    bwkv = T.bufs_n("wkv", 2)
                    kst = [s0.enter_context(nc.sbuf_tensor("s_kst%d" % i, [128, 2048], BF16)) for i in range(2)]
                    bkst = T.bufs_n("kst", 2)
                    vst = [s0.enter_context(nc.sbuf_tensor("s_vst%d" % i, [128, 16, 128], BF16)) for i in range(2)]
                    bvst = T.bufs_n("vst", 2)
                    u = 0
                    for h in range(NH):
                        for g in range(3):
                            d = DILS[g]
                            s = u % 2
                            u += 1
                            T.dma("pool", wkv[s][:], wqkv[h, g, :, :, 128:384], writes=[bwkv[s]])
                            W["bk_dst"] = bkst[s]; W["bv_dst"] = bvst[s]; W["bq_dst"] = None
                            def s0_proj(r):
                                pz, bpz = PZ[pzc[0] % 3]
                                pzc[0] += 1
                                start = 2048 - 128 * d + r
                                for kc in range(16):
                                    T.op("pe", lambda e: e.matmul(pz[:, 0:256], xnH[:, kc, start:start + 127 * d + 1:d],
                                                                  wkv[s][:, kc, :], start=(kc == 0), stop=(kc == 15)),
                                         reads=[b_xnH, bwkv[s]], writes=[bpz])
                                return pz, bpz
                            pend = {}
                            for n in range(d + LA):
                                if n < d:
                                    pend[n] = s0_proj(n)
                                if n >= LA:
                                    r = n - LA
                                    pz, bpz = pend.pop(r)
                                    post_block(W, pz, bpz, 128, g, ROPE_BASE[g] + r * (16 // d + 1), False,
                                               kst[s][:, r * 128:(r + 1) * 128], None, vst[s][:, r, :], None)
                            T.dma("sp", hK[g, h, :, 0:d * 128], kst[s][:, 0:d * 128], reads=[bkst[s]], writes=[b_hK[g][h]])
                            T.dma("sp", hV[g, h, :, 0:d, :], vst[s][:, 0:d, :], reads=[bvst[s]], writes=[b_hV[g][h]])

                T.barrier()
                _stage("S0")
                s1.close()
                agT = sbe("agT", [128, NH, TOK + NSAMP], BF16)
                wq = [sb1("wq0", [128, 16, 384], BF16)] * 2; bwq = [T.buf("wq")] * 2
                wg = [sb1("wg0", [128, 16, 128], BF16)] * 2; bwg = [T.buf("wg")] * 2
                QT = [sb1("QT0", [128, 2048], BF16)] * 2; bQT = [T.buf("QT")] * 2
                KT = [sb1("KT0", [128, 20 * 128], BF16)] * 2; bKT = [T.buf("KT")] * 2
                VV = [sb1("VV0", [128, 20, 128], BF16)] * 2; bVV = [T.buf("VV")] * 2
                OL = sb1("OL", [128, 2, TOK]); bOL = T.buf("OL")
                EX = [sb1("EX%d" % i, [128, 256], BF16) for i in range(2)]; bEX = T.bufs_n("EX", 2)
                PP = [sb1("PP%d" % i, [128, 256], BF16) for i in range(2)]; bPP = T.bufs_n("PP", 2)
                sgt = [sb1("sgt%d" % i, [128, 512]) for i in range(2)]; bsgt = T.bufs_n("sgt", 2)
                ot = [sb1("ot%d" % i, [128, 512]) for i in range(2)]; bot = T.bufs_n("ot", 2)
                QsT = sb1("QsT", [128, 3, NH, NSAMP], BF16); bQsT = T.buf("QsT")
                KsT = sb1("KsT", [128, 3, NH, NSAMP], BF16); bKsT = T.buf("KsT")
                Vs = sb1("Vs", [NSAMP, 3, NH, 128], BF16); bVs = T.buf("Vs")
                sgS = sb1("sgS", [128, NH, NSAMP]); bsgS = T.buf("sgS")

                units = []
                for h in range(NH):
                    for g in range(3):
                        d = DILS[g]
                        parts = 2 if g == 2 else 1
                        for p in range(parts):
                            rs = list(range(d)) if parts == 1 else list(range(8 * p, 8 * p + 8))
                            units.append((h, g, p, rs))

                def load_unit_w(ui):
                    h, g, p, rs = units[ui]
                    if p == 0:
                        slot = (h * 3 + g) % 2
                        T.dma("pool", wq[slot][:], wqkv[h, g], writes=[bwq[slot]])

                cc_bufs = [T.buf("skvc%d" % g) for g in range(3)]
                cc_chunks = []
                for g in (2, 1, 0):
                    L = CACHE_LEN[g]
                    for b in range(4):
                        r0 = 0
                        while r0 < L - 4:
                            nr_ = min(256, L - 4 - r0)
                            cc_chunks.append((g, b, r0, nr_))
                            r0 += nr_
                cc_pos = [0]

                def emit_cc(k):
                    for _ in range(k):
                        if cc_pos[0] < len(cc_chunks):
                            g_, b_, r0, nr_ = cc_chunks[cc_pos[0]]
                            cc_pos[0] += 1
                            T.dma("act", skv_o[g_][b_, r0:r0 + nr_], ck[g_][b_, 4 + r0:4 + r0 + nr_], writes=[cc_bufs[g_]],
                                  nobarrier=True, free=True)
                load_unit_w(0)
                first_in_head = True
                for ui, (h, g, p, rs) in enumerate(units):
                    d = DILS[g]
                    nkb = 16 // d + 1
                    us = ui % 2
                    slot = (h * 3 + g) % 2
                    emit_cc(2)
                    if g == 0 and p == 0:
                        T.dma("pool", wg[h % 2][:], wag[h], writes=[bwg[h % 2]])
                    W["bk_dst"] = bKT[us]; W["bv_dst"] = bVV[us]; W["bq_dst"] = bQT[us]
                    nr = len(rs)
                    kdst = KT[us][:, 0:nr * nkb * 128].rearrange("p (r k c) -> p r k c", r=nr, k=nkb)[:, :, 0, :]
                    T.dma("pool", kdst, hK[g, h, :, rs[0] * 128:(rs[0] + nr) * 128].rearrange("p (r c) -> p r c", r=nr),
                          reads=[b_hK[g][h]], writes=[bKT[us]])
                    vdst = VV[us][:, 0:nr * nkb, :].rearrange("p (r k) c -> p r k c", r=nr)[:, :, 0, :]
                    T.dma("pool", vdst, hV[g, h, :, rs[0]:rs[0] + nr, :], reads=[b_hV[g][h]], writes=[bVV[us]])
                    blist = []
                    for rl, r in enumerate(rs):
                        for kb in range(1, nkb):
                            blist.append((rl, r, kb))
                    if p == 0:
                        blist.append(None)

                    def s1_proj(item):
                        pz, bpz = PZ[pzc[0] % 3]
                        pzc[0] += 1
                        if item is None:
                            for kc in range(16):
                                T.op("pe", lambda e: e.matmul(pz[:NSAMP, 0:384], xnS[:, kc, 2:18], wq[slot][:, kc, :],
                                                              start=(kc == 0), stop=(kc == 15)),
                                     reads=[b_xnS, bwq[slot]], writes=[bpz])
                        else:
                            rl, r, kb = item
                            start = (kb - 1) * 128 * d + r
                            for kc in range(16):
                                T.op("pe", lambda e: e.matmul(pz[:, 0:384], xnT[:, kc, start:start + 127 * d + 1:d],
                                                              wq[slot][:, kc, :], start=(kc == 0), stop=(kc == 15)),
                                     reads=[b_xnT, bwq[slot]], writes=[bpz])
                        return pz, bpz

                    def s1_post(item, pz, bpz):
                        if item is None:
                            L = CACHE_LEN[g]
                            kv_out = []
                            for b in range(4):
                                kv_out.append((skv_o[g][b, L - 4:L, 0, h * 128:(h + 1) * 128],
                                               skv_o[g][b, L - 4:L, 1, h * 128:(h + 1) * 128], 4 * b, 4 * b + 4))
                            W["bk_dst"] = bKsT; W["bv_dst"] = bVs; W["bq_dst"] = bQsT
                            post_block(W, pz, bpz, NSAMP, g, ROPE_SAMPLE, True, KsT[:, g, h, :], QsT[:, g, h, :], Vs[:, g, h, :], kv_out)
                            W["bk_dst"] = bKT[us]; W["bv_dst"] = bVV[us]; W["bq_dst"] = bQT[us]
                            return
                        rl, r, kb = item
                        bi = rl * nkb + kb
                        qi = rl * (nkb - 1) + (kb - 1)
                        kv_out = None
                        lo_tok = (kb - 1) * 128 * d + r
                        keep0 = TOK - CACHE_LEN[g]
                        if lo_tok >= keep0:
                            row0 = lo_tok - keep0
                            kd = kv_o[g][row0:row0 + 127 * d + 1:d, 0, h * 128:(h + 1) * 128]
                            vd = kv_o[g][row0:row0 + 127 * d + 1:d, 1, h * 128:(h + 1) * 128]
                            kv_out = [(kd, vd, 0, 128)]
                        post_block(W, pz, bpz, 128, g, ROPE_BASE[g] + r * nkb + kb, True,
                                   KT[us][:, bi * 128:(bi + 1) * 128], QT[us][:, qi * 128:(qi + 1) * 128],
                                   VV[us][:, bi, :], kv_out)

                    pend = {}
                    NBk = len(blist)
                    for n in range(NBk + LA):
                        if n < NBk:
                            pend[n] = s1_proj(blist[n])
                        if n >= LA:
                            pz, bpz = pend.pop(n - LA)
                            s1_post(blist[n - LA], pz, bpz)
                    if ui + 1 < len(units):
                        load_unit_w(ui + 1)
                    if ui + 1 < len(units):
                        load_unit_w(ui + 1)
                    qlist = []
                    for rl, r in enumerate(rs):
                        for kq in range(1, nkb):
                            qlist.append((rl, r, kq))

                    def att_scores(n):
                        rl, r, kq = qlist[n]
                        bi = rl * nkb + kq
                        qi = rl * (nkb - 1) + (kq - 1)
                        a = n % 2
                        pS, bpS = PS[3 + a], PSB[3 + a]
                        T.op("pe", lambda e: e.matmul(pS[:, 0:128], KT[us][:, (bi - 1) * 128: bi * 128], QT[us][:, qi * 128:(qi + 1) * 128],
                                                      start=True, stop=True), reads=[bKT[us], bQT[us]], writes=[bpS])
                        T.op("pe", lambda e: e.matmul(pS[:, 128:256], KT[us][:, bi * 128:(bi + 1) * 128], QT[us][:, qi * 128:(qi + 1) * 128],
                                                      start=True, stop=True), reads=[bKT[us], bQT[us]], writes=[bpS])
                        T.op("act", lambda e: e.activation(out=EX[a][:], in_=pS[:, 0:256], func=AF.Exp, scale=SCALE),
                             reads=[bpS], writes=[bEX[a]])
                        mk = masks[:, 1, :] if kq == 1 else masks[:, 0, :]
                        T.op("dve", lambda e: e.tensor_tensor(out=PP[a][:], in0=EX[a][:], in1=mk, op=ALU.mult),
                             reads=[bEX[a], b_masks], writes=[bPP[a]])

                    def att_pv(n):
                        rl, r, kq = qlist[n]
                        bi = rl * nkb + kq
                        a = n % 2
                        pO, bpO = PS[5 + a], PSB[5 + a]
                        T.op("pe", lambda e: e.matmul(pO[:, 0:128], VV[us][:, bi - 1, :], PP[a][:, 0:128], start=True, stop=False),
                             reads=[bVV[us], bPP[a]], writes=[bpO])
                        T.op("pe", lambda e: e.matmul(pO[:, 0:128], VV[us][:, bi, :], PP[a][:, 128:256], start=False, stop=True),
                             reads=[bVV[us], bPP[a]], writes=[bpO])
                        T.op("pe", lambda e: e.matmul(pO[:, 128:256], ones[:], PP[a][:, 0:128], start=True, stop=False),
                             reads=[b_ones, bPP[a]], writes=[bpO])
                        T.op("pe", lambda e: e.matmul(pO[:, 128:256], ones[:], PP[a][:, 128:256], start=False, stop=True),
                             reads=[b_ones, bPP[a]], writes=[bpO])
                        t0 = (kq - 1) * 128 * d + r
                        dst = OL[:, :, t0:t0 + 127 * d + 1:d]
                        src_ = pO[:, 0:256].rearrange("p (a b) -> p a b", a=2)
                        if first_in_head:
                            T.op("act", lambda e: e.activation(out=dst, in_=src_, func=AF.Copy), reads=[bpO], writes=[bOL])
                        else:
                            T.op("dve", lambda e: e.tensor_tensor(out=dst, in0=src_, in1=dst, op=ALU.add), reads=[bpO, bOL], writes=[bOL])

                    NQ = len(qlist)
                    for n in range(NQ + 1):
                        if n < NQ:
                            att_scores(n)
                        if n >= 1:
                            att_pv(n - 1)
                    if g == 0:
                        first_in_head = False
                    if g == 2 and p == 1:
                        first_in_head = True
                        if _DEBUG[0] and h == 0:
                            dbgOL = dout("dbgOL", [128, 2, TOK])
                            T.dma("sp", dbgOL, OL[:], reads=[bOL])
                        T.op("dve", lambda e: e.reciprocal(out=OL[:, 1, :], in_=OL[:, 1, :]), reads=[bOL], writes=[bOL])
                        gs = h % 2
                        blocks = [(tb * 512, 512, xnT, b_xnT, tb * 512) for tb in range(4)] + [(TOK, NSAMP, xnS, b_xnS, 2)]
                        pbs = [(PS[0], PSB[0]), (PS[1], PSB[1]), (PS[3], PSB[3]), (PS[4], PSB[4]), (PS[7], PSB[7])]
                        for kc in range(16):
                            for bi_, (c0, n, src_t, src_b, sc0) in enumerate(blocks):
                                pb, bpb = pbs[bi_]
                                T.op("pe", lambda e: e.matmul(pb[:, 0:n], wg[gs][:, kc, :], src_t[:, kc, sc0:sc0 + n],
                                                              start=(kc == 0), stop=(kc == 15)),
                                     reads=[bwg[gs], src_b], writes=[bpb])
                        for bi_, (c0, n, src_t, src_b, sc0) in enumerate(blocks):
                            pb, bpb = pbs[bi_]
                            if bi_ < 4:
                                a = bi_ % 2
                                T.op("act", lambda e: e.activation(out=sgt[a][:], in_=pb[:, 0:512], func=AF.Silu), reads=[bpb], writes=[bsgt[a]])
                                if _DEBUG[0] and h == 0 and bi_ == 0:
                                    dbgsg = dout("dbgsg", [128, 512])
                                    T.dma("sp", dbgsg, sgt[a][:], reads=[bsgt[a]])
                                T.op("dve", lambda e: e.tensor_tensor(out=ot[a][:], in0=OL[:, 0, c0:c0 + 512], in1=OL[:, 1, c0:c0 + 512], op=ALU.mult),
                                     reads=[bOL], writes=[bot[a]])
                                T.op("dve", lambda e: e.tensor_tensor(out=agT[:, h, c0:c0 + 512], in0=ot[a][:], in1=sgt[a][:], op=ALU.mult),
                                     reads=[bot[a], bsgt[a]], writes=[b_agT])
                            else:
                                T.op("act", lambda e: e.activation(out=sgS[:, h, :], in_=pb[:, 0:NSAMP], func=AF.Silu), reads=[bpb], writes=[bsgS])

                emit_cc(len(cc_chunks))
                T.barrier()
                _stage("S1")
                with ExitStack() as ss_:
                    kt_ = [ss_.enter_context(nc.sbuf_tensor("s_skt%d" % i, [128, 1024], BF16)) for i in range(2)]; bkt_ = T.bufs_n("skt", 2)
                    vt_ = [ss_.enter_context(nc.sbuf_tensor("s_svt%d" % i, [128, 1024], BF16)) for i in range(2)]; bvt_ = T.bufs_n("svt", 2)
                    ktT = [ss_.enter_context(nc.sbuf_tensor("s_sktT%d" % i, [128, 1024], BF16)) for i in range(2)]; bktT = T.bufs_n("sktT", 2)
                    pe_ = [ss_.enter_context(nc.sbuf_tensor("s_spe%d" % i, [128, NH, NSAMP], BF16)) for i in range(2)]; bpe_ = T.bufs_n("spe", 2)
                    pp_ = [ss_.enter_context(nc.sbuf_tensor("s_spp%d" % i, [128, NH, NSAMP], BF16)) for i in range(2)]; bpp_ = T.bufs_n("spp", 2)
                    osb = ss_.enter_context(nc.sbuf_tensor("s_osb", [NSAMP, 1024], F32)); bosb = T.buf("osb")
                    lsb = ss_.enter_context(nc.sbuf_tensor("s_lsb", [NSAMP, NH], F32)); blsb = T.buf("lsb")
                    accO = [PS[5], PS[6]]; baccO = [PSB[5], PSB[6]]
                    accL, baccL = PS[7], PSB[7]
                    trp = PS[2][:].bitcast(BF16); btrp = PSB[2]
                    tiles = []
                    for b in range(4):
                        tiles.append((0, b, 0, 0))
                        for g in (1, 2):
                            for t in range(4):
                                tiles.append((g, b, t, 1 + (g - 1) * 4 + t))
                    for ti, (g, b, t, mi) in enumerate(tiles):
                        d = DILS[g]
                        s = ti % 2
                        T.dma("pool", kt_[s][:], ck[g][b, t:t + 127 * d + 1:d, 0, :], writes=[bkt_[s]])
                        T.dma("pool", vt_[s][:], ck[g][b, t:t + 127 * d + 1:d, 1, :], writes=[bvt_[s]])
                        for h in range(NH):
                            T.op("pe", lambda e: e.transpose(out=trp[:, h * 128:(h + 1) * 128], in_=kt_[s][:, h * 128:(h + 1) * 128],
                                                             identity=identb[:]), reads=[bkt_[s], b_identb], writes=[btrp])
                        T.op("act", lambda e: e.activation(out=ktT[s][:], in_=trp[:, 0:1024], func=AF.Copy), reads=[btrp], writes=[bktT[s]])
                        pS, bpS = PS[3 + s], PSB[3 + s]
                        for h in range(NH):
                            T.op("pe", lambda e: e.matmul(pS[:, h * NSAMP:(h + 1) * NSAMP], ktT[s][:, h * 128:(h + 1) * 128], QsT[:, g, h, :],
                                                          start=True, stop=True), reads=[bktT[s], bQsT], writes=[bpS])
                        T.op("act", lambda e: e.activation(out=pe_[s][:], in_=pS[:, 0:NH * NSAMP].rearrange("p (h t) -> p h t", h=NH),
                                                           func=AF.Exp, scale=SCALE), reads=[bpS], writes=[bpe_[s]])
                        T.op("dve", lambda e: e.tensor_tensor(out=pp_[s][:], in0=pe_[s][:], in1=bmid(smask[:, b * 9 + mi, :], NH), op=ALU.mult),
                             reads=[bpe_[s], b_smask], writes=[bpp_[s]])
                        for h in range(NH):
                            T.op("pe", lambda e: e.matmul(accO[h // 4][:NSAMP, (h % 4) * 128:(h % 4 + 1) * 128], pp_[s][:, h, :],
                                                          vt_[s][:, h * 128:(h + 1) * 128], start=(ti == 0 and h % 4 == 0), stop=False),
                                 reads=[bpp_[s], bvt_[s]], writes=[baccO[h // 4]])
                            T.op("pe", lambda e: e.matmul(accL[:NSAMP, h:h + 1], pp_[s][:, h, :], ones[:, 0:1], start=(ti == 0 and h == 0), stop=False),
                                 reads=[bpp_[s], b_ones], writes=[baccL])
                    ne_ = ss_.enter_context(nc.sbuf_tensor("s_sne", [NSAMP, NH, NSAMP], BF16)); bne_ = T.buf("sne")
                    np_ = ss_.enter_context(nc.sbuf_tensor("s_snp", [NSAMP, NH, NSAMP], BF16)); bnp_ = T.buf("snp")
                    for g in range(3):
                        pS, bpS = PS[3 + g % 2], PSB[3 + g % 2]
                        for h in range(NH):
                            T.op("pe", lambda e: e.matmul(pS[:NSAMP, h * NSAMP:(h + 1) * NSAMP], KsT[:, g, h, :], QsT[:, g, h, :],
                                                          start=True, stop=True), reads=[bKsT, bQsT], writes=[bpS])
                        T.op("act", lambda e: e.activation(out=ne_[:], in_=pS[:NSAMP, 0:NH * NSAMP].rearrange("p (h t) -> p h t", h=NH),
                                                           func=AF.Exp, scale=SCALE), reads=[bpS], writes=[bne_])
                        T.op("dve", lambda e: e.tensor_tensor(out=np_[:], in0=ne_[:], in1=bmid(nmask[:, g, :], NH), op=ALU.mult),
                             reads=[bne_, b_nmask], writes=[bnp_])
                        for h in range(NH):
                            last = (g == 2)
                            T.op("pe", lambda e: e.matmul(accO[h // 4][:NSAMP, (h % 4) * 128:(h % 4 + 1) * 128], np_[:, h, :],
                                                          Vs[:, g, h, :], start=False, stop=last),
                                 reads=[bnp_, bVs], writes=[baccO[h // 4]])
                            T.op("pe", lambda e: e.matmul(accL[:NSAMP, h:h + 1], np_[:, h, :], ones[:NSAMP, 0:1], start=False, stop=last),
                                 reads=[bnp_, b_ones], writes=[baccL])
                    T.op("dve", lambda e: e.reciprocal(out=lsb[:], in_=accL[:NSAMP, 0:NH]), reads=[baccL], writes=[blsb])
                    for hh in range(2):
                        la = lsb[:, hh * 4:(hh + 1) * 4]
                        lb_ = bass.AP(la.tensor, la.offset, [list(la.ap[0]), [1, 4], [0, 128]])
                        T.op("dve", lambda e: e.tensor_tensor(out=osb[:, hh * 512:(hh + 1) * 512].rearrange("p (h c) -> p h c", h=4),
                                                              in0=accO[hh][:NSAMP, :].rearrange("p (h c) -> p h c", h=4),
                                                              in1=lb_, op=ALU.mult),
                             reads=[baccO[hh], blsb], writes=[bosb])
                    pT, bpT = PS[0], PSB[0]
                    for h in range(NH):
                        T.op("pe", lambda e: e.transpose(out=pT[:, h * NSAMP:(h + 1) * NSAMP], in_=osb[:, h * 128:(h + 1) * 128],
                                                         identity=identf[:NSAMP, :NSAMP]), reads=[bosb, b_identf], writes=[bpT])
                    T.op("dve", lambda e: e.tensor_tensor(out=agT[:, :, TOK:TOK + NSAMP],
                                                          in0=pT[:, 0:NH * NSAMP].rearrange("p (h t) -> p h t", h=NH),
                                                          in1=sgS[:], op=ALU.mult), reads=[bpT, bsgS], writes=[b_agT])

                _stage("S1s")
                ags = (dout if _DEBUG[0] else dscr)("ags", [128, NH, TOK + NSAMP], BF16); b_ags = T.buf("ags")
                T.dma("sp", ags, agT[:], reads=[b_agT], writes=[b_ags])
            sE.close()

            T.barrier()
            with ExitStack() as s2:
                def sb2(name, shape, dt=F32):
                    return s2.enter_context(nc.sbuf_tensor("s_s_" + name, list(shape), dt))

                cyT = sb2("cyT", [128, 16, TOK + NSAMP], BF16); b_cyT = T.buf("cyT")
                wsl = [sb2("wsl%d" % i, [128, 16, 128], BF16) for i in range(6)]; bwsl = T.bufs_n("wsl", 6)
                wrr = [0]

                def load_slab(src, nkc=16):
                    i = wrr[0] % 6
                    wrr[0] += 1
                    T.dma("pool", wsl[i][:, 0:nkc, :], src, writes=[bwsl[i]])
                    return wsl[i], bwsl[i]

                OWN = [(tb * 512, 512) for tb in range(4)]

                PASSES = [[0, 1, 4], [2, 3]]

                def proj_fm(slab, bslab, nkc, act_own, b_own, act_s, b_s, s0, sn, which):
                    outs = {bi_: nextbank() for bi_ in which}
                    for kc in range(nkc):
                        for bi_ in which:
                            pb, bpb = outs[bi_]
                            if bi_ < 4:
                                c0, n = OWN[bi_]
                                T.op("pe", lambda e: e.matmul(pb[:, 0:n], slab[:, kc, :], act_own[:, kc, c0:c0 + n],
                                                              start=(kc == 0), stop=(kc == nkc - 1)),
                                     reads=[bslab, b_own], writes=[bpb])
                            else:
                                T.op("pe", lambda e: e.matmul(pb[:, 0:sn], slab[:, kc, :], act_s[:, kc, s0:s0 + sn],
                                                              start=(kc == 0), stop=(kc == nkc - 1)),
                                     reads=[bslab, b_s], writes=[bpb])
                    return outs

                with ExitStack() as sc:
                    def sbc(name, shape, dt=F32):
                        return sc.enter_context(nc.sbuf_tensor("s_s_" + name, list(shape), dt))
                    uext = [sbc("uext%d" % i, [128, TOK + 2]) for i in range(2)]; buext = T.bufs_n("uext", 2)
                    usx = [sbc("usx%d" % i, [128, 4, 6]) for i in range(2)]; busx = T.bufs_n("usx", 2)
                    hS = [sbc("hS%d" % i, [128, 512]) for i in range(4)]; bhS = T.bufs_n("hS", 4)
                    hs_s = sbc("hs_s", [128, 18]); bhs_s = T.buf("hs_s")
                    us_s = sbc("us_s", [128, 18]); bus_s = T.buf("us_s")
                    acc = [sbc("acc%d" % i, [128, 512]) for i in range(2)]; bacc = T.bufs_n("acc", 2)
                    yv = [sbc("yv%d" % i, [128, 512]) for i in range(2)]; byv = T.bufs_n("yv", 2)
                    sg2 = [sbc("sg2%d" % i, [128, 512]) for i in range(2)]; bsg2 = T.bufs_n("sg2", 2)
                    accs = sbc("accs", [128, 4, 4]); baccs = T.buf("accs")
                    ys = sbc("ys", [128, 4, 4]); bys = T.buf("ys")
                    sgs2 = sbc("sgs2", [128, 4, 4]); bsgs2 = T.buf("sgs2")
                    ncvS = sbc("ncvS", [128, 16, 2]); bncvS = T.buf("ncvS")
                    sncvS = sbc("sncvS", [128, 16, 4, 2]); bsncvS = T.buf("sncvS")
                    for j in range(16):
                        ue, bue = uext[j % 2], buext[j % 2]
                        ux, bux = usx[j % 2], busx[j % 2]
                        sl_h = load_slab(wcv[j, 0]); sl_c = load_slab(wcv[j, 1])
                        sl_b = load_slab(wcv[j, 2]); sl_g = load_slab(wcv[j, 3])
                        for ps_ in PASSES:
                            ph = proj_fm(sl_h[0], sl_h[1], 16, xnT, b_xnT, xnS, b_xnS, 0, 18, ps_)
                            pc = proj_fm(sl_c[0], sl_c[1], 16, xnT, b_xnT, xnS, b_xnS, 0, 18, ps_)
                            for bi_ in ps_:
                                pb, bpb = ph[bi_]
                                pb2, bpb2 = pc[bi_]
                                if bi_ < 4:
                                    c0, n = OWN[bi_]
                                    T.op("act", lambda e: e.activation(out=hS[bi_][:], in_=pb[:, 0:512], func=AF.Copy), reads=[bpb], writes=[bhS[bi_]])
                                    T.op("dve", lambda e: e.tensor_tensor(out=ue[:, 2 + c0:2 + c0 + n], in0=pb2[:, 0:n], in1=hS[bi_][:], op=ALU.mult),
                                         reads=[bpb2, bhS[bi_]], writes=[bue])
                                else:
                                    T.op("act", lambda e: e.activation(out=hs_s[:], in_=pb[:, 0:18], func=AF.Copy), reads=[bpb], writes=[bhs_s])
                                    T.op("dve", lambda e: e.tensor_tensor(out=us_s[:], in0=pb2[:, 0:18], in1=hs_s[:], op=ALU.mult),
                                         reads=[bpb2, bhs_s], writes=[bus_s])
                                    T.op("dve", lambda e: e.tensor_copy(out=ue[:, 0:2], in_=us_s[:, 0:2]), reads=[bus_s], writes=[bue])
                                    T.op("dve", lambda e: e.tensor_copy(out=ux[:, :, 2:6], in_=us_s[:, 2:18].rearrange("p (b t) -> p b t", b=4)),
                                         reads=[bus_s], writes=[bux])
                                    T.op("dve", lambda e: e.tensor_copy(out=ux[:, :, 0:2], in_=scTs[:, j, :, :]), reads=[b_scT], writes=[bux])
                        T.op("dve", lambda e: e.tensor_copy(out=ncvS[:, j, :], in_=ue[:, TOK:TOK + 2]), reads=[bue], writes=[bncvS])
                        T.op("dve", lambda e: e.tensor_copy(out=sncvS[:, j, :, :], in_=ux[:, :, 4:6]), reads=[bux], writes=[bsncvS])
                        for ps_ in PASSES:
                            pbb = proj_fm(sl_b[0], sl_b[1], 16, xnT, b_xnT, xnS, b_xnS, 0, 18, ps_)
                            pgg = proj_fm(sl_g[0], sl_g[1], 16, xnT, b_xnT, xnS, b_xnS, 0, 18, ps_)
                            for bi_ in ps_:
                                pb, bpb = pbb[bi_]
                                pg, bpg = pgg[bi_]
                                if bi_ < 4:
                                    c0, n = OWN[bi_]
                                    a = bi_ % 2
                                    T.op("act", lambda e: e.activation(out=acc[a][:], in_=ue[:, 2 + c0:2 + c0 + n], func=AF.Copy, scale=cw[:, j, 2:3]),
                                         reads=[bue, b_cw], writes=[bacc[a]])
                                    T.op("dve", lambda e: e.scalar_tensor_tensor(out=acc[a][:], in0=ue[:, 1 + c0:1 + c0 + n], scalar=cw[:, j, 1:2],
                                                                                 in1=acc[a][:], op0=ALU.mult, op1=ALU.add),
                                         reads=[bue, b_cw, bacc[a]], writes=[bacc[a]])
                                    T.op("dve", lambda e: e.scalar_tensor_tensor(out=acc[a][:], in0=ue[:, c0:c0 + n], scalar=cw[:, j, 0:1],
                                                                                 in1=acc[a][:], op0=ALU.mult, op1=ALU.add),
                                         reads=[bue, b_cw, bacc[a]], writes=[bacc[a]])
                                    T.op("dve", lambda e: e.tensor_tensor(out=yv[a][:], in0=pb[:, 0:n], in1=acc[a][:], op=ALU.mult),
                                         reads=[bpb, bacc[a]], writes=[byv[a]])
                                    T.op("act", lambda e: e.activation(out=sg2[a][:], in_=pg[:, 0:n], func=AF.Silu), reads=[bpg], writes=[bsg2[a]])
                                    T.op("dve", lambda e: e.tensor_tensor(out=cyT[:, j, c0:c0 + n], in0=yv[a][:], in1=sg2[a][:], op=ALU.mult),
                                         reads=[byv[a], bsg2[a]], writes=[b_cyT])
                                else:
                                    T.op("act", lambda e: e.activation(out=accs[:], in_=ux[:, :, 2:6], func=AF.Copy, scale=cw[:, j, 2:3]),
                                         reads=[bux, b_cw], writes=[baccs])
                                    T.op("dve", lambda e: e.scalar_tensor_tensor(out=accs[:], in0=ux[:, :, 1:5], scalar=cw[:, j, 1:2], in1=accs[:],
                                                                                 op0=ALU.mult, op1=ALU.add), reads=[bux, b_cw, baccs], writes=[baccs])
                                    T.op("dve", lambda e: e.scalar_tensor_tensor(out=accs[:], in0=ux[:, :, 0:4], scalar=cw[:, j, 0:1], in1=accs[:],
                                                                                 op0=ALU.mult, op1=ALU.add), reads=[bux, b_cw, baccs], writes=[baccs])
                                    T.op("dve", lambda e: e.tensor_tensor(out=ys[:], in0=pb[:, 2:18].rearrange("p (b t) -> p b t", b=4), in1=accs[:], op=ALU.mult),
                                         reads=[bpb, baccs], writes=[bys])
                                    T.op("act", lambda e: e.activation(out=sgs2[:], in_=pg[:, 2:18].rearrange("p (b t) -> p b t", b=4), func=AF.Silu),
                                         reads=[bpg], writes=[bsgs2])
                                    T.op("dve", lambda e: e.tensor_tensor(out=cyT[:, j, TOK:TOK + NSAMP].rearrange("p (b t) -> p b t", b=4), in0=ys[:], in1=sgs2[:],
                                                                          op=ALU.mult), reads=[bys, bsgs2], writes=[b_cyT])
                    T.dma("sp", ncv_o, ncvS[:], reads=[bncvS])
                    T.dma("sp", sncv_o, sncvS[:], reads=[bsncvS])

                T.barrier()
                _stage("S2")
                _bt2 = T.buf("t2s"); b_t2s = [_bt2] * 16
                with ExitStack() as sc:
                    def sbc(name, shape, dt=F32):
                        return sc.enter_context(nc.sbuf_tensor("s_s_" + name, list(shape), dt))
                    sgm = [sbc("sgm%d" % i, [128, 512]) for i in range(2)]; bsgm = T.bufs_n("sgm", 2)
                    t2o = [sbc("t2o%d" % i, [128, TOK + NSAMP], BF16) for i in range(2)]; bt2o = T.bufs_n("t2o", 2)
                    for i in range(16):
                        sl_m = load_slab(wml[i, 1]); sl_p = load_slab(wcp[i])
                        to, bto = t2o[i % 2], bt2o[i % 2]
                        BL = OWN + [(TOK, NSAMP)]
                        for ps_ in PASSES:
                            pm = proj_fm(sl_m[0], sl_m[1], 16, xnT, b_xnT, xnS, b_xnS, 2, NSAMP, ps_)
                            pp2 = proj_fm(sl_p[0], sl_p[1], 16, cyT, b_cyT, cyT, b_cyT, TOK, NSAMP, ps_)
                            for bi_ in ps_:
                                c0, n = BL[bi_]
                                a = bi_ % 2
                                T.op("act", lambda e: e.activation(out=sgm[a][:, 0:n], in_=pm[bi_][0][:, 0:n], func=AF.Sigmoid),
                                     reads=[pm[bi_][1]], writes=[bsgm[a]])
                                T.op("dve", lambda e: e.tensor_tensor(out=to[:, c0:c0 + n], in0=pp2[bi_][0][:, 0:n], in1=sgm[a][:, 0:n], op=ALU.mult),
                                     reads=[pp2[bi_][1], bsgm[a]], writes=[bto])
                        T.dma("sp", t2s[i], to[:], reads=[bto], writes=[b_t2s[i]])

            T.barrier()
            _stage("S3b")
            with ExitStack() as s3:
                def sb3(name, shape, dt=F32):
                    return s3.enter_context(nc.sbuf_tensor("s_s_" + name, list(shape), dt))
                mT = sb3("mT", [128, 16, TOK + NSAMP], BF16); b_mT = T.buf("mT")
                agT = sb3("agT2", [128, NH, TOK + NSAMP], BF16); b_agT = T.buf("agT2")
                T.dma("pool", agT[:], ags, reads=[b_ags], writes=[b_agT])
                s3t = ExitStack()

                def sb3t(name, shape, dt=F32):
                    return s3t.enter_context(nc.sbuf_tensor("s_t_" + name, list(shape), dt))
                wsl = [sb3t("wsm%d" % i, [128, 16, 128], BF16) for i in range(4)]; bwsl = T.bufs_n("wsm", 4)
                wrr = [0]

                def load_slab3(src, nkc=16):
                    i = wrr[0] % 4
                    wrr[0] += 1
                    T.dma("pool", wsl[i][:, 0:nkc, :], src, writes=[bwsl[i]])
                    return wsl[i], bwsl[i]

                OWN = [(tb * 512, 512) for tb in range(4)]
                PASSES = [[0, 1, 4], [2, 3]]

                def proj_fm3(slab, bslab, nkc, act_own, b_own, act_s, b_s, s0, sn, which):
                    outs = {bi_: nextbank() for bi_ in which}
                    for kc in range(nkc):
                        for bi_ in which:
                            pb, bpb = outs[bi_]
                            if bi_ < 4:
                                c0, n = OWN[bi_]
                                T.op("pe", lambda e: e.matmul(pb[:, 0:n], slab[:, kc, :], act_own[:, kc, c0:c0 + n],
                                                              start=(kc == 0), stop=(kc == nkc - 1)),
                                     reads=[bslab, b_own], writes=[bpb])
                            else:
                                T.op("pe", lambda e: e.matmul(pb[:, 0:sn], slab[:, kc, :], act_s[:, kc, s0:s0 + sn],
                                                              start=(kc == 0), stop=(kc == nkc - 1)),
                                     reads=[bslab, b_s], writes=[bpb])
                    return outs

                sgm = [sb3t("sgn%d" % i, [128, 512]) for i in range(2)]; bsgm = T.bufs_n("sgn", 2)
                t1 = [sb3t("t1%d" % i, [128, 512]) for i in range(2)]; bt1 = T.bufs_n("t1", 2)
                t2i = [sb3t("t2i%d" % i, [128, TOK + NSAMP], BF16) for i in range(2)]; bt2i = T.bufs_n("t2i", 2)
                for i in range(16):
                    sl_m = load_slab3(wml[i, 0]); sl_a = load_slab3(watt[i], 8)
                    T.dma("pool", t2i[i % 2][:], t2s[i], reads=[b_t2s[i]], writes=[bt2i[i % 2]])
                    BL = OWN + [(TOK, NSAMP)]
                    for ps_ in PASSES:
                        pm = proj_fm3(sl_m[0], sl_m[1], 16, xnT, b_xnT, xnS, b_xnS, 2, NSAMP, ps_)
                        pa = proj_fm3(sl_a[0], sl_a[1], 8, agT, b_agT, agT, b_agT, TOK, NSAMP, ps_)
                        for bi_ in ps_:
                            c0, n = BL[bi_]
                            a = bi_ % 2
                            T.op("act", lambda e: e.activation(out=sgm[a][:, 0:n], in_=pm[bi_][0][:, 0:n], func=AF.Sigmoid),
                                 reads=[pm[bi_][1]], writes=[bsgm[a]])
                            T.op("dve", lambda e: e.tensor_tensor(out=t1[a][:, 0:n], in0=pa[bi_][0][:, 0:n], in1=sgm[a][:, 0:n], op=ALU.mult),
                                 reads=[pa[bi_][1], bsgm[a]], writes=[bt1[a]])
                            T.op("dve", lambda e: e.tensor_tensor(out=mT[:, i, c0:c0 + n], in0=t1[a][:, 0:n], in1=t2i[i % 2][:, c0:c0 + n], op=ALU.add),
                                 reads=[bt1[a], bt2i[i % 2]], writes=[b_mT])
                s3t.close()
                T.barrier()
                _stage("S3a")
                wo = [sb3("wo%d" % i, [128, 16, 512], BF16) for i in range(2)]; bwo = T.bufs_n("wo", 2)
                xs = [sb3("xs%d" % i, [128, 512]) for i in range(2)]; bxs = T.bufs_n("xs", 2)
                yo = [sb3("yo%d" % i, [128, 512]) for i in range(2)]; byo = T.bufs_n("yo", 2)
                b_y = T.buf("y_o")
                n4 = 0
                for cb in range(4):
                    T.dma("pool", wo[cb % 2][:], wout[cb], writes=[bwo[cb % 2]])
                    for tt in range(17):
                        rows = 128 if tt < 16 else NSAMP
                        a = n4 % 2
                        n4 += 1
                        T.dma("pool", xs[a][:rows, :], x[2048 + tt * 128: 2048 + tt * 128 + rows, cb * 512:(cb + 1) * 512], writes=[bxs[a]])
                        pb, bpb = nextbank()
                        for kc in range(16):
                            T.op("pe", lambda e: e.matmul(pb[:rows, :], mT[:, kc, tt * 128: tt * 128 + rows], wo[cb % 2][:, kc, :],
                                                          start=(kc == 0), stop=(kc == 15)), reads=[b_mT, bwo[cb % 2]], writes=[bpb])
                        T.op("dve", lambda e: e.tensor_tensor(out=yo[a][:rows, :], in0=pb[:rows, :], in1=xs[a][:rows, :], op=ALU.add),
                             reads=[bpb, bxs[a]], writes=[byo[a]])
                        T.dma("sp", y_o[tt * 128: tt * 128 + rows, cb * 512:(cb + 1) * 512], yo[a][:rows, :], reads=[byo[a]])

        except _Stop:
            if sE is not None:
                sE.close()
        T.finish("sp")
    return nc


def _slabs(w, cols, nkc=16):
    return np.ascontiguousarray(w[:, cols].reshape(nkc, 128, len(cols)).transpose(1, 0, 2))


def _rope_tables(c0):
    half = 16
    inv = (np.float32(500000.0) ** (-np.arange(half, dtype=np.float32) * np.float32(2.0 / 32))).astype(np.float32)
    tab = np.zeros((128, 70, 48), np.float32)
    i = np.arange(128)
    for g, d in enumerate(DILS):
        nkb = 16 // d + 1
        for r in range(d):
            for kb in range(nkb):
                pos = (c0 - 128 * d + (kb * 128 + i) * d + r).astype(np.float32)
                ang = pos[:, None] * inv[None, :]
                c, s = np.cos(ang).astype(np.float32), np.sin(ang).astype(np.float32)
                t = ROPE_BASE[g] + r * nkb + kb
                tab[:, t, 0:16] = c; tab[:, t, 16:32] = s; tab[:, t, 32:48] = c
    pos = (PAST_LEN + (np.arange(NSAMP) % 4)).astype(np.float32)
    ang = pos[:, None] * inv[None, :]
    tab[:NSAMP, ROPE_SAMPLE, 0:16] = np.cos(ang); tab[:NSAMP, ROPE_SAMPLE, 16:32] = np.sin(ang); tab[:NSAMP, ROPE_SAMPLE, 32:48] = np.cos(ang)
    return tab


def _const_masks(halo_valid):
    k = np.arange(128)[:, None]
    q = np.arange(128)[None, :]
    prev = (k >= q).astype(np.float32)
    cur = (k <= q).astype(np.float32)
    masks = np.zeros((128, 2, 256), np.float32)
    masks[:, 0, :128] = prev; masks[:, 0, 128:] = cur
    masks[:, 1, :128] = prev * halo_valid; masks[:, 1, 128:] = cur
    smask = np.zeros((128, 36, 16), np.float32)
    m = np.arange(128)
    for b in range(4):
        for t in range(4):
            smask[:, b * 9 + 0, b * 4 + t] = (m >= t)
            for gi in range(2):
                smask[:, b * 9 + 1 + gi * 4 + t, b * 4 + t] = 1.0
    nmask = np.zeros((16, 3, 16), np.float32)
    for b in range(4):
        for tk in range(4):
            for tq in range(4):
                nmask[b * 4 + tk, 0, b * 4 + tq] = float(tk <= tq)
                nmask[b * 4 + tk, 1, b * 4 + tq] = float(tk == tq)
                nmask[b * 4 + tk, 2, b * 4 + tq] = float(tk == tq)
    return masks, smask, nmask


_NC_CACHE = {}


def _prepare(x_prompt, x_sample, cache_kv_w128, cache_kv_w512, cache_kv_w2048, state_conv,
             norm_w, w_in, q_norm_w, k_norm_w, conv_w, w_att_proj, w_conv_proj, w_out):
    f = np.float32
    x_prompt = np.asarray(x_prompt, f); x_sample = np.asarray(x_sample, f)
    caches = [np.asarray(c, f)[0] for c in (cache_kv_w128, cache_kv_w512, cache_kv_w2048)]
    state_conv = np.asarray(state_conv, f)[0]
    w_in = np.asarray(w_in, f)[0]; w_att = np.asarray(w_att_proj, f)[0]
    w_cp = np.asarray(w_conv_proj, f)[0]; w_o = np.asarray(w_out, f)[0]
    norm_w = np.asarray(norm_w, f)[0]; qn = np.asarray(q_norm_w, f)[0]; kn = np.asarray(k_norm_w, f)[0]
    conv_w = np.asarray(conv_w, f)[0]

    ar = np.arange(128)
    wqkv = np.empty((NH, 3, 128, 16, 384), f)
    for h in range(NH):
        for g in range(3):
            cols = np.concatenate([g * 3072 + s * 1024 + h * 128 + ar for s in range(3)])
            wqkv[h, g] = _slabs(w_in, cols)
    wag = np.stack([_slabs(w_in, OFF_AGATE + h * 128 + ar) for h in range(NH)])
    wcv = np.empty((16, 4, 128, 16, 128), f)
    for j in range(16):
        wcv[j, 0] = _slabs(w_in, OFF_CONV + j * 128 + ar)
        wcv[j, 1] = _slabs(w_in, OFF_CONV + 4096 + j * 128 + ar)
        wcv[j, 2] = _slabs(w_in, OFF_CONV + 2048 + j * 128 + ar)
        wcv[j, 3] = _slabs(w_in, OFF_CGATE + j * 128 + ar)
    wml = np.empty((16, 2, 128, 16, 128), f)
    for i in range(16):
        wml[i, 0] = _slabs(w_in, OFF_MERGE + i * 128 + ar)
        wml[i, 1] = _slabs(w_in, OFF_MERGE + 2048 + i * 128 + ar)
    watt = np.stack([_slabs(w_att, i * 128 + ar, 8) for i in range(16)])
    wcp = np.stack([_slabs(w_cp, i * 128 + ar) for i in range(16)])
    wout = np.stack([_slabs(w_o, cb * 512 + np.arange(512)) for cb in range(4)])
    nw = np.ascontiguousarray(np.broadcast_to(norm_w[None, :], (128, D_MODEL)))
    qkw = np.ascontiguousarray(np.broadcast_to(np.stack([qn, kn], axis=1)[None], (128, 3, 2, 128)))
    cw = np.ascontiguousarray(conv_w.reshape(3, 16, 128).transpose(2, 1, 0))
    ident = np.eye(128, dtype=f)

    in_maps = []
    for c in range(NCORES):
        b, q = c // 4, c % 4
        c0 = q * TOK
        xe = np.zeros((4096 + NSAMP, D_MODEL), f)
        if q > 0:
            xe[0:2048] = x_prompt[b, c0 - 2048:c0]
        xe[2048:4096] = x_prompt[b, c0:c0 + TOK]
        xe[4096:] = x_sample[4 * c:4 * c + 4].reshape(NSAMP, D_MODEL)
        masks, smask, nmask = _const_masks(1.0 if q > 0 else 0.0)
        sc = state_conv[4 * c:4 * c + 4]
        scT = np.ascontiguousarray(sc.reshape(4, 2, 16, 128).transpose(3, 2, 0, 1))
        m = {"x": xe, "scT": scT, "wqkv": wqkv, "wag": wag, "wcv": wcv, "wml": wml, "watt": watt, "wcp": wcp,
             "wout": wout, "nw": nw, "qkw": qkw, "cw": cw, "rope": _rope_tables(c0), "masks": masks,
             "smask": smask, "nmask": nmask, "ident": ident}
        for g in range(3):
            m["ck%d" % g] = np.ascontiguousarray(caches[g][4 * c:4 * c + 4].reshape(4, CACHE_LEN[g], 2, 1024))
        in_maps.append(m)

    return in_maps


def _assemble(R):
    f = np.float32
    y_p = np.empty((2, SEQ, D_MODEL), f)
    y_s = np.empty((32, 4, D_MODEL), f)
    for c in range(NCORES):
        b, q = c // 4, c % 4
        y_p[b, q * TOK:(q + 1) * TOK] = R[c]["y"][:TOK]
        y_s[4 * c:4 * c + 4] = R[c]["y"][TOK:].reshape(4, 4, D_MODEL)
    kvp = []
    for g in range(3):
        L = CACHE_LEN[g]
        kvp.append(np.stack([R[3]["kv%d" % g], R[7]["kv%d" % g]]).reshape(1, 2, L, 2, NH, HD))
    ncp = np.stack([R[3]["ncv"], R[7]["ncv"]])
    ncp = np.ascontiguousarray(ncp.transpose(0, 3, 2, 1)).reshape(1, 2, 2, D_MODEL)
    kvs = []
    for g in range(3):
        L = CACHE_LEN[g]
        kvs.append(np.concatenate([R[c]["skv%d" % g] for c in range(NCORES)], axis=0).reshape(1, 32, L, 2, NH, HD))
    ncs = np.concatenate([np.ascontiguousarray(R[c]["sncv"].transpose(2, 3, 1, 0)).reshape(4, 2, D_MODEL) for c in range(NCORES)],
                         axis=0).reshape(1, 32, 2, D_MODEL)
    return (y_p, y_s, kvp[0], kvp[1], kvp[2], ncp, kvs[0], kvs[1], kvs[2], ncs)


def kernel(**inputs):
    in_maps = _prepare(**inputs)
    if "nc" not in _NC_CACHE:
        _NC_CACHE["nc"] = build_program()
    res = run_bass_kernel_spmd(_NC_CACHE["nc"], in_maps, core_ids=list(range(NCORES)))
    return _assemble(res.results)
```

```python
import numpy as np
from contextlib import ExitStack
import concourse.bass as bass
import concourse.mybir as mybir
from concourse.bass_utils import run_bass_kernel_spmd

F32 = mybir.dt.float32
BF16 = mybir.dt.bfloat16
AF = mybir.ActivationFunctionType
ALU = mybir.AluOpType

NCORES = 8
D_MODEL = 2048
SEQ = 8192
PAST_LEN = 16384
HD = 128
NH = 8
DILS = (1, 4, 16)
D_ATT = 1024
QKV_COLS = 9216
OFF_AGATE = QKV_COLS
OFF_CONV = OFF_AGATE + D_ATT
OFF_CGATE = OFF_CONV + 3 * 2048
OFF_MERGE = OFF_CGATE + 2048
EPS = 1e-6
SCALE = HD ** -0.5
TOK = 2048
NSAMP = 16
ROPE_BASE = (0, 17, 37)
ROPE_SAMPLE = 69
CACHE_LEN = (128, 512, 2048)


class Buf:
    __slots__ = ("name", "w", "r", "wsem", "wcnt", "rsem", "rcnt", "excl")

    def __init__(self, name):
        self.name = name
        self.w = None
        self.r = {}
        self.wsem = None
        self.wcnt = 0
        self.rsem = None
        self.rcnt = 0
        self.excl = False


class Trk:
    def __init__(self, nc, stack):
        self.nc = nc
        self.stack = stack
        self.eng = {"pe": nc.tensor, "act": nc.scalar, "dve": nc.vector, "pool": nc.gpsimd, "sp": nc.sync}
        self.done = {}
        self.cnt = {}
        self.sems = {}
        for k in ("pe", "act", "dve", "pool"):
            self.done[k] = stack.enter_context(nc.semaphore("done_" + k))
            self.cnt[k] = 0
            self.sems[("done", k)] = self.done[k]
        self.seen = {k: {} for k in self.eng}
        self.nsem = 0
        self.bufs = []
        self.semmax = {}

    def buf(self, name):
        b = Buf(name)
        self.bufs.append(b)
        return b

    def bufs_n(self, name, n):
        return [self.buf("%s%d" % (name, i)) for i in range(n)]

    def _newsem(self, name):
        self.nsem += 1
        h = self.stack.enter_context(self.nc.semaphore("%s_%d" % (name, self.nsem)))
        key = ("dma", self.nsem)
        self.sems[key] = h
        return key

    def _wait(self, e, key, val):
        if self.seen[e].get(key, 0) >= val:
            return
        self.seen[e][key] = val
        self.eng[e].wait_ge(self.sems[key], val)

    def _deps(self, e, reads, writes):
        for b in reads:
            if b.w is not None:
                self._wait(e, b.w[0], b.w[1])
            if b.excl:
                for k, (v, re_) in b.r.items():
                    if re_ != e:
                        self._wait(e, k, v)
        for b in writes:
            if b.w is not None:
                k, v, we = b.w
                if we != e or (_STRICT[0] and e != "pe"):
                    self._wait(e, k, v)
            for k, (v, re) in b.r.items():
                if re != e:
                    self._wait(e, k, v)

    def op(self, e, fn, reads=(), writes=()):
        if _MUTE[0]:
            return None
        self._deps(e, reads, writes)
        ins = fn(self.eng[e])
        self.cnt[e] += 1
        ins.then_inc(self.done[e], 1)
        key = ("done", e)
        ev = self.cnt[e]
        for b in reads:
            b.r[key] = (ev, e)
        for b in writes:
            b.w = (key, ev, e)
            b.r = {}
        return ins

    def dma(self, q, out, in_, reads=(), writes=(), nobarrier=False, free=False):
        if _MUTE[0]:
            return None
        if not free:
            self._deps(q, reads, writes)
        if writes:
            b = writes[0]
            if b.wsem is None:
                b.wsem = self._newsem("w")
            b.wcnt += 1
            key, val = b.wsem, 16 * b.wcnt
        else:
            b = reads[0]
            if b.rsem is None:
                b.rsem = self._newsem("r")
            b.rcnt += 1
            key, val = b.rsem, 16 * b.rcnt
        ins = self.eng[q].dma_start(out=out, in_=in_)
        ins.then_inc(self.sems[key], 16)
        if not nobarrier:
            self.semmax[key] = max(self.semmax.get(key, 0), val)
        tag = "dma"
        for b2 in reads:
            b2.r[key] = (val, tag)
        for b2 in writes:
            b2.w = (key, val, tag)
            b2.r = {}
        return ins

    def barrier(self):
        if _MUTE[0]:
            return
        tg = [(("done", k), self.cnt[k]) for k in self.cnt if self.cnt[k] > 0] + list(self.semmax.items())
        for e in self.eng:
            for key, val in tg:
                if not (key == ("done", e)):
                    self._wait(e, key, val)

    def finish(self, e="sp"):
        for b in self.bufs:
            if b.w is not None:
                self._wait(e, b.w[0], b.w[1])
            for k, (v, _) in b.r.items():
                self._wait(e, k, v)


class _Stop(Exception):
    pass


_STOP = [None]


_MUTE = [False]
_DEBUG = [False]
_CFG = {'pz': 3, 'att_la': 2}
_DBG_NOSAMP = [False]
_DBG_NOSKEW = [False]
_DBG_NOATT = [False]
_STRICT = [False]
_DBG_UNITS = [None]


def _stage(name):
    if _STOP[0] == name:
        _MUTE[0] = True


def bmid(ap, n):
    a = ap.ap
    return bass.AP(ap.tensor, ap.offset, [list(a[0]), [0, n]] + [list(x) for x in a[1:]])


def build_program():
    _MUTE[0] = False
    nc = bass.Bass("TRN2", target_bir_lowering=False)

    def din(name, shape, dt=F32):
        return nc.dram_tensor(name, list(shape), dt, kind="ExternalInput").ap()

    def dout(name, shape, dt=F32):
        return nc.dram_tensor(name, list(shape), dt, kind="ExternalOutput").ap()

    def dscr(name, shape, dt):
        return nc.dram_tensor(name, list(shape), dt).ap()

    x = din("x", [4096 + NSAMP, D_MODEL])
    ck = [din("ck%d" % g, [4, CACHE_LEN[g], 2, 1024]) for g in range(3)]
    scT = din("scT", [128, 16, 4, 2])
    wqkv = din("wqkv", [NH, 3, 128, 16, 384])
    wag = din("wag", [NH, 128, 16, 128])
    wcv = din("wcv", [16, 4, 128, 16, 128])
    wml = din("wml", [16, 2, 128, 16, 128])
    watt = din("watt", [16, 128, 8, 128])
    wcp = din("wcp", [16, 128, 16, 128])
    wout = din("wout", [4, 128, 16, 512])
    nw_d = din("nw", [128, D_MODEL])
    qkw_d = din("qkw", [128, 3, 2, 128])
    cw_d = din("cw", [128, 16, 3])
    rope_d = din("rope", [128, 70, 48])
    masks_d = din("masks", [128, 2, 256])
    smask_d = din("smask", [128, 36, 16])
    nmask_d = din("nmask", [16, 3, 16])
    ident_d = din("ident", [128, 128])

    y_o = dout("y", [TOK + NSAMP, D_MODEL])
    kv_o = [dout("kv%d" % g, [CACHE_LEN[g], 2, 1024]) for g in range(3)]
    ncv_o = dout("ncv", [128, 16, 2])
    skv_o = [dout("skv%d" % g, [4, CACHE_LEN[g], 2, 1024]) for g in range(3)]
    sncv_o = dout("sncv", [128, 16, 4, 2])

    hK = (dout if _DEBUG[0] else dscr)("hK", [3, NH, 128, 2048], BF16)
    hV = (dout if _DEBUG[0] else dscr)("hV", [3, NH, 128, 16, 128], BF16)
    t2s = (dout if _DEBUG[0] else dscr)("t2s", [16, 128, TOK + NSAMP], BF16)

    with ExitStack() as st:
        T = Trk(nc, st)
        sE = None
        try:

            def sb(name, shape, dt=F32):
                return st.enter_context(nc.sbuf_tensor("s_s_" + name, list(shape), dt))

            PS = [st.enter_context(nc.psum_tensor("ps%d" % i, [128, 512], F32)) for i in range(8)]
            PSB = T.bufs_n("ps", 8)
            for b_ in PSB:
                b_.excl = True
            bank_rr = [0]

            def nextbank():
                i = bank_rr[0] % 8
                bank_rr[0] += 1
                return PS[i], PSB[i]

            xnT = sb("xnT", [128, 16, TOK], BF16); b_xnT = T.buf("xnT")
            xnS = sb("xnS", [128, 16, 18], BF16); b_xnS = T.buf("xnS")
            b_agT = T.buf("agT")

            cw = sb("cw", [128, 16, 3]); b_cw = T.buf("cw")
            scTs = sb("scTs", [128, 16, 4, 2]); b_scT = T.buf("scT")
            sE = ExitStack()

            def sbe(name, shape, dt=F32):
                return sE.enter_context(nc.sbuf_tensor("s_e_" + name, list(shape), dt))

            qkw = sbe("qkw", [128, 3, 2, 128]); b_qkw = T.buf("qkw")
            rope = sbe("rope", [128, 70, 48]); b_rope = T.buf("rope")
            masks = sbe("masks", [128, 2, 256], BF16); b_masks = T.buf("masks")
            smask = sbe("smask", [128, 36, 16], BF16); b_smask = T.buf("smask")
            nmask = sbe("nmask", [16, 3, 16], BF16); b_nmask = T.buf("nmask")
            identb = sbe("identb", [128, 128], BF16); b_identb = T.buf("identb")
            identf = sbe("identf", [128, 128], F32); b_identf = T.buf("identf")
            ones = sbe("ones", [128, 128], BF16); b_ones = T.buf("ones")
            T.dma("pool", qkw[:], qkw_d, writes=[b_qkw])
            T.dma("pool", cw[:], cw_d, writes=[b_cw])
            T.dma("pool", rope[:], rope_d, writes=[b_rope])
            T.dma("pool", masks[:], masks_d, writes=[b_masks])
            T.dma("pool", smask[:], smask_d, writes=[b_smask])
            T.dma("pool", nmask[:], nmask_d, writes=[b_nmask])
            T.dma("pool", identb[:], ident_d, writes=[b_identb])
            T.dma("pool", identf[:], ident_d, writes=[b_identf])
            T.dma("pool", scTs[:], scT, writes=[b_scT])
            T.op("dve", lambda e: e.memset(ones[:], 1.0), writes=[b_ones])

            def post_a(W, pz, bpz, rows, g, tile, has_q, kT_dst, q_dst, v_dst, kv_out):
                i = W["i"]; W["i"] += 1
                s = i % WD
                qk, bqk = W["qk"][s], W["bqk"][s]
                junk, bjunk = W["junk"], W["bjunk"]
                ss, bss = W["ss"][s], W["bss"][s]
                tmp, btmp = W["tmp"][s], W["btmp"][s]
                qkb, bqkb = W["qkb"][s], W["bqkb"][s]
                vf, bvf = W["vf"][s], W["bvf"][s]
                o0 = 0 if has_q else -128
                slots = ([0] if has_q else []) + [1]
                vc = o0 + 256
                for sl in slots:
                    c0 = o0 + sl * 128
                    T.op("act", lambda e: e.activation(out=junk[:rows, 2 * s + sl, :], in_=pz[:rows, c0:c0 + 128], func=AF.Square,
                                                       accum_out=ss[:rows, sl:sl + 1]),
                         reads=[bpz], writes=[bjunk, bss])
                if not has_q:
                    T.op("dve", lambda e: e.memset(ss[:rows, 0:1], 1.0), writes=[bss])
                    T.op("dve", lambda e: e.memset(qk[:rows, 0, :], 0.0), writes=[bqk])
                T.op("act", lambda e: e.activation(out=ss[:rows, :], in_=ss[:rows, :], func=AF.Ln, scale=1.0 / HD, bias=W["eps"][:rows, 0:1]),
                     reads=[bss, W["beps"]], writes=[bss])
                T.op("act", lambda e: e.activation(out=ss[:rows, :], in_=ss[:rows, :], func=AF.Exp, scale=-0.5),
                     reads=[bss], writes=[bss])
                if kv_out is not None:
                    T.op("act", lambda e: e.activation(out=vf[:rows, :], in_=pz[:rows, vc:vc + 128], func=AF.Copy),
                         reads=[bpz], writes=[bvf])
                T.op("dve", lambda e: e.tensor_copy(out=v_dst, in_=pz[:rows, vc:vc + 128]), reads=[bpz], writes=[W["bv_dst"]])
                for sl in slots:
                    c0 = o0 + sl * 128
                    T.op("dve", lambda e: e.scalar_tensor_tensor(out=qk[:rows, sl, :], in0=pz[:rows, c0:c0 + 128],
                                                                 scalar=ss[:rows, sl:sl + 1], in1=qkw[:rows, g, sl, :],
                                                                 op0=ALU.mult, op1=ALU.mult),
                         reads=[bpz, bss, b_qkw], writes=[bqk])
                if True:
                    X = qk[:rows, :, 0:32]
                    CS = bmid(rope[:rows, tile, 0:32], 2)
                    SC = bmid(rope[:rows, tile, 16:48], 2)
                    T.op("dve", lambda e: e.tensor_tensor(out=tmp[:rows, 0], in0=X, in1=CS, op=ALU.mult),
                         reads=[bqk, b_rope], writes=[btmp])
                    T.op("dve", lambda e: e.tensor_tensor(out=tmp[:rows, 1], in0=X, in1=SC, op=ALU.mult),
                         reads=[bqk, b_rope], writes=[btmp])
                    T.op("dve", lambda e: e.tensor_tensor(out=qk[:rows, :, 0:16], in0=tmp[:rows, 0, :, 0:16], in1=tmp[:rows, 0, :, 16:32],
                                                          op=ALU.subtract), reads=[btmp], writes=[bqk])
                    T.op("dve", lambda e: e.tensor_tensor(out=qk[:rows, :, 16:32], in0=tmp[:rows, 1, :, 0:16], in1=tmp[:rows, 1, :, 16:32],
                                                          op=ALU.add), reads=[btmp], writes=[bqk])
                    T.op("dve", lambda e: e.tensor_copy(out=qkb[:rows], in_=qk[:rows]), reads=[bqk], writes=[bqkb])
                else:
                    X = qk[:rows, 1, 0:32]
                    T.op("dve", lambda e: e.tensor_tensor(out=tmp[:rows, 0, 1], in0=X, in1=rope[:rows, tile, 0:32], op=ALU.mult),
                         reads=[bqk, b_rope], writes=[btmp])
                    T.op("dve", lambda e: e.tensor_tensor(out=tmp[:rows, 1, 1], in0=X, in1=rope[:rows, tile, 16:48], op=ALU.mult),
                         reads=[bqk, b_rope], writes=[btmp])
                    T.op("dve", lambda e: e.tensor_tensor(out=qk[:rows, 1, 0:16], in0=tmp[:rows, 0, 1, 0:16], in1=tmp[:rows, 0, 1, 16:32],
                                                          op=ALU.subtract), reads=[btmp], writes=[bqk])
                    T.op("dve", lambda e: e.tensor_tensor(out=qk[:rows, 1, 16:32], in0=tmp[:rows, 1, 1, 0:16], in1=tmp[:rows, 1, 1, 16:32],
                                                          op=ALU.add), reads=[btmp], writes=[bqk])
                    T.op("dve", lambda e: e.tensor_copy(out=qkb[:rows, 1], in_=qk[:rows, 1]), reads=[bqk], writes=[bqkb])
                if kv_out is not None:
                    for (kd, vd, p0, p1) in kv_out:
                        T.dma("sp", kd, qk[p0:p1, 1, :], reads=[bqk])
                        T.dma("sp", vd, vf[p0:p1, :], reads=[bvf])
                return dict(rows=rows, slots=slots, has_q=has_q, qkb=qkb, bqkb=bqkb, kT_dst=kT_dst, q_dst=q_dst,
                            bk=W["bk_dst"], bq=W["bq_dst"])

            def post_b(W, c):
                rows = c["rows"]
                ptr, bptr = W["ptr"], W["bptr"]
                for sl in c["slots"]:
                    T.op("pe", lambda e: e.transpose(out=ptr[:, sl * 128: sl * 128 + rows], in_=c["qkb"][:rows, sl, :],
                                                     identity=identb[:rows, :rows]),
                         reads=[c["bqkb"], b_identb], writes=[bptr])
                if c["has_q"]:
                    T.op("act", lambda e: e.activation(out=c["q_dst"], in_=ptr[:, 0:rows], func=AF.Copy), reads=[bptr], writes=[c["bq"]])
                T.op("act", lambda e: e.activation(out=c["kT_dst"], in_=ptr[:, 128:128 + rows], func=AF.Copy), reads=[bptr], writes=[c["bk"]])

            with ExitStack() as s1:
                def sb1(name, shape, dt=F32):
                    return s1.enter_context(nc.sbuf_tensor("s_s_" + name, list(shape), dt))

                WD = 3
                W = {"i": 0}
                W["qk"] = [sbe("qk%d" % i, [128, 2, 128]) for i in range(WD)]; W["bqk"] = T.bufs_n("qk", WD)
                W["junk"] = sbe("junk", [128, 2 * WD, 128], BF16); W["bjunk"] = T.buf("junk")
                W["ss"] = [sbe("ss%d" % i, [128, 2]) for i in range(WD)]; W["bss"] = T.bufs_n("ss", WD)
                W["tmp"] = [sbe("tmp%d" % i, [128, 2, 2, 32]) for i in range(WD)]; W["btmp"] = T.bufs_n("tmp", WD)
                W["qkb"] = [sbe("qkb%d" % i, [128, 2, 128], BF16) for i in range(WD)]; W["bqkb"] = T.bufs_n("qkb", WD)
                W["vf"] = [sbe("vf%d" % i, [128, 128]) for i in range(WD)]; W["bvf"] = T.bufs_n("vf", WD)
                W["eps"] = sbe("epsc", [128, 1]); W["beps"] = T.buf("eps")
                T.op("dve", lambda e: e.memset(W["eps"][:], EPS), writes=[W["beps"]])
                W["bkvout"] = T.buf("kvout")
                ptr_ap = PS[2][:].bitcast(BF16)
                W["ptr"] = ptr_ap
                W["bptr"] = PSB[2]

                with ExitStack() as sA:
                    xnH = s1.enter_context(nc.sbuf_tensor("s_xnH", [128, 16, 2048], BF16)); b_xnH = T.buf("xnH")
                    xt = [sA.enter_context(nc.sbuf_tensor("s_xt%d" % i, [128, D_MODEL], F32)) for i in range(2)]
                    bxt = T.bufs_n("xt", 2)
                    xnb = [sA.enter_context(nc.sbuf_tensor("s_xnb%d" % i, [128, D_MODEL], BF16)) for i in range(2)]
                    bxnb = T.bufs_n("xnb", 2)
                    nw = sA.enter_context(nc.sbuf_tensor("s_nw", [128, D_MODEL], F32)); b_nw = T.buf("nw")
                    T.dma("pool", nw[:], nw_d, writes=[b_nw])
                    junkA = sA.enter_context(nc.sbuf_tensor("s_junkA", [128, D_MODEL], BF16)); bjunkA = T.buf("junkA")
                    ssA = [sA.enter_context(nc.sbuf_tensor("s_ssA%d" % i, [128, 1], F32)) for i in range(2)]
                    bssA = T.bufs_n("ssA", 2)
                    trA = [PS[0][:].bitcast(BF16), PS[1][:].bitcast(BF16)]
                    btrA = [PSB[0], PSB[1]]
                    for ti in range(33):
                        rows = 128 if ti < 32 else NSAMP
                        s = ti % 2
                        T.dma("sp", xt[s][:rows, :], x[ti * 128: ti * 128 + rows, :], writes=[bxt[s]])
                        T.op("act", lambda e: e.activation(out=junkA[:rows, :], in_=xt[s][:rows, :], func=AF.Square,
                                                           accum_out=ssA[s][:rows, 0:1]),
                             reads=[bxt[s]], writes=[bjunkA, bssA[s]])
                        T.op("act", lambda e: e.activation(out=ssA[s][:rows, :], in_=ssA[s][:rows, :], func=AF.Ln,
                                                           scale=1.0 / D_MODEL, bias=W["eps"][:rows, 0:1]),
                             reads=[bssA[s], W["beps"]], writes=[bssA[s]])
                        T.op("act", lambda e: e.activation(out=ssA[s][:rows, :], in_=ssA[s][:rows, :], func=AF.Exp, scale=-0.5),
                             reads=[bssA[s]], writes=[bssA[s]])
                        T.op("dve", lambda e: e.scalar_tensor_tensor(out=xnb[s][:rows, :], in0=xt[s][:rows, :],
                                                                     scalar=ssA[s][:rows, 0:1], in1=nw[:rows, :],
                                                                     op0=ALU.mult, op1=ALU.mult),
                             reads=[bxt[s], bssA[s], b_nw], writes=[bxnb[s]])
                        for half in range(2):
                            tr, btr = trA[half], btrA[half]
                            for k in range(8):
                                kc = half * 8 + k
                                T.op("pe", lambda e: e.transpose(out=tr[:, k * 128: k * 128 + rows],
                                                                 in_=xnb[s][:rows, kc * 128:(kc + 1) * 128],
                                                                 identity=identb[:rows, :rows]),
                                     reads=[bxnb[s], b_identb], writes=[btr])
                            src = tr.rearrange("p (k t) -> p k t", k=8)[:, :, 0:rows]
                            eng = "act" if half == 0 else "dve"
                            if ti < 16:
                                dst, bd = xnH[:, half * 8:(half + 1) * 8, ti * 128: ti * 128 + rows], b_xnH
                            elif ti < 32:
                                dst, bd = xnT[:, half * 8:(half + 1) * 8, (ti - 16) * 128:(ti - 16) * 128 + rows], b_xnT
                            else:
                                dst, bd = xnS[:, half * 8:(half + 1) * 8, 2:18], b_xnS
                            if eng == "act":
                                T.op("act", lambda e: e.activation(out=dst, in_=src, func=AF.Copy), reads=[btr], writes=[bd])
                            else:
                                T.op("dve", lambda e: e.tensor_copy(out=dst, in_=src), reads=[btr], writes=[bd])
                    T.op("dve", lambda e: e.tensor_copy(out=xnS[:, :, 0:2], in_=xnH[:, :, 2046:2048]), reads=[b_xnH], writes=[b_xnS])

                T.barrier()
                _stage("A")
                PZ = [(PS[0], PSB[0]), (PS[1], PSB[1]), (PS[7], PSB[7])][:_CFG['pz']]
                pzc = [0]
                LA = len(PZ) - 1
                _bhk = T.buf("hK"); b_hK = [[_bhk] * NH for g in range(3)]
                _bhv = T.buf("hV"); b_hV = [[_bhv] * NH for g in range(3)]
                with ExitStack() as s0:
                    wkv = [s0.enter_context(nc.sbuf_tensor("s_wkv%d" % i, [128, 16, 256], BF16)) for i in range(2)]
                    bwkv = T.bufs_n("wkv", 2)
                    kst = [s0.enter_context(nc.sbuf_tensor("s_kst%d" % i, [128, 2048], BF16)) for i in range(2)]
                    bkst = T.bufs_n("kst", 2)
                    vst = [s0.enter_context(nc.sbuf_tensor("s_vst%d" % i, [128, 16, 128], BF16)) for i in range(2)]
                    bvst = T.bufs_n("vst", 2)
                    u = 0
                    for h in range(NH if _DBG_UNITS[0] is None else min(NH, (_DBG_UNITS[0] + 3) // 4)):
                        for g in range(3):
                            d = DILS[g]
                            s = u % 2
                            u += 1
                            if u == 1:
                                T.dma("pool", wkv[0][:], wqkv[0, 0, :, :, 128:384], writes=[bwkv[0]])
                            if u < 3 * NH:
                                hn, gn = u // 3, u % 3
                                T.dma("pool", wkv[u % 2][:], wqkv[hn, gn, :, :, 128:384], writes=[bwkv[u % 2]])
                            W["bk_dst"] = bkst[s]; W["bv_dst"] = bvst[s]; W["bq_dst"] = None
                            def s0_proj(r):
                                pz, bpz = PZ[pzc[0] % len(PZ)]
                                pzc[0] += 1
                                start = 2048 - 128 * d + r
                                for kc in range(16):
                                    T.op("pe", lambda e: e.matmul(pz[:, 0:256], xnH[:, kc, start:start + 127 * d + 1:d],
                                                                  wkv[s][:, kc, :], start=(kc == 0), stop=(kc == 15)),
                                         reads=[b_xnH, bwkv[s]], writes=[bpz])
                                return pz, bpz
                            pend = {}
                            ctxs = {}
                            for n in range(d + 2):
                                if n < d:
                                    pend[n] = s0_proj(n)
                                if 1 <= n <= d:
                                    r = n - 1
                                    pz, bpz = pend.pop(r)
                                    ctxs[r] = post_a(W, pz, bpz, 128, g, ROPE_BASE[g] + r * (16 // d + 1), False,
                                                     kst[s][:, r * 128:(r + 1) * 128], None, vst[s][:, r, :], None)
                                if n >= 2:
                                    post_b(W, ctxs.pop(n - 2))
                            T.dma("sp", hK[g, h, :, 0:d * 128], kst[s][:, 0:d * 128], reads=[bkst[s]], writes=[b_hK[g][h]])
                            T.dma("sp", hV[g, h, :, 0:d, :], vst[s][:, 0:d, :], reads=[bvst[s]], writes=[b_hV[g][h]])

                T.barrier()
                _stage("S0")
                s1.close()
                agT = sbe("agT", [128, NH, TOK + NSAMP], BF16)
                wq = [sb1("wq0", [128, 16, 384], BF16)] * 2; bwq = [T.buf("wq")] * 2
                wg = [sb1("wg0", [128, 16, 128], BF16)] * 2; bwg = [T.buf("wg")] * 2
                QT = [sb1("QT0", [128, 2048], BF16)] * 2; bQT = [T.buf("QT")] * 2
                KT = [sb1("KT0", [128, 20 * 128], BF16)] * 2; bKT = [T.buf("KT")] * 2
                VV = [sb1("VV0", [128, 20, 128], BF16)] * 2; bVV = [T.buf("VV")] * 2
                OL = sb1("OL", [128, 2, TOK]); bOL = T.buf("OL")
                EX = [sb1("EX%d" % i, [128, 256], BF16) for i in range(3)]; bEX = T.bufs_n("EX", 3)
                PP = [sb1("PP%d" % i, [128, 256], BF16) for i in range(3)]; bPP = T.bufs_n("PP", 3)
                sgt = [sb1("sgt0", [128, 512])] * 2; bsgt = [T.buf("sgt")] * 2
                ot = [sb1("ot%d" % i, [128, 512]) for i in range(2)]; bot = T.bufs_n("ot", 2)
                QsT = sb1("QsT", [128, 3, NH, NSAMP], BF16); bQsT = T.buf("QsT")
                KsT = sb1("KsT", [128, 3, NH, NSAMP], BF16); bKsT = T.buf("KsT")
                Vs = sb1("Vs", [NSAMP, 3, NH, 128], BF16); bVs = T.buf("Vs")
                sgS = sb1("sgS", [128, NH, NSAMP]); bsgS = T.buf("sgS")
                bKTh = T.buf("KTh"); bVVh = T.buf("VVh")
                s1w = ExitStack()
                wq = [wq[0], s1w.enter_context(nc.sbuf_tensor("s_wq1", [128, 16, 384], BF16))]
                bwq = [bwq[0], T.buf("wq1")]

                units = []
                for h in range(NH):
                    for g in range(3):
                        d = DILS[g]
                        parts = 2 if g == 2 else 1
                        for p in range(parts):
                            rs = list(range(d)) if parts == 1 else list(range(8 * p, 8 * p + 8))
                            units.append((h, g, p, rs))

                def load_unit_w(ui):
                    h, g, p, rs = units[ui]
                    if p == 0:
                        slot = (h * 3 + g) % 2
                        T.dma("pool", wq[slot][:], wqkv[h, g], writes=[bwq[slot]])

                cc_bufs = [T.buf("skvc%d" % g) for g in range(3)]
                cc_chunks = []
                for g in (2, 1, 0):
                    L = CACHE_LEN[g]
                    for b in range(4):
                        r0 = 0
                        while r0 < L - 4:
                            nr_ = min(256, L - 4 - r0)
                            cc_chunks.append((g, b, r0, nr_))
                            r0 += nr_
                cc_pos = [0]

                def emit_cc(k):
                    for _ in range(k):
                        if cc_pos[0] < len(cc_chunks):
                            g_, b_, r0, nr_ = cc_chunks[cc_pos[0]]
                            cc_pos[0] += 1
                            T.dma("act", skv_o[g_][b_, r0:r0 + nr_], ck[g_][b_, 4 + r0:4 + r0 + nr_], writes=[cc_bufs[g_]],
                                  nobarrier=True, free=True)
                load_unit_w(0)
                first_in_head = True
                if _DBG_UNITS[0] is not None:
                    units = units[:_DBG_UNITS[0]]
                for ui, (h, g, p, rs) in enumerate(units):
                    d = DILS[g]
                    nkb = 16 // d + 1
                    us = ui % 2
                    slot = (h * 3 + g) % 2
                    if ui + 1 < len(units):
                        load_unit_w(ui + 1)
                    if g == 0 and p == 0:
                        T.dma("pool", wg[h % 2][:], wag[h], writes=[bwg[h % 2]])
                    W["bk_dst"] = bKT[us]; W["bv_dst"] = bVV[us]; W["bq_dst"] = bQT[us]
                    nr = len(rs)
                    kdst = KT[us][:, 0:nr * nkb * 128].rearrange("p (r k c) -> p r k c", r=nr, k=nkb)[:, :, 0, :]
                    T.dma("pool", kdst, hK[g, h, :, rs[0] * 128:(rs[0] + nr) * 128].rearrange("p (r c) -> p r c", r=nr),
                          reads=[b_hK[g][h]], writes=[bKTh])
                    vdst = VV[us][:, 0:nr * nkb, :].rearrange("p (r k) c -> p r k c", r=nr)[:, :, 0, :]
                    T.dma("pool", vdst, hV[g, h, :, rs[0]:rs[0] + nr, :], reads=[b_hV[g][h]], writes=[bVVh])
                    blist = []
                    for rl, r in enumerate(rs):
                        for kb in range(1, nkb):
                            blist.append((rl, r, kb))
                    if p == 0 and not _DBG_NOSAMP[0]:
                        blist.append(None)

                    def s1_proj(item):
                        pz, bpz = PZ[pzc[0] % len(PZ)]
                        pzc[0] += 1
                        if item is None:
                            for kc in range(16):
                                T.op("pe", lambda e: e.matmul(pz[:NSAMP, 0:384], xnS[:, kc, 2:18], wq[slot][:, kc, :],
                                                              start=(kc == 0), stop=(kc == 15)),
                                     reads=[b_xnS, bwq[slot]], writes=[bpz])
                        else:
                            rl, r, kb = item
                            start = (kb - 1) * 128 * d + r
                            for kc in range(16):
                                T.op("pe", lambda e: e.matmul(pz[:, 0:384], xnT[:, kc, start:start + 127 * d + 1:d],
                                                              wq[slot][:, kc, :], start=(kc == 0), stop=(kc == 15)),
                                     reads=[b_xnT, bwq[slot]], writes=[bpz])
                        return pz, bpz

                    def s1_post(item, pz, bpz):
                        if item is None:
                            L = CACHE_LEN[g]
                            kv_out = []
                            for b in range(4):
                                kv_out.append((skv_o[g][b, L - 4:L, 0, h * 128:(h + 1) * 128],
                                               skv_o[g][b, L - 4:L, 1, h * 128:(h + 1) * 128], 4 * b, 4 * b + 4))
                            W["bk_dst"] = bKsT; W["bv_dst"] = bVs; W["bq_dst"] = bQsT
                            c_ = post_a(W, pz, bpz, NSAMP, g, ROPE_SAMPLE, True, KsT[:, g, h, :], QsT[:, g, h, :], Vs[:, g, h, :], kv_out)
                            W["bk_dst"] = bKT[us]; W["bv_dst"] = bVV[us]; W["bq_dst"] = bQT[us]
                            return c_
                        rl, r, kb = item
                        bi = rl * nkb + kb
                        qi = rl * (nkb - 1) + (kb - 1)
                        kv_out = None
                        lo_tok = (kb - 1) * 128 * d + r
                        keep0 = TOK - CACHE_LEN[g]
                        if lo_tok >= keep0:
                            row0 = lo_tok - keep0
                            kd = kv_o[g][row0:row0 + 127 * d + 1:d, 0, h * 128:(h + 1) * 128]
                            vd = kv_o[g][row0:row0 + 127 * d + 1:d, 1, h * 128:(h + 1) * 128]
                            kv_out = [(kd, vd, 0, 128)]
                        return post_a(W, pz, bpz, 128, g, ROPE_BASE[g] + r * nkb + kb, True,
                                      KT[us][:, bi * 128:(bi + 1) * 128], QT[us][:, qi * 128:(qi + 1) * 128],
                                      VV[us][:, bi, :], kv_out)

                    pend = {}
                    ctxs = {}
                    NBk = len(blist)
                    for n in range(NBk + 2):
                        if n < NBk:
                            pend[n] = s1_proj(blist[n])
                        if 1 <= n <= NBk:
                            pz, bpz = pend.pop(n - 1)
                            ctxs[n - 1] = s1_post(blist[n - 1], pz, bpz)
                            if _DBG_NOSKEW[0]:
                                post_b(W, ctxs.pop(n - 1))
                        if n >= 2 and not _DBG_NOSKEW[0]:
                            post_b(W, ctxs.pop(n - 2))
                    emit_cc(2)
                    ALA = _CFG['att_la']
                    PSs = [(PS[3], PSB[3]), (PS[4], PSB[4]), (PS[0], PSB[0])][:ALA + 1]
                    PSo = [(PS[5], PSB[5]), (PS[6], PSB[6]), (PS[1], PSB[1])][:ALA + 1]
                    qlist = []
                    for rl, r in enumerate(rs):
                        for kq in range(1, nkb):
                            qlist.append((rl, r, kq))

                    def att_scores(n):
                        rl, r, kq = qlist[n]
                        bi = rl * nkb + kq
                        qi = rl * (nkb - 1) + (kq - 1)
                        a = n % (ALA + 1)
                        pS, bpS = PSs[a]
                        T.op("pe", lambda e: e.matmul(pS[:, 0:128], KT[us][:, (bi - 1) * 128: bi * 128], QT[us][:, qi * 128:(qi + 1) * 128],
                                                      start=True, stop=True), reads=[bKT[us], bKTh, bQT[us]], writes=[bpS])
                        T.op("pe", lambda e: e.matmul(pS[:, 128:256], KT[us][:, bi * 128:(bi + 1) * 128], QT[us][:, qi * 128:(qi + 1) * 128],
                                                      start=True, stop=True), reads=[bKT[us], bKTh, bQT[us]], writes=[bpS])
                        T.op("act", lambda e: e.activation(out=EX[a][:], in_=pS[:, 0:256], func=AF.Exp, scale=SCALE),
                             reads=[bpS], writes=[bEX[a]])
                        mk = masks[:, 1, :] if kq == 1 else masks[:, 0, :]
                        T.op("dve", lambda e: e.tensor_tensor(out=PP[a][:], in0=EX[a][:], in1=mk, op=ALU.mult),
                             reads=[bEX[a], b_masks], writes=[bPP[a]])

                    def att_pv(n):
                        rl, r, kq = qlist[n]
                        bi = rl * nkb + kq
                        a = n % (ALA + 1)
                        pO, bpO = PSo[a]
                        T.op("pe", lambda e: e.matmul(pO[:, 0:128], VV[us][:, bi - 1, :], PP[a][:, 0:128], start=True, stop=False),
                             reads=[bVV[us], bVVh, bPP[a]], writes=[bpO])
                        T.op("pe", lambda e: e.matmul(pO[:, 0:128], VV[us][:, bi, :], PP[a][:, 128:256], start=False, stop=True),
                             reads=[bVV[us], bVVh, bPP[a]], writes=[bpO])
                        T.op("pe", lambda e: e.matmul(pO[:, 128:256], ones[:], PP[a][:, 0:128], start=True, stop=False),
                             reads=[b_ones, bPP[a]], writes=[bpO])
                        T.op("pe", lambda e: e.matmul(pO[:, 128:256], ones[:], PP[a][:, 128:256], start=False, stop=True),
                             reads=[b_ones, bPP[a]], writes=[bpO])
                        t0 = (kq - 1) * 128 * d + r
                        dst = OL[:, :, t0:t0 + 127 * d + 1:d]
                        src_ = pO[:, 0:256].rearrange("p (a b) -> p a b", a=2)
                        if first_in_head:
                            T.op("dve", lambda e: e.tensor_copy(out=dst, in_=src_), reads=[bpO], writes=[bOL])
                        else:
                            T.op("dve", lambda e: e.tensor_tensor(out=dst, in0=src_, in1=dst, op=ALU.add), reads=[bpO, bOL], writes=[bOL])

                    NQ = len(qlist)
                    for n in range(0 if _DBG_NOATT[0] else NQ + ALA):
                        if n < NQ:
                            att_scores(n)
                        if n >= ALA:
                            att_pv(n - ALA)
                    if g == 0:
                        first_in_head = False
                    if g == 2 and p == 1:
                        first_in_head = True
                        if _DEBUG[0] and h == 0:
                            dbgOL = dout("dbgOL", [128, 2, TOK])
                            T.dma("sp", dbgOL, OL[:], reads=[bOL])
                        T.op("dve", lambda e: e.reciprocal(out=OL[:, 1, :], in_=OL[:, 1, :]), reads=[bOL], writes=[bOL])
                        gs = h % 2
                        blocks = [(tb * 512, 512, xnT, b_xnT, tb * 512) for tb in range(4)] + [(TOK, NSAMP, xnS, b_xnS, 2)]
                        pbs = [(PS[0], PSB[0]), (PS[1], PSB[1]), (PS[3], PSB[3]), (PS[4], PSB[4]), (PS[7], PSB[7])]
                        for kc in range(16):
                            for bi_, (c0, n, src_t, src_b, sc0) in enumerate(blocks):
                                pb, bpb = pbs[bi_]
                                T.op("pe", lambda e: e.matmul(pb[:, 0:n], wg[gs][:, kc, :], src_t[:, kc, sc0:sc0 + n],
                                                              start=(kc == 0), stop=(kc == 15)),
                                     reads=[bwg[gs], src_b], writes=[bpb])
                        for bi_, (c0, n, src_t, src_b, sc0) in enumerate(blocks):
                            pb, bpb = pbs[bi_]
                            if bi_ < 4:
                                a = bi_ % 2
                                T.op("act", lambda e: e.activation(out=sgt[a][:], in_=pb[:, 0:512], func=AF.Silu), reads=[bpb], writes=[bsgt[a]])
                                if _DEBUG[0] and h == 0 and bi_ == 0:
                                    dbgsg = dout("dbgsg", [128, 512])
                                    T.dma("sp", dbgsg, sgt[a][:], reads=[bsgt[a]])
                                T.op("dve", lambda e: e.tensor_tensor(out=ot[a][:], in0=OL[:, 0, c0:c0 + 512], in1=OL[:, 1, c0:c0 + 512], op=ALU.mult),
                                     reads=[bOL], writes=[bot[a]])
                                T.op("dve", lambda e: e.tensor_tensor(out=agT[:, h, c0:c0 + 512], in0=ot[a][:], in1=sgt[a][:], op=ALU.mult),
                                     reads=[bot[a], bsgt[a]], writes=[b_agT])
                            else:
                                T.op("act", lambda e: e.activation(out=sgS[:, h, :], in_=pb[:, 0:NSAMP], func=AF.Silu), reads=[bpb], writes=[bsgS])

                emit_cc(len(cc_chunks))
                s1w.close()
                T.barrier()
                _stage("S1")
                with ExitStack() as ss_:
                    kt_ = [ss_.enter_context(nc.sbuf_tensor("s_skt%d" % i, [128, 1024], BF16)) for i in range(2)]; bkt_ = T.bufs_n("skt", 2)
                    vt_ = [ss_.enter_context(nc.sbuf_tensor("s_svt%d" % i, [128, 1024], BF16)) for i in range(2)]; bvt_ = T.bufs_n("svt", 2)
                    ktT = [ss_.enter_context(nc.sbuf_tensor("s_sktT%d" % i, [128, 1024], BF16)) for i in range(2)]; bktT = T.bufs_n("sktT", 2)
                    pe_ = [ss_.enter_context(nc.sbuf_tensor("s_spe%d" % i, [128, NH, NSAMP], BF16)) for i in range(2)]; bpe_ = T.bufs_n("spe", 2)
                    pp_ = [ss_.enter_context(nc.sbuf_tensor("s_spp%d" % i, [128, NH, NSAMP], BF16)) for i in range(2)]; bpp_ = T.bufs_n("spp", 2)
                    osb = ss_.enter_context(nc.sbuf_tensor("s_osb", [NSAMP, 1024], F32)); bosb = T.buf("osb")
                    lsb = ss_.enter_context(nc.sbuf_tensor("s_lsb", [NSAMP, NH], F32)); blsb = T.buf("lsb")
                    accO = [PS[5], PS[6]]; baccO = [PSB[5], PSB[6]]
                    accL, baccL = PS[7], PSB[7]
                    trp = PS[2][:].bitcast(BF16); btrp = PSB[2]
                    tiles = []
                    for b in range(4):
                        tiles.append((0, b, 0, 0))
                        for g in (1, 2):
                            for t in range(4):
                                tiles.append((g, b, t, 1 + (g - 1) * 4 + t))
                    for ti, (g, b, t, mi) in enumerate(tiles):
                        d = DILS[g]
                        s = ti % 2
                        T.dma("pool", kt_[s][:], ck[g][b, t:t + 127 * d + 1:d, 0, :], writes=[bkt_[s]])
                        T.dma("pool", vt_[s][:], ck[g][b, t:t + 127 * d + 1:d, 1, :], writes=[bvt_[s]])
                        for h in range(NH):
                            T.op("pe", lambda e: e.transpose(out=trp[:, h * 128:(h + 1) * 128], in_=kt_[s][:, h * 128:(h + 1) * 128],
                                                             identity=identb[:]), reads=[bkt_[s], b_identb], writes=[btrp])
                        T.op("act", lambda e: e.activation(out=ktT[s][:], in_=trp[:, 0:1024], func=AF.Copy), reads=[btrp], writes=[bktT[s]])
                        pS, bpS = PS[3 + s], PSB[3 + s]
                        for h in range(NH):
                            T.op("pe", lambda e: e.matmul(pS[:, h * NSAMP:(h + 1) * NSAMP], ktT[s][:, h * 128:(h + 1) * 128], QsT[:, g, h, :],
                                                          start=True, stop=True), reads=[bktT[s], bQsT], writes=[bpS])
                        T.op("act", lambda e: e.activation(out=pe_[s][:], in_=pS[:, 0:NH * NSAMP].rearrange("p (h t) -> p h t", h=NH),
                                                           func=AF.Exp, scale=SCALE), reads=[bpS], writes=[bpe_[s]])
                        T.op("dve", lambda e: e.tensor_tensor(out=pp_[s][:], in0=pe_[s][:], in1=bmid(smask[:, b * 9 + mi, :], NH), op=ALU.mult),
                             reads=[bpe_[s], b_smask], writes=[bpp_[s]])
                        for h in range(NH):
                            T.op("pe", lambda e: e.matmul(accO[h // 4][:NSAMP, (h % 4) * 128:(h % 4 + 1) * 128], pp_[s][:, h, :],
                                                          vt_[s][:, h * 128:(h + 1) * 128], start=(ti == 0 and h % 4 == 0), stop=False),
                                 reads=[bpp_[s], bvt_[s]], writes=[baccO[h // 4]])
                            T.op("pe", lambda e: e.matmul(accL[:NSAMP, h:h + 1], pp_[s][:, h, :], ones[:, 0:1], start=(ti == 0 and h == 0), stop=False),
                                 reads=[bpp_[s], b_ones], writes=[baccL])
                    ne_ = ss_.enter_context(nc.sbuf_tensor("s_sne", [NSAMP, NH, NSAMP], BF16)); bne_ = T.buf("sne")
                    np_ = ss_.enter_context(nc.sbuf_tensor("s_snp", [NSAMP, NH, NSAMP], BF16)); bnp_ = T.buf("snp")
                    for g in range(3):
                        pS, bpS = PS[3 + g % 2], PSB[3 + g % 2]
                        for h in range(NH):
                            T.op("pe", lambda e: e.matmul(pS[:NSAMP, h * NSAMP:(h + 1) * NSAMP], KsT[:, g, h, :], QsT[:, g, h, :],
                                                          start=True, stop=True), reads=[bKsT, bQsT], writes=[bpS])
                        T.op("act", lambda e: e.activation(out=ne_[:], in_=pS[:NSAMP, 0:NH * NSAMP].rearrange("p (h t) -> p h t", h=NH),
                                                           func=AF.Exp, scale=SCALE), reads=[bpS], writes=[bne_])
                        T.op("dve", lambda e: e.tensor_tensor(out=np_[:], in0=ne_[:], in1=bmid(nmask[:, g, :], NH), op=ALU.mult),
                             reads=[bne_, b_nmask], writes=[bnp_])
                        for h in range(NH):
                            last = (g == 2)
                            T.op("pe", lambda e: e.matmul(accO[h // 4][:NSAMP, (h % 4) * 128:(h % 4 + 1) * 128], np_[:, h, :],
                                                          Vs[:, g, h, :], start=False, stop=last),
                                 reads=[bnp_, bVs], writes=[baccO[h // 4]])
                            T.op("pe", lambda e: e.matmul(accL[:NSAMP, h:h + 1], np_[:, h, :], ones[:NSAMP, 0:1], start=False, stop=last),
                                 reads=[bnp_, b_ones], writes=[baccL])
                    T.op("dve", lambda e: e.reciprocal(out=lsb[:], in_=accL[:NSAMP, 0:NH]), reads=[baccL], writes=[blsb])
                    for hh in range(2):
                        la = lsb[:, hh * 4:(hh + 1) * 4]
                        lb_ = bass.AP(la.tensor, la.offset, [list(la.ap[0]), [1, 4], [0, 128]])
                        T.op("dve", lambda e: e.tensor_tensor(out=osb[:, hh * 512:(hh + 1) * 512].rearrange("p (h c) -> p h c", h=4),
                                                              in0=accO[hh][:NSAMP, :].rearrange("p (h c) -> p h c", h=4),
                                                              in1=lb_, op=ALU.mult),
                             reads=[baccO[hh], blsb], writes=[bosb])
                    pT, bpT = PS[0], PSB[0]
                    for h in range(NH):
                        T.op("pe", lambda e: e.transpose(out=pT[:, h * NSAMP:(h + 1) * NSAMP], in_=osb[:, h * 128:(h + 1) * 128],
                                                         identity=identf[:NSAMP, :NSAMP]), reads=[bosb, b_identf], writes=[bpT])
                    T.op("dve", lambda e: e.tensor_tensor(out=agT[:, :, TOK:TOK + NSAMP],
                                                          in0=pT[:, 0:NH * NSAMP].rearrange("p (h t) -> p h t", h=NH),
                                                          in1=sgS[:], op=ALU.mult), reads=[bpT, bsgS], writes=[b_agT])

                _stage("S1s")
                ags = (dout if _DEBUG[0] else dscr)("ags", [128, NH, TOK + NSAMP], BF16); b_ags = T.buf("ags")
                T.dma("sp", ags, agT[:], reads=[b_agT], writes=[b_ags])
            sE.close()

            T.barrier()
            with ExitStack() as s2:
                def sb2(name, shape, dt=F32):
                    return s2.enter_context(nc.sbuf_tensor("s_s_" + name, list(shape), dt))

                cyT = sb2("cyT", [128, 16, TOK + NSAMP], BF16); b_cyT = T.buf("cyT")
                wsl = [sb2("wsl%d" % i, [128, 16, 128], BF16) for i in range(6)]; bwsl = T.bufs_n("wsl", 6)
                wrr = [0]

                def load_slab(src, nkc=16):
                    i = wrr[0] % 6
                    wrr[0] += 1
                    T.dma("pool", wsl[i][:, 0:nkc, :], src, writes=[bwsl[i]])
                    return wsl[i], bwsl[i]

                OWN = [(tb * 512, 512) for tb in range(4)]

                PASSES = [[0, 1, 4], [2, 3]]

                def proj_fm(slab, bslab, nkc, act_own, b_own, act_s, b_s, s0, sn, which):
                    outs = {bi_: nextbank() for bi_ in which}
                    for kc in range(nkc):
                        for bi_ in which:
                            pb, bpb = outs[bi_]
                            if bi_ < 4:
                                c0, n = OWN[bi_]
                                T.op("pe", lambda e: e.matmul(pb[:, 0:n], slab[:, kc, :], act_own[:, kc, c0:c0 + n],
                                                              start=(kc == 0), stop=(kc == nkc - 1)),
                                     reads=[bslab, b_own], writes=[bpb])
                            else:
                                T.op("pe", lambda e: e.matmul(pb[:, 0:sn], slab[:, kc, :], act_s[:, kc, s0:s0 + sn],
                                                              start=(kc == 0), stop=(kc == nkc - 1)),
                                     reads=[bslab, b_s], writes=[bpb])
                    return outs

                with ExitStack() as sc:
                    def sbc(name, shape, dt=F32):
                        return sc.enter_context(nc.sbuf_tensor("s_s_" + name, list(shape), dt))
                    uext = [sbc("uext%d" % i, [128, TOK + 2]) for i in range(2)]; buext = T.bufs_n("uext", 2)
                    usx = [sbc("usx%d" % i, [128, 4, 6]) for i in range(2)]; busx = T.bufs_n("usx", 2)
                    hS = [sbc("hS%d" % i, [128, 512]) for i in range(4)]; bhS = T.bufs_n("hS", 4)
                    hs_s = sbc("hs_s", [128, 18]); bhs_s = T.buf("hs_s")
                    us_s = sbc("us_s", [128, 18]); bus_s = T.buf("us_s")
                    acc = [sbc("acc%d" % i, [128, 512]) for i in range(2)]; bacc = T.bufs_n("acc", 2)
                    yv = [sbc("yv%d" % i, [128, 512]) for i in range(2)]; byv = T.bufs_n("yv", 2)
                    sg2 = [sbc("sg2%d" % i, [128, 512]) for i in range(2)]; bsg2 = T.bufs_n("sg2", 2)
                    accs = sbc("accs", [128, 4, 4]); baccs = T.buf("accs")
                    ys = sbc("ys", [128, 4, 4]); bys = T.buf("ys")
                    sgs2 = sbc("sgs2", [128, 4, 4]); bsgs2 = T.buf("sgs2")
                    ncvS = sbc("ncvS", [128, 16, 2]); bncvS = T.buf("ncvS")
                    sncvS = sbc("sncvS", [128, 16, 4, 2]); bsncvS = T.buf("sncvS")
                    for j in range(16):
                        ue, bue = uext[j % 2], buext[j % 2]
                        ux, bux = usx[j % 2], busx[j % 2]
                        sl_h = load_slab(wcv[j, 0]); sl_c = load_slab(wcv[j, 1])
                        sl_b = load_slab(wcv[j, 2]); sl_g = load_slab(wcv[j, 3])
                        for ps_ in PASSES:
                            ph = proj_fm(sl_h[0], sl_h[1], 16, xnT, b_xnT, xnS, b_xnS, 0, 18, ps_)
                            pc = proj_fm(sl_c[0], sl_c[1], 16, xnT, b_xnT, xnS, b_xnS, 0, 18, ps_)
                            for bi_ in ps_:
                                pb, bpb = ph[bi_]
                                pb2, bpb2 = pc[bi_]
                                if bi_ < 4:
                                    c0, n = OWN[bi_]
                                    T.op("act", lambda e: e.activation(out=hS[bi_][:], in_=pb[:, 0:512], func=AF.Copy), reads=[bpb], writes=[bhS[bi_]])
                                    T.op("dve", lambda e: e.tensor_tensor(out=ue[:, 2 + c0:2 + c0 + n], in0=pb2[:, 0:n], in1=hS[bi_][:], op=ALU.mult),
                                         reads=[bpb2, bhS[bi_]], writes=[bue])
                                else:
                                    T.op("act", lambda e: e.activation(out=hs_s[:], in_=pb[:, 0:18], func=AF.Copy), reads=[bpb], writes=[bhs_s])
                                    T.op("dve", lambda e: e.tensor_tensor(out=us_s[:], in0=pb2[:, 0:18], in1=hs_s[:], op=ALU.mult),
                                         reads=[bpb2, bhs_s], writes=[bus_s])
                                    T.op("dve", lambda e: e.tensor_copy(out=ue[:, 0:2], in_=us_s[:, 0:2]), reads=[bus_s], writes=[bue])
                                    T.op("dve", lambda e: e.tensor_copy(out=ux[:, :, 2:6], in_=us_s[:, 2:18].rearrange("p (b t) -> p b t", b=4)),
                                         reads=[bus_s], writes=[bux])
                                    T.op("dve", lambda e: e.tensor_copy(out=ux[:, :, 0:2], in_=scTs[:, j, :, :]), reads=[b_scT], writes=[bux])
                        T.op("dve", lambda e: e.tensor_copy(out=ncvS[:, j, :], in_=ue[:, TOK:TOK + 2]), reads=[bue], writes=[bncvS])
                        T.op("dve", lambda e: e.tensor_copy(out=sncvS[:, j, :, :], in_=ux[:, :, 4:6]), reads=[bux], writes=[bsncvS])
                        for ps_ in PASSES:
                            pbb = proj_fm(sl_b[0], sl_b[1], 16, xnT, b_xnT, xnS, b_xnS, 0, 18, ps_)
                            pgg = proj_fm(sl_g[0], sl_g[1], 16, xnT, b_xnT, xnS, b_xnS, 0, 18, ps_)
                            for bi_ in ps_:
                                pb, bpb = pbb[bi_]
                                pg, bpg = pgg[bi_]
                                if bi_ < 4:
                                    c0, n = OWN[bi_]
                                    a = bi_ % 2
                                    T.op("act", lambda e: e.activation(out=acc[a][:], in_=ue[:, 2 + c0:2 + c0 + n], func=AF.Copy, scale=cw[:, j, 2:3]),
                                         reads=[bue, b_cw], writes=[bacc[a]])
                                    T.op("dve", lambda e: e.scalar_tensor_tensor(out=acc[a][:], in0=ue[:, 1 + c0:1 + c0 + n], scalar=cw[:, j, 1:2],
                                                                                 in1=acc[a][:], op0=ALU.mult, op1=ALU.add),
                                         reads=[bue, b_cw, bacc[a]], writes=[bacc[a]])
                                    T.op("dve", lambda e: e.scalar_tensor_tensor(out=acc[a][:], in0=ue[:, c0:c0 + n], scalar=cw[:, j, 0:1],
                                                                                 in1=acc[a][:], op0=ALU.mult, op1=ALU.add),
                                         reads=[bue, b_cw, bacc[a]], writes=[bacc[a]])
                                    T.op("dve", lambda e: e.tensor_tensor(out=yv[a][:], in0=pb[:, 0:n], in1=acc[a][:], op=ALU.mult),
                                         reads=[bpb, bacc[a]], writes=[byv[a]])
                                    T.op("act", lambda e: e.activation(out=sg2[a][:], in_=pg[:, 0:n], func=AF.Silu), reads=[bpg], writes=[bsg2[a]])
                                    T.op("dve", lambda e: e.tensor_tensor(out=cyT[:, j, c0:c0 + n], in0=yv[a][:], in1=sg2[a][:], op=ALU.mult),
                                         reads=[byv[a], bsg2[a]], writes=[b_cyT])
                                else:
                                    T.op("act", lambda e: e.activation(out=accs[:], in_=ux[:, :, 2:6], func=AF.Copy, scale=cw[:, j, 2:3]),
                                         reads=[bux, b_cw], writes=[baccs])
                                    T.op("dve", lambda e: e.scalar_tensor_tensor(out=accs[:], in0=ux[:, :, 1:5], scalar=cw[:, j, 1:2], in1=accs[:],
                                                                                 op0=ALU.mult, op1=ALU.add), reads=[bux, b_cw, baccs], writes=[baccs])
                                    T.op("dve", lambda e: e.scalar_tensor_tensor(out=accs[:], in0=ux[:, :, 0:4], scalar=cw[:, j, 0:1], in1=accs[:],
                                                                                 op0=ALU.mult, op1=ALU.add), reads=[bux, b_cw, baccs], writes=[baccs])
                                    T.op("dve", lambda e: e.tensor_tensor(out=ys[:], in0=pb[:, 2:18].rearrange("p (b t) -> p b t", b=4), in1=accs[:], op=ALU.mult),
                                         reads=[bpb, baccs], writes=[bys])
                                    T.op("act", lambda e: e.activation(out=sgs2[:], in_=pg[:, 2:18].rearrange("p (b t) -> p b t", b=4), func=AF.Silu),
                                         reads=[bpg], writes=[bsgs2])
                                    T.op("dve", lambda e: e.tensor_tensor(out=cyT[:, j, TOK:TOK + NSAMP].rearrange("p (b t) -> p b t", b=4), in0=ys[:], in1=sgs2[:],
                                                                          op=ALU.mult), reads=[bys, bsgs2], writes=[b_cyT])
                    T.dma("sp", ncv_o, ncvS[:], reads=[bncvS])
                    T.dma("sp", sncv_o, sncvS[:], reads=[bsncvS])

                T.barrier()
                _stage("S2")
                _bt2 = T.buf("t2s"); b_t2s = [_bt2] * 16
                with ExitStack() as sc:
                    def sbc(name, shape, dt=F32):
                        return sc.enter_context(nc.sbuf_tensor("s_s_" + name, list(shape), dt))
                    sgm = [sbc("sgm%d" % i, [128, 512]) for i in range(2)]; bsgm = T.bufs_n("sgm", 2)
                    t2o = [sbc("t2o%d" % i, [128, TOK + NSAMP], BF16) for i in range(2)]; bt2o = T.bufs_n("t2o", 2)
                    for i in range(16):
                        sl_m = load_slab(wml[i, 1]); sl_p = load_slab(wcp[i])
                        to, bto = t2o[i % 2], bt2o[i % 2]
                        BL = OWN + [(TOK, NSAMP)]
                        for ps_ in PASSES:
                            pm = proj_fm(sl_m[0], sl_m[1], 16, xnT, b_xnT, xnS, b_xnS, 2, NSAMP, ps_)
                            pp2 = proj_fm(sl_p[0], sl_p[1], 16, cyT, b_cyT, cyT, b_cyT, TOK, NSAMP, ps_)
                            for bi_ in ps_:
                                c0, n = BL[bi_]
                                a = bi_ % 2
                                T.op("act", lambda e: e.activation(out=sgm[a][:, 0:n], in_=pm[bi_][0][:, 0:n], func=AF.Sigmoid),
                                     reads=[pm[bi_][1]], writes=[bsgm[a]])
                                T.op("dve", lambda e: e.tensor_tensor(out=to[:, c0:c0 + n], in0=pp2[bi_][0][:, 0:n], in1=sgm[a][:, 0:n], op=ALU.mult),
                                     reads=[pp2[bi_][1], bsgm[a]], writes=[bto])
                        T.dma("sp", t2s[i], to[:], reads=[bto], writes=[b_t2s[i]])

            T.barrier()
            _stage("S3b")
            with ExitStack() as s3:
                def sb3(name, shape, dt=F32):
                    return s3.enter_context(nc.sbuf_tensor("s_s_" + name, list(shape), dt))
                mT = sb3("mT", [128, 16, TOK + NSAMP], BF16); b_mT = T.buf("mT")
                agT = sb3("agT2", [128, NH, TOK + NSAMP], BF16); b_agT = T.buf("agT2")
                T.dma("pool", agT[:], ags, reads=[b_ags], writes=[b_agT])
                s3t = ExitStack()

                def sb3t(name, shape, dt=F32):
                    return s3t.enter_context(nc.sbuf_tensor("s_t_" + name, list(shape), dt))
                wsl = [sb3t("wsm%d" % i, [128, 16, 128], BF16) for i in range(4)]; bwsl = T.bufs_n("wsm", 4)
                wrr = [0]

                def load_slab3(src, nkc=16):
                    i = wrr[0] % 4
                    wrr[0] += 1
                    T.dma("pool", wsl[i][:, 0:nkc, :], src, writes=[bwsl[i]])
                    return wsl[i], bwsl[i]

                OWN = [(tb * 512, 512) for tb in range(4)]
                PASSES = [[0, 1, 4], [2, 3]]

                def proj_fm3(slab, bslab, nkc, act_own, b_own, act_s, b_s, s0, sn, which):
                    outs = {bi_: nextbank() for bi_ in which}
                    for kc in range(nkc):
                        for bi_ in which:
                            pb, bpb = outs[bi_]
                            if bi_ < 4:
                                c0, n = OWN[bi_]
                                T.op("pe", lambda e: e.matmul(pb[:, 0:n], slab[:, kc, :], act_own[:, kc, c0:c0 + n],
                                                              start=(kc == 0), stop=(kc == nkc - 1)),
                                     reads=[bslab, b_own], writes=[bpb])
                            else:
                                T.op("pe", lambda e: e.matmul(pb[:, 0:sn], slab[:, kc, :], act_s[:, kc, s0:s0 + sn],
                                                              start=(kc == 0), stop=(kc == nkc - 1)),
                                     reads=[bslab, b_s], writes=[bpb])
                    return outs

                sgm = [sb3t("sgn%d" % i, [128, 512]) for i in range(2)]; bsgm = T.bufs_n("sgn", 2)
                t1 = [sb3t("t1%d" % i, [128, 512]) for i in range(2)]; bt1 = T.bufs_n("t1", 2)
                t2i = [sb3t("t2i%d" % i, [128, TOK + NSAMP], BF16) for i in range(2)]; bt2i = T.bufs_n("t2i", 2)
                for i in range(16):
                    sl_m = load_slab3(wml[i, 0]); sl_a = load_slab3(watt[i], 8)
                    T.dma("pool", t2i[i % 2][:], t2s[i], reads=[b_t2s[i]], writes=[bt2i[i % 2]])
                    BL = OWN + [(TOK, NSAMP)]
                    for ps_ in PASSES:
                        pm = proj_fm3(sl_m[0], sl_m[1], 16, xnT, b_xnT, xnS, b_xnS, 2, NSAMP, ps_)
                        pa = proj_fm3(sl_a[0], sl_a[1], 8, agT, b_agT, agT, b_agT, TOK, NSAMP, ps_)
                        for bi_ in ps_:
                            c0, n = BL[bi_]
                            a = bi_ % 2
                            T.op("act", lambda e: e.activation(out=sgm[a][:, 0:n], in_=pm[bi_][0][:, 0:n], func=AF.Sigmoid),
                                 reads=[pm[bi_][1]], writes=[bsgm[a]])
                            T.op("dve", lambda e: e.tensor_tensor(out=t1[a][:, 0:n], in0=pa[bi_][0][:, 0:n], in1=sgm[a][:, 0:n], op=ALU.mult),
                                 reads=[pa[bi_][1], bsgm[a]], writes=[bt1[a]])
                            T.op("dve", lambda e: e.tensor_tensor(out=mT[:, i, c0:c0 + n], in0=t1[a][:, 0:n], in1=t2i[i % 2][:, c0:c0 + n], op=ALU.add),
                                 reads=[bt1[a], bt2i[i % 2]], writes=[b_mT])
                s3t.close()
                T.barrier()
                _stage("S3a")
                wo = [sb3("wo%d" % i, [128, 16, 512], BF16) for i in range(2)]; bwo = T.bufs_n("wo", 2)
                xs = [sb3("xs%d" % i, [128, 512]) for i in range(2)]; bxs = T.bufs_n("xs", 2)
                yo = [sb3("yo%d" % i, [128, 512]) for i in range(2)]; byo = T.bufs_n("yo", 2)
                b_y = T.buf("y_o")
                n4 = 0
                for cb in range(4):
                    T.dma("pool", wo[cb % 2][:], wout[cb], writes=[bwo[cb % 2]])
                    for tt in range(17):
                        rows = 128 if tt < 16 else NSAMP
                        a = n4 % 2
                        n4 += 1
                        T.dma("pool", xs[a][:rows, :], x[2048 + tt * 128: 2048 + tt * 128 + rows, cb * 512:(cb + 1) * 512], writes=[bxs[a]])
                        pb, bpb = nextbank()
                        for kc in range(16):
                            T.op("pe", lambda e: e.matmul(pb[:rows, :], mT[:, kc, tt * 128: tt * 128 + rows], wo[cb % 2][:, kc, :],
                                                          start=(kc == 0), stop=(kc == 15)), reads=[b_mT, bwo[cb % 2]], writes=[bpb])
                        T.op("dve", lambda e: e.tensor_tensor(out=yo[a][:rows, :], in0=pb[:rows, :], in1=xs[a][:rows, :], op=ALU.add),
                             reads=[bpb, bxs[a]], writes=[byo[a]])
                        T.dma("sp", y_o[tt * 128: tt * 128 + rows, cb * 512:(cb + 1) * 512], yo[a][:rows, :], reads=[byo[a]])

        except _Stop:
            if sE is not None:
                sE.close()
        T.finish("sp")
    return nc


def _slabs(w, cols, nkc=16):
    return np.ascontiguousarray(w[:, cols].reshape(nkc, 128, len(cols)).transpose(1, 0, 2))


def _rope_tables(c0):
    half = 16
    inv = (np.float32(500000.0) ** (-np.arange(half, dtype=np.float32) * np.float32(2.0 / 32))).astype(np.float32)
    tab = np.zeros((128, 70, 48), np.float32)
    i = np.arange(128)
    for g, d in enumerate(DILS):
        nkb = 16 // d + 1
        for r in range(d):
            for kb in range(nkb):
                pos = (c0 - 128 * d + (kb * 128 + i) * d + r).astype(np.float32)
                ang = pos[:, None] * inv[None, :]
                c, s = np.cos(ang).astype(np.float32), np.sin(ang).astype(np.float32)
                t = ROPE_BASE[g] + r * nkb + kb
                tab[:, t, 0:16] = c; tab[:, t, 16:32] = s; tab[:, t, 32:48] = c
    pos = (PAST_LEN + (np.arange(NSAMP) % 4)).astype(np.float32)
    ang = pos[:, None] * inv[None, :]
    tab[:NSAMP, ROPE_SAMPLE, 0:16] = np.cos(ang); tab[:NSAMP, ROPE_SAMPLE, 16:32] = np.sin(ang); tab[:NSAMP, ROPE_SAMPLE, 32:48] = np.cos(ang)
    return tab


def _const_masks(halo_valid):
    k = np.arange(128)[:, None]
    q = np.arange(128)[None, :]
    prev = (k >= q).astype(np.float32)
    cur = (k <= q).astype(np.float32)
    masks = np.zeros((128, 2, 256), np.float32)
    masks[:, 0, :128] = prev; masks[:, 0, 128:] = cur
    masks[:, 1, :128] = prev * halo_valid; masks[:, 1, 128:] = cur
    smask = np.zeros((128, 36, 16), np.float32)
    m = np.arange(128)
    for b in range(4):
        for t in range(4):
            smask[:, b * 9 + 0, b * 4 + t] = (m >= t)
            for gi in range(2):
                smask[:, b * 9 + 1 + gi * 4 + t, b * 4 + t] = 1.0
    nmask = np.zeros((16, 3, 16), np.float32)
    for b in range(4):
        for tk in range(4):
            for tq in range(4):
                nmask[b * 4 + tk, 0, b * 4 + tq] = float(tk <= tq)
                nmask[b * 4 + tk, 1, b * 4 + tq] = float(tk == tq)
                nmask[b * 4 + tk, 2, b * 4 + tq] = float(tk == tq)
    return masks, smask, nmask


_NC_CACHE = {}


def _prepare(x_prompt, x_sample, cache_kv_w128, cache_kv_w512, cache_kv_w2048, state_conv,
             norm_w, w_in, q_norm_w, k_norm_w, conv_w, w_att_proj, w_conv_proj, w_out):
    f = np.float32
    x_prompt = np.asarray(x_prompt, f); x_sample = np.asarray(x_sample, f)
    caches = [np.asarray(c, f)[0] for c in (cache_kv_w128, cache_kv_w512, cache_kv_w2048)]
    state_conv = np.asarray(state_conv, f)[0]
    w_in = np.asarray(w_in, f)[0]; w_att = np.asarray(w_att_proj, f)[0]
    w_cp = np.asarray(w_conv_proj, f)[0]; w_o = np.asarray(w_out, f)[0]
    norm_w = np.asarray(norm_w, f)[0]; qn = np.asarray(q_norm_w, f)[0]; kn = np.asarray(k_norm_w, f)[0]
    conv_w = np.asarray(conv_w, f)[0]

    ar = np.arange(128)
    wqkv = np.empty((NH, 3, 128, 16, 384), f)
    for h in range(NH):
        for g in range(3):
            cols = np.concatenate([g * 3072 + s * 1024 + h * 128 + ar for s in range(3)])
            wqkv[h, g] = _slabs(w_in, cols)
    wag = np.stack([_slabs(w_in, OFF_AGATE + h * 128 + ar) for h in range(NH)])
    wcv = np.empty((16, 4, 128, 16, 128), f)
    for j in range(16):
        wcv[j, 0] = _slabs(w_in, OFF_CONV + j * 128 + ar)
        wcv[j, 1] = _slabs(w_in, OFF_CONV + 4096 + j * 128 + ar)
        wcv[j, 2] = _slabs(w_in, OFF_CONV + 2048 + j * 128 + ar)
        wcv[j, 3] = _slabs(w_in, OFF_CGATE + j * 128 + ar)
    wml = np.empty((16, 2, 128, 16, 128), f)
    for i in range(16):
        wml[i, 0] = _slabs(w_in, OFF_MERGE + i * 128 + ar)
        wml[i, 1] = _slabs(w_in, OFF_MERGE + 2048 + i * 128 + ar)
    watt = np.stack([_slabs(w_att, i * 128 + ar, 8) for i in range(16)])
    wcp = np.stack([_slabs(w_cp, i * 128 + ar) for i in range(16)])
    wout = np.stack([_slabs(w_o, cb * 512 + np.arange(512)) for cb in range(4)])
    nw = np.ascontiguousarray(np.broadcast_to(norm_w[None, :], (128, D_MODEL)))
    qkw = np.ascontiguousarray(np.broadcast_to(np.stack([qn, kn], axis=1)[None], (128, 3, 2, 128)))
    cw = np.ascontiguousarray(conv_w.reshape(3, 16, 128).transpose(2, 1, 0))
    ident = np.eye(128, dtype=f)

    in_maps = []
    for c in range(NCORES):
        b, q = c // 4, c % 4
        c0 = q * TOK
        xe = np.zeros((4096 + NSAMP, D_MODEL), f)
        if q > 0:
            xe[0:2048] = x_prompt[b, c0 - 2048:c0]
        xe[2048:4096] = x_prompt[b, c0:c0 + TOK]
        xe[4096:] = x_sample[4 * c:4 * c + 4].reshape(NSAMP, D_MODEL)
        masks, smask, nmask = _const_masks(1.0 if q > 0 else 0.0)
        sc = state_conv[4 * c:4 * c + 4]
        scT = np.ascontiguousarray(sc.reshape(4, 2, 16, 128).transpose(3, 2, 0, 1))
        m = {"x": xe, "scT": scT, "wqkv": wqkv, "wag": wag, "wcv": wcv, "wml": wml, "watt": watt, "wcp": wcp,
             "wout": wout, "nw": nw, "qkw": qkw, "cw": cw, "rope": _rope_tables(c0), "masks": masks,
             "smask": smask, "nmask": nmask, "ident": ident}
        for g in range(3):
            m["ck%d" % g] = np.ascontiguousarray(caches[g][4 * c:4 * c + 4].reshape(4, CACHE_LEN[g], 2, 1024))
        in_maps.append(m)

    return in_maps


def _assemble(R):
    f = np.float32
    y_p = np.empty((2, SEQ, D_MODEL), f)
    y_s = np.empty((32, 4, D_MODEL), f)
    for c in range(NCORES):
        b, q = c // 4, c % 4
        y_p[b, q * TOK:(q + 1) * TOK] = R[c]["y"][:TOK]
        y_s[4 * c:4 * c + 4] = R[c]["y"][TOK:].reshape(4, 4, D_MODEL)
    kvp = []
    for g in range(3):
        L = CACHE_LEN[g]
        kvp.append(np.stack([R[3]["kv%d" % g], R[7]["kv%d" % g]]).reshape(1, 2, L, 2, NH, HD))
    ncp = np.stack([R[3]["ncv"], R[7]["ncv"]])
    ncp = np.ascontiguousarray(ncp.transpose(0, 3, 2, 1)).reshape(1, 2, 2, D_MODEL)
    kvs = []
    for g in range(3):
        L = CACHE_LEN[g]
        kvs.append(np.concatenate([R[c]["skv%d" % g] for c in range(NCORES)], axis=0).reshape(1, 32, L, 2, NH, HD))
    ncs = np.concatenate([np.ascontiguousarray(R[c]["sncv"].transpose(2, 3, 1, 0)).reshape(4, 2, D_MODEL) for c in range(NCORES)],
                         axis=0).reshape(1, 32, 2, D_MODEL)
    return (y_p, y_s, kvp[0], kvp[1], kvp[2], ncp, kvs[0], kvs[1], kvs[2], ncs)


def kernel(**inputs):
    in_maps = _prepare(**inputs)
    if "nc" not in _NC_CACHE:
        _NC_CACHE["nc"] = build_program()
    res = run_bass_kernel_spmd(_NC_CACHE["nc"], in_maps, core_ids=list(range(NCORES)))
    return _assemble(res.results)
```

```python
import numpy as np
from contextlib import ExitStack
import concourse.bass as bass
import concourse.mybir as mybir
from concourse.bass_utils import run_bass_kernel_spmd

F32 = mybir.dt.float32
BF16 = mybir.dt.bfloat16
AF = mybir.ActivationFunctionType
ALU = mybir.AluOpType

NCORES = 8
D_MODEL = 2048
SEQ = 8192
PAST_LEN = 16384
HD = 128
NH = 8
DILS = (1, 4, 16)
D_ATT = 1024
QKV_COLS = 9216
OFF_AGATE = QKV_COLS
OFF_CONV = OFF_AGATE + D_ATT
OFF_CGATE = OFF_CONV + 3 * 2048
OFF_MERGE = OFF_CGATE + 2048
EPS = 1e-6
SCALE = HD ** -0.5
TOK = 2048
NSAMP = 16
ROPE_BASE = (0, 17, 37)
ROPE_SAMPLE = 69
CACHE_LEN = (128, 512, 2048)


class Buf:
    __slots__ = ("name", "w", "r", "wsem", "wcnt", "rsem", "rcnt", "excl")

    def __init__(self, name):
        self.name = name
        self.w = None
        self.r = {}
        self.wsem = None
        self.wcnt = 0
        self.rsem = None
        self.rcnt = 0
        self.excl = False


class Trk:
    def __init__(self, nc, stack):
        self.nc = nc
        self.stack = stack
        self.eng = {"pe": nc.tensor, "act": nc.scalar, "dve": nc.vector, "pool": nc.gpsimd, "sp": nc.sync}
        self.done = {}
        self.cnt = {}
        self.sems = {}
        for k in ("pe", "act", "dve", "pool"):
            self.done[k] = stack.enter_context(nc.semaphore("done_" + k))
            self.cnt[k] = 0
            self.sems[("done", k)] = self.done[k]
        self.seen = {k: {} for k in self.eng}
        self.nsem = 0
        self.bufs = []
        self.semmax = {}

    def buf(self, name):
        b = Buf(name)
        self.bufs.append(b)
        return b

    def bufs_n(self, name, n):
        return [self.buf("%s%d" % (name, i)) for i in range(n)]

    def _newsem(self, name):
        self.nsem += 1
        h = self.stack.enter_context(self.nc.semaphore("%s_%d" % (name, self.nsem)))
        key = ("dma", self.nsem)
        self.sems[key] = h
        return key

    def _wait(self, e, key, val):
        if self.seen[e].get(key, 0) >= val:
            return
        self.seen[e][key] = val
        self.eng[e].wait_ge(self.sems[key], val)

    def _deps(self, e, reads, writes):
        for b in reads:
            if b.w is not None:
                self._wait(e, b.w[0], b.w[1])
            if b.excl:
                for k, (v, re_) in b.r.items():
                    if re_ != e:
                        self._wait(e, k, v)
        for b in writes:
            if b.w is not None:
                k, v, we = b.w
                if we != e or (_STRICT[0] and e != "pe"):
                    self._wait(e, k, v)
            for k, (v, re) in b.r.items():
                if re != e:
                    self._wait(e, k, v)

    def op(self, e, fn, reads=(), writes=()):
        if _MUTE[0]:
            return None
        self._deps(e, reads, writes)
        ins = fn(self.eng[e])
        self.cnt[e] += 1
        ins.then_inc(self.done[e], 1)
        key = ("done", e)
        ev = self.cnt[e]
        for b in reads:
            b.r[key] = (ev, e)
        for b in writes:
            b.w = (key, ev, e)
            b.r = {}
        return ins

    def dma(self, q, out, in_, reads=(), writes=(), nobarrier=False, free=False):
        if _MUTE[0]:
            return None
        if not free:
            self._deps(q, reads, writes)
        if writes:
            b = writes[0]
            if b.wsem is None:
                b.wsem = self._newsem("w")
            b.wcnt += 1
            key, val = b.wsem, 16 * b.wcnt
        else:
            b = reads[0]
            if b.rsem is None:
                b.rsem = self._newsem("r")
            b.rcnt += 1
            key, val = b.rsem, 16 * b.rcnt
        ins = self.eng[q].dma_start(out=out, in_=in_)
        ins.then_inc(self.sems[key], 16)
        if not nobarrier:
            self.semmax[key] = max(self.semmax.get(key, 0), val)
        tag = "dma"
        for b2 in reads:
            b2.r[key] = (val, tag)
        for b2 in writes:
            b2.w = (key, val, tag)
            b2.r = {}
        return ins

    def barrier(self):
        if _MUTE[0]:
            return
        tg = [(("done", k), self.cnt[k]) for k in self.cnt if self.cnt[k] > 0] + list(self.semmax.items())
        for e in self.eng:
            for key, val in tg:
                if not (key == ("done", e)):
                    self._wait(e, key, val)

    def finish(self, e="sp"):
        for b in self.bufs:
            if b.w is not None:
                self._wait(e, b.w[0], b.w[1])
            for k, (v, _) in b.r.items():
                self._wait(e, k, v)


class _Stop(Exception):
    pass


_STOP = [None]


_MUTE = [False]
_DEBUG = [False]
_CFG = {'pz': 3, 'att_la': 2}
_DBG_NOSAMP = [False]
_DBG_NOSKEW = [False]
_DBG_NOATT = [False]
_STRICT = [False]
_DBG_UNITS = [None]


def _stage(name):
    if _STOP[0] == name:
        _MUTE[0] = True


def bmid(ap, n):
    a = ap.ap
    return bass.AP(ap.tensor, ap.offset, [list(a[0]), [0, n]] + [list(x) for x in a[1:]])


def build_program():
    _MUTE[0] = False
    nc = bass.Bass("TRN2", target_bir_lowering=False)

    def din(name, shape, dt=F32):
        return nc.dram_tensor(name, list(shape), dt, kind="ExternalInput").ap()

    def dout(name, shape, dt=F32):
        return nc.dram_tensor(name, list(shape), dt, kind="ExternalOutput").ap()

    def dscr(name, shape, dt):
        return nc.dram_tensor(name, list(shape), dt).ap()

    x = din("x", [4096 + NSAMP, D_MODEL])
    ck = [din("ck%d" % g, [4, CACHE_LEN[g], 2, 1024]) for g in range(3)]
    scT = din("scT", [128, 16, 4, 2])
    wqkv = din("wqkv", [NH, 3, 128, 16, 384])
    wag = din("wag", [NH, 128, 16, 128])
    wcv = din("wcv", [16, 4, 128, 16, 128])
    wml = din("wml", [16, 2, 128, 16, 128])
    watt = din("watt", [16, 128, 8, 128])
    wcp = din("wcp", [16, 128, 16, 128])
    wout = din("wout", [4, 128, 16, 512])
    nw_d = din("nw", [128, D_MODEL])
    qkw_d = din("qkw", [128, 3, 2, 128])
    cw_d = din("cw", [128, 16, 3])
    rope_d = din("rope", [128, 70, 48])
    masks_d = din("masks", [128, 2, 256])
    smask_d = din("smask", [128, 36, 16])
    nmask_d = din("nmask", [16, 3, 16])
    ident_d = din("ident", [128, 128])

    y_o = dout("y", [TOK + NSAMP, D_MODEL])
    kv_o = [dout("kv%d" % g, [CACHE_LEN[g], 2, 1024]) for g in range(3)]
    ncv_o = dout("ncv", [128, 16, 2])
    skv_o = [dout("skv%d" % g, [4, CACHE_LEN[g], 2, 1024]) for g in range(3)]
    sncv_o = dout("sncv", [128, 16, 4, 2])

    hK = (dout if _DEBUG[0] else dscr)("hK", [3, NH, 128, 2048], BF16)
    hV = (dout if _DEBUG[0] else dscr)("hV", [3, NH, 128, 16, 128], BF16)
    t2s = (dout if _DEBUG[0] else dscr)("t2s", [16, 128, TOK + NSAMP], BF16)

    with ExitStack() as st:
        T = Trk(nc, st)
        sE = None
        try:

            def sb(name, shape, dt=F32):
                return st.enter_context(nc.sbuf_tensor("s_s_" + name, list(shape), dt))

            PS = [st.enter_context(nc.psum_tensor("ps%d" % i, [128, 512], F32)) for i in range(8)]
            PSB = T.bufs_n("ps", 8)
            for b_ in PSB:
                b_.excl = True
            bank_rr = [0]

            def nextbank():
                i = bank_rr[0] % 8
                bank_rr[0] += 1
                return PS[i], PSB[i]

            xnT = sb("xnT", [128, 16, TOK], BF16); b_xnT = T.buf("xnT")
            xnS = sb("xnS", [128, 16, 18], BF16); b_xnS = T.buf("xnS")
            b_agT = T.buf("agT")

            cw = sb("cw", [128, 16, 3]); b_cw = T.buf("cw")
            scTs = sb("scTs", [128, 16, 4, 2]); b_scT = T.buf("scT")
            sE = ExitStack()

            def sbe(name, shape, dt=F32):
                return sE.enter_context(nc.sbuf_tensor("s_e_" + name, list(shape), dt))

            qkw = sbe("qkw", [128, 3, 2, 128]); b_qkw = T.buf("qkw")
            rope = sbe("rope", [128, 70, 48]); b_rope = T.buf("rope")
            masks = sbe("masks", [128, 2, 256], BF16); b_masks = T.buf("masks")
            smask = sbe("smask", [128, 36, 16], BF16); b_smask = T.buf("smask")
            nmask = sbe("nmask", [16, 3, 16], BF16); b_nmask = T.buf("nmask")
            identb = sbe("identb", [128, 128], BF16); b_identb = T.buf("identb")
            identf = sbe("identf", [128, 128], F32); b_identf = T.buf("identf")
            ones = sbe("ones", [128, 128], BF16); b_ones = T.buf("ones")
            T.dma("pool", qkw[:], qkw_d, writes=[b_qkw])
            T.dma("pool", cw[:], cw_d, writes=[b_cw])
            T.dma("pool", rope[:], rope_d, writes=[b_rope])
            T.dma("pool", masks[:], masks_d, writes=[b_masks])
            T.dma("pool", smask[:], smask_d, writes=[b_smask])
            T.dma("pool", nmask[:], nmask_d, writes=[b_nmask])
            T.dma("pool", identb[:], ident_d, writes=[b_identb])
            T.dma("pool", identf[:], ident_d, writes=[b_identf])
            T.dma("pool", scTs[:], scT, writes=[b_scT])
            T.op("dve", lambda e: e.memset(ones[:], 1.0), writes=[b_ones])

            def post_a(W, pz, bpz, rows, g, tile, has_q, kT_dst, q_dst, v_dst, kv_out):
                i = W["i"]; W["i"] += 1
                s = i % WD
                qk, bqk = W["qk"][s], W["bqk"][s]
                junk, bjunk = W["junk"], W["bjunk"]
                ss, bss = W["ss"][s], W["bss"][s]
                tmp, btmp = W["tmp"][s], W["btmp"][s]
                qkb, bqkb = W["qkb"][s], W["bqkb"][s]
                vf, bvf = W["vf"][s], W["bvf"][s]
                o0 = 0 if has_q else -128
                slots = ([0] if has_q else []) + [1]
                vc = o0 + 256
                for sl in slots:
                    c0 = o0 + sl * 128
                    T.op("act", lambda e: e.activation(out=junk[:rows, 2 * s + sl, :], in_=pz[:rows, c0:c0 + 128], func=AF.Square,
                                                       accum_out=ss[:rows, sl:sl + 1]),
                         reads=[bpz], writes=[bjunk, bss])
                if not has_q:
                    T.op("dve", lambda e: e.memset(ss[:rows, 0:1], 1.0), writes=[bss])
                    T.op("dve", lambda e: e.memset(qk[:rows, 0, :], 0.0), writes=[bqk])
                T.op("act", lambda e: e.activation(out=ss[:rows, :], in_=ss[:rows, :], func=AF.Ln, scale=1.0 / HD, bias=W["eps"][:rows, 0:1]),
                     reads=[bss, W["beps"]], writes=[bss])
                T.op("act", lambda e: e.activation(out=ss[:rows, :], in_=ss[:rows, :], func=AF.Exp, scale=-0.5),
                     reads=[bss], writes=[bss])
                if kv_out is not None:
                    T.op("act", lambda e: e.activation(out=vf[:rows, :], in_=pz[:rows, vc:vc + 128], func=AF.Copy),
                         reads=[bpz], writes=[bvf])
                T.op("dve", lambda e: e.tensor_copy(out=v_dst, in_=pz[:rows, vc:vc + 128]), reads=[bpz], writes=[W["bv_dst"]])
                for sl in slots:
                    c0 = o0 + sl * 128
                    T.op("dve", lambda e: e.scalar_tensor_tensor(out=qk[:rows, sl, :], in0=pz[:rows, c0:c0 + 128],
                                                                 scalar=ss[:rows, sl:sl + 1], in1=qkw[:rows, g, sl, :],
                                                                 op0=ALU.mult, op1=ALU.mult),
                         reads=[bpz, bss, b_qkw], writes=[bqk])
                if True:
                    X = qk[:rows, :, 0:32]
                    CS = bmid(rope[:rows, tile, 0:32], 2)
                    SC = bmid(rope[:rows, tile, 16:48], 2)
                    T.op("dve", lambda e: e.tensor_tensor(out=tmp[:rows, 0], in0=X, in1=CS, op=ALU.mult),
                         reads=[bqk, b_rope], writes=[btmp])
                    T.op("dve", lambda e: e.tensor_tensor(out=tmp[:rows, 1], in0=X, in1=SC, op=ALU.mult),
                         reads=[bqk, b_rope], writes=[btmp])
                    T.op("dve", lambda e: e.tensor_tensor(out=qk[:rows, :, 0:16], in0=tmp[:rows, 0, :, 0:16], in1=tmp[:rows, 0, :, 16:32],
                                                          op=ALU.subtract), reads=[btmp], writes=[bqk])
                    T.op("dve", lambda e: e.tensor_tensor(out=qk[:rows, :, 16:32], in0=tmp[:rows, 1, :, 0:16], in1=tmp[:rows, 1, :, 16:32],
                                                          op=ALU.add), reads=[btmp], writes=[bqk])
                    T.op("dve", lambda e: e.tensor_copy(out=qkb[:rows], in_=qk[:rows]), reads=[bqk], writes=[bqkb])
                else:
                    X = qk[:rows, 1, 0:32]
                    T.op("dve", lambda e: e.tensor_tensor(out=tmp[:rows, 0, 1], in0=X, in1=rope[:rows, tile, 0:32], op=ALU.mult),
                         reads=[bqk, b_rope], writes=[btmp])
                    T.op("dve", lambda e: e.tensor_tensor(out=tmp[:rows, 1, 1], in0=X, in1=rope[:rows, tile, 16:48], op=ALU.mult),
                         reads=[bqk, b_rope], writes=[btmp])
                    T.op("dve", lambda e: e.tensor_tensor(out=qk[:rows, 1, 0:16], in0=tmp[:rows, 0, 1, 0:16], in1=tmp[:rows, 0, 1, 16:32],
                                                          op=ALU.subtract), reads=[btmp], writes=[bqk])
                    T.op("dve", lambda e: e.tensor_tensor(out=qk[:rows, 1, 16:32], in0=tmp[:rows, 1, 1, 0:16], in1=tmp[:rows, 1, 1, 16:32],
                                                          op=ALU.add), reads=[btmp], writes=[bqk])
                    T.op("dve", lambda e: e.tensor_copy(out=qkb[:rows, 1], in_=qk[:rows, 1]), reads=[bqk], writes=[bqkb])
                if kv_out is not None:
                    for (kd, vd, p0, p1) in kv_out:
                        T.dma("sp", kd, qk[p0:p1, 1, :], reads=[bqk])
                        T.dma("sp", vd, vf[p0:p1, :], reads=[bvf])
                return dict(rows=rows, slots=slots, has_q=has_q, qkb=qkb, bqkb=bqkb, kT_dst=kT_dst, q_dst=q_dst,
                            bk=W["bk_dst"], bq=W["bq_dst"])

            def post_b(W, c):
                rows = c["rows"]
                ptr, bptr = W["ptr"], W["bptr"]
                for sl in c["slots"]:
                    T.op("pe", lambda e: e.transpose(out=ptr[:, sl * 128: sl * 128 + rows], in_=c["qkb"][:rows, sl, :],
                                                     identity=identb[:rows, :rows]),
                         reads=[c["bqkb"], b_identb], writes=[bptr])
                if c["has_q"]:
                    T.op("act", lambda e: e.activation(out=c["q_dst"], in_=ptr[:, 0:rows], func=AF.Copy), reads=[bptr], writes=[c["bq"]])
                T.op("act", lambda e: e.activation(out=c["kT_dst"], in_=ptr[:, 128:128 + rows], func=AF.Copy), reads=[bptr], writes=[c["bk"]])

            with ExitStack() as s1:
                def sb1(name, shape, dt=F32):
                    return s1.enter_context(nc.sbuf_tensor("s_s_" + name, list(shape), dt))

                WD = 3
                W = {"i": 0}
                W["qk"] = [sbe("qk%d" % i, [128, 2, 128]) for i in range(WD)]; W["bqk"] = T.bufs_n("qk", WD)
                W["junk"] = sbe("junk", [128, 2 * WD, 128], BF16); W["bjunk"] = T.buf("junk")
                W["ss"] = [sbe("ss%d" % i, [128, 2]) for i in range(WD)]; W["bss"] = T.bufs_n("ss", WD)
                W["tmp"] = [sbe("tmp%d" % i, [128, 2, 2, 32]) for i in range(WD)]; W["btmp"] = T.bufs_n("tmp", WD)
                W["qkb"] = [sbe("qkb%d" % i, [128, 2, 128], BF16) for i in range(WD)]; W["bqkb"] = T.bufs_n("qkb", WD)
                W["vf"] = [sbe("vf%d" % i, [128, 128]) for i in range(WD)]; W["bvf"] = T.bufs_n("vf", WD)
                W["eps"] = sbe("epsc", [128, 1]); W["beps"] = T.buf("eps")
                T.op("dve", lambda e: e.memset(W["eps"][:], EPS), writes=[W["beps"]])
                W["bkvout"] = T.buf("kvout")
                ptr_ap = PS[2][:].bitcast(BF16)
                W["ptr"] = ptr_ap
                W["bptr"] = PSB[2]

                with ExitStack() as sA:
                    xnH = s1.enter_context(nc.sbuf_tensor("s_xnH", [128, 16, 2048], BF16)); b_xnH = T.buf("xnH")
                    xt = [sA.enter_context(nc.sbuf_tensor("s_xt%d" % i, [128, D_MODEL], F32)) for i in range(2)]
                    bxt = T.bufs_n("xt", 2)
                    xnb = [sA.enter_context(nc.sbuf_tensor("s_xnb%d" % i, [128, D_MODEL], BF16)) for i in range(2)]
                    bxnb = T.bufs_n("xnb", 2)
                    nw = sA.enter_context(nc.sbuf_tensor("s_nw", [128, D_MODEL], F32)); b_nw = T.buf("nw")
                    T.dma("pool", nw[:], nw_d, writes=[b_nw])
                    junkA = sA.enter_context(nc.sbuf_tensor("s_junkA", [128, D_MODEL], BF16)); bjunkA = T.buf("junkA")
                    ssA = [sA.enter_context(nc.sbuf_tensor("s_ssA%d" % i, [128, 1], F32)) for i in range(2)]
                    bssA = T.bufs_n("ssA", 2)
                    trA = [PS[0][:].bitcast(BF16), PS[1][:].bitcast(BF16)]
                    btrA = [PSB[0], PSB[1]]
                    for ti in range(33):
                        rows = 128 if ti < 32 else NSAMP
                        s = ti % 2
                        T.dma("sp", xt[s][:rows, :], x[ti * 128: ti * 128 + rows, :], writes=[bxt[s]])
                        T.op("act", lambda e: e.activation(out=junkA[:rows, :], in_=xt[s][:rows, :], func=AF.Square,
                                                           accum_out=ssA[s][:rows, 0:1]),
                             reads=[bxt[s]], writes=[bjunkA, bssA[s]])
                        T.op("act", lambda e: e.activation(out=ssA[s][:rows, :], in_=ssA[s][:rows, :], func=AF.Ln,
                                                           scale=1.0 / D_MODEL, bias=W["eps"][:rows, 0:1]),
                             reads=[bssA[s], W["beps"]], writes=[bssA[s]])
                        T.op("act", lambda e: e.activation(out=ssA[s][:rows, :], in_=ssA[s][:rows, :], func=AF.Exp, scale=-0.5),
                             reads=[bssA[s]], writes=[bssA[s]])
                        T.op("dve", lambda e: e.scalar_tensor_tensor(out=xnb[s][:rows, :], in0=xt[s][:rows, :],
                                                                     scalar=ssA[s][:rows, 0:1], in1=nw[:rows, :],
                                                                     op0=ALU.mult, op1=ALU.mult),
                             reads=[bxt[s], bssA[s], b_nw], writes=[bxnb[s]])
                        for half in range(2):
                            tr, btr = trA[half], btrA[half]
                            for k in range(8):
                                kc = half * 8 + k
                                T.op("pe", lambda e: e.transpose(out=tr[:, k * 128: k * 128 + rows],
                                                                 in_=xnb[s][:rows, kc * 128:(kc + 1) * 128],
                                                                 identity=identb[:rows, :rows]),
                                     reads=[bxnb[s], b_identb], writes=[btr])
                            src = tr.rearrange("p (k t) -> p k t", k=8)[:, :, 0:rows]
                            eng = "act" if half == 0 else "dve"
                            if ti < 16:
                                dst, bd = xnH[:, half * 8:(half + 1) * 8, ti * 128: ti * 128 + rows], b_xnH
                            elif ti < 32:
                                dst, bd = xnT[:, half * 8:(half + 1) * 8, (ti - 16) * 128:(ti - 16) * 128 + rows], b_xnT
                            else:
                                dst, bd = xnS[:, half * 8:(half + 1) * 8, 2:18], b_xnS
                            if eng == "act":
                                T.op("act", lambda e: e.activation(out=dst, in_=src, func=AF.Copy), reads=[btr], writes=[bd])
                            else:
                                T.op("dve", lambda e: e.tensor_copy(out=dst, in_=src), reads=[btr], writes=[bd])
                    T.op("dve", lambda e: e.tensor_copy(out=xnS[:, :, 0:2], in_=xnH[:, :, 2046:2048]), reads=[b_xnH], writes=[b_xnS])

                T.barrier()
                _stage("A")
                PZ = [(PS[0], PSB[0]), (PS[1], PSB[1]), (PS[7], PSB[7])][:_CFG['pz']]
                pzc = [0]
                LA = len(PZ) - 1
                _bhk = T.buf("hK"); b_hK = [[_bhk] * NH for g in range(3)]
                _bhv = T.buf("hV"); b_hV = [[_bhv] * NH for g in range(3)]
                with ExitStack() as s0:
                    wkv = [s0.enter_context(nc.sbuf_tensor("s_wkv%d" % i, [128, 16, 256], BF16)) for i in range(2)]
                    bwkv = T.bufs_n("wkv", 2)
                    kst = [s0.enter_context(nc.sbuf_tensor("s_kst%d" % i, [128, 2048], BF16)) for i in range(3)]
                    bkst = T.bufs_n("kst", 3)
                    vst = [s0.enter_context(nc.sbuf_tensor("s_vst%d" % i, [128, 16, 128], BF16)) for i in range(3)]
                    bvst = T.bufs_n("vst", 3)
                    nh0 = NH if _DBG_UNITS[0] is None else min(NH, (_DBG_UNITS[0] + 3) // 4)
                    s0_units = [(h, g) for h in range(nh0) for g in range(3)]
                    items = []
                    for u, (h, g) in enumerate(s0_units):
                        for r in range(DILS[g]):
                            items.append((u, h, g, r))

                    def s0_load(u):
                        h, g = s0_units[u]
                        T.dma("pool", wkv[u % 2][:], wqkv[h, g, :, :, 128:384], writes=[bwkv[u % 2]])

                    def s0_proj(it):
                        u, h, g, r = it
                        d = DILS[g]
                        s = u % 2
                        if r == 0 and u + 1 < len(s0_units):
                            s0_load(u + 1)
                        pz, bpz = PZ[pzc[0] % len(PZ)]
                        pzc[0] += 1
                        start = 2048 - 128 * d + r
                        for kc in range(16):
                            T.op("pe", lambda e: e.matmul(pz[:, 0:256], xnH[:, kc, start:start + 127 * d + 1:d],
                                                          wkv[s][:, kc, :], start=(kc == 0), stop=(kc == 15)),
                                 reads=[b_xnH, bwkv[s]], writes=[bpz])
                        return pz, bpz

                    def s0_post_a(it, pz, bpz):
                        u, h, g, r = it
                        d = DILS[g]
                        s = u % 3
                        W["bk_dst"] = bkst[s]; W["bv_dst"] = bvst[s]; W["bq_dst"] = None
                        return post_a(W, pz, bpz, 128, g, ROPE_BASE[g] + r * (16 // d + 1), False,
                                      kst[s][:, r * 128:(r + 1) * 128], None, vst[s][:, r, :], None)

                    def s0_post_b(it, c):
                        u, h, g, r = it
                        d = DILS[g]
                        s = u % 3
                        post_b(W, c)
                        if r == d - 1:
                            T.dma("sp", hK[g, h, :, 0:d * 128], kst[s][:, 0:d * 128], reads=[bkst[s]], writes=[b_hK[g][h]])
                            T.dma("sp", hV[g, h, :, 0:d, :], vst[s][:, 0:d, :], reads=[bvst[s]], writes=[b_hV[g][h]])

                    s0_load(0)
                    pend = {}
                    ctxs = {}
                    NI = len(items)
                    for n in range(NI + 2):
                        if n < NI:
                            pend[n] = s0_proj(items[n])
                        if 1 <= n <= NI:
                            pz, bpz = pend.pop(n - 1)
                            ctxs[n - 1] = s0_post_a(items[n - 1], pz, bpz)
                        if n >= 2:
                            s0_post_b(items[n - 2], ctxs.pop(n - 2))

                T.barrier()
                _stage("S0")
                s1.close()
                agT = sbe("agT", [128, NH, TOK + NSAMP], BF16)
                wq = [sb1("wq0", [128, 16, 384], BF16)] * 2; bwq = [T.buf("wq")] * 2
                wg = [sb1("wg0", [128, 16, 128], BF16)] * 2; bwg = [T.buf("wg")] * 2
                QT = [sb1("QT0", [128, 2048], BF16)] * 2; bQT = [T.buf("QT")] * 2
                KT = [sb1("KT0", [128, 20 * 128], BF16)] * 2; bKT = [T.buf("KT")] * 2
                VV = [sb1("VV0", [128, 20, 128], BF16)] * 2; bVV = [T.buf("VV")] * 2
                OL = sb1("OL", [128, 2, TOK]); bOL = T.buf("OL")
                EX = [sb1("EX%d" % i, [128, 256], BF16) for i in range(3)]; bEX = T.bufs_n("EX", 3)
                PP = [sb1("PP%d" % i, [128, 256], BF16) for i in range(3)]; bPP = T.bufs_n("PP", 3)
                sgt = [sb1("sgt0", [128, 512])] * 2; bsgt = [T.buf("sgt")] * 2
                ot = [sb1("ot%d" % i, [128, 512]) for i in range(2)]; bot = T.bufs_n("ot", 2)
                QsT = sb1("QsT", [128, 3, NH, NSAMP], BF16); bQsT = T.buf("QsT")
                KsT = sb1("KsT", [128, 3, NH, NSAMP], BF16); bKsT = T.buf("KsT")
                Vs = sb1("Vs", [NSAMP, 3, NH, 128], BF16); bVs = T.buf("Vs")
                sgS = sb1("sgS", [128, NH, NSAMP]); bsgS = T.buf("sgS")
                bKTh = T.buf("KTh"); bVVh = T.buf("VVh")
                s1w = ExitStack()
                wq = [wq[0], s1w.enter_context(nc.sbuf_tensor("s_wq1", [128, 16, 384], BF16))]
                bwq = [bwq[0], T.buf("wq1")]

                units = []
                for h in range(NH):
                    for g in range(3):
                        d = DILS[g]
                        parts = 2 if g == 2 else 1
                        for p in range(parts):
                            rs = list(range(d)) if parts == 1 else list(range(8 * p, 8 * p + 8))
                            units.append((h, g, p, rs))

                def load_unit_w(ui):
                    h, g, p, rs = units[ui]
                    if p == 0:
                        slot = (h * 3 + g) % 2
                        T.dma("pool", wq[slot][:], wqkv[h, g], writes=[bwq[slot]])

                cc_bufs = [T.buf("skvc%d" % g) for g in range(3)]
                cc_chunks = []
                for g in (2, 1, 0):
                    L = CACHE_LEN[g]
                    for b in range(4):
                        r0 = 0
                        while r0 < L - 4:
                            nr_ = min(256, L - 4 - r0)
                            cc_chunks.append((g, b, r0, nr_))
                            r0 += nr_
                cc_pos = [0]

                def emit_cc(k):
                    for _ in range(k):
                        if cc_pos[0] < len(cc_chunks):
                            g_, b_, r0, nr_ = cc_chunks[cc_pos[0]]
                            cc_pos[0] += 1
                            T.dma("act", skv_o[g_][b_, r0:r0 + nr_], ck[g_][b_, 4 + r0:4 + r0 + nr_], writes=[cc_bufs[g_]],
                                  nobarrier=True, free=True)
                load_unit_w(0)
                first_in_head = True
                if _DBG_UNITS[0] is not None:
                    units = units[:_DBG_UNITS[0]]
                for ui, (h, g, p, rs) in enumerate(units):
                    d = DILS[g]
                    nkb = 16 // d + 1
                    us = ui % 2
                    slot = (h * 3 + g) % 2
                    if ui + 1 < len(units):
                        load_unit_w(ui + 1)
                    if g == 0 and p == 0:
                        T.dma("pool", wg[h % 2][:], wag[h], writes=[bwg[h % 2]])
                    W["bk_dst"] = bKT[us]; W["bv_dst"] = bVV[us]; W["bq_dst"] = bQT[us]
                    nr = len(rs)
                    kdst = KT[us][:, 0:nr * nkb * 128].rearrange("p (r k c) -> p r k c", r=nr, k=nkb)[:, :, 0, :]
                    T.dma("pool", kdst, hK[g, h, :, rs[0] * 128:(rs[0] + nr) * 128].rearrange("p (r c) -> p r c", r=nr),
                          reads=[b_hK[g][h]], writes=[bKTh])
                    vdst = VV[us][:, 0:nr * nkb, :].rearrange("p (r k) c -> p r k c", r=nr)[:, :, 0, :]
                    T.dma("pool", vdst, hV[g, h, :, rs[0]:rs[0] + nr, :], reads=[b_hV[g][h]], writes=[bVVh])
                    blist = []
                    for rl, r in enumerate(rs):
                        for kb in range(1, nkb):
                            blist.append((rl, r, kb))
                    if p == 0 and not _DBG_NOSAMP[0]:
                        blist.append(None)

                    def s1_proj(item):
                        pz, bpz = PZ[pzc[0] % len(PZ)]
                        pzc[0] += 1
                        if item is None:
                            for kc in range(16):
                                T.op("pe", lambda e: e.matmul(pz[:NSAMP, 0:384], xnS[:, kc, 2:18], wq[slot][:, kc, :],
                                                              start=(kc == 0), stop=(kc == 15)),
                                     reads=[b_xnS, bwq[slot]], writes=[bpz])
                        else:
                            rl, r, kb = item
                            start = (kb - 1) * 128 * d + r
                            for kc in range(16):
                                T.op("pe", lambda e: e.matmul(pz[:, 0:384], xnT[:, kc, start:start + 127 * d + 1:d],
                                                              wq[slot][:, kc, :], start=(kc == 0), stop=(kc == 15)),
                                     reads=[b_xnT, bwq[slot]], writes=[bpz])
                        return pz, bpz

                    def s1_post(item, pz, bpz):
                        if item is None:
                            L = CACHE_LEN[g]
                            kv_out = []
                            for b in range(4):
                                kv_out.append((skv_o[g][b, L - 4:L, 0, h * 128:(h + 1) * 128],
                                               skv_o[g][b, L - 4:L, 1, h * 128:(h + 1) * 128], 4 * b, 4 * b + 4))
                            W["bk_dst"] = bKsT; W["bv_dst"] = bVs; W["bq_dst"] = bQsT
                            c_ = post_a(W, pz, bpz, NSAMP, g, ROPE_SAMPLE, True, KsT[:, g, h, :], QsT[:, g, h, :], Vs[:, g, h, :], kv_out)
                            W["bk_dst"] = bKT[us]; W["bv_dst"] = bVV[us]; W["bq_dst"] = bQT[us]
                            return c_
                        rl, r, kb = item
                        bi = rl * nkb + kb
                        qi = rl * (nkb - 1) + (kb - 1)
                        kv_out = None
                        lo_tok = (kb - 1) * 128 * d + r
                        keep0 = TOK - CACHE_LEN[g]
                        if lo_tok >= keep0:
                            row0 = lo_tok - keep0
                            kd = kv_o[g][row0:row0 + 127 * d + 1:d, 0, h * 128:(h + 1) * 128]
                            vd = kv_o[g][row0:row0 + 127 * d + 1:d, 1, h * 128:(h + 1) * 128]
                            kv_out = [(kd, vd, 0, 128)]
                        return post_a(W, pz, bpz, 128, g, ROPE_BASE[g] + r * nkb + kb, True,
                                      KT[us][:, bi * 128:(bi + 1) * 128], QT[us][:, qi * 128:(qi + 1) * 128],
                                      VV[us][:, bi, :], kv_out)

                    pend = {}
                    ctxs = {}
                    NBk = len(blist)
                    for n in range(NBk + 2):
                        if n < NBk:
                            pend[n] = s1_proj(blist[n])
                        if 1 <= n <= NBk:
                            pz, bpz = pend.pop(n - 1)
                            ctxs[n - 1] = s1_post(blist[n - 1], pz, bpz)
                            if _DBG_NOSKEW[0]:
                                post_b(W, ctxs.pop(n - 1))
                        if n >= 2 and not _DBG_NOSKEW[0]:
                            post_b(W, ctxs.pop(n - 2))
                    emit_cc(2)
                    ALA = _CFG['att_la']
                    PSs = [(PS[3], PSB[3]), (PS[4], PSB[4]), (PS[0], PSB[0])][:ALA + 1]
                    PSo = [(PS[5], PSB[5]), (PS[6], PSB[6]), (PS[1], PSB[1])][:ALA + 1]
                    qlist = []
                    for rl, r in enumerate(rs):
                        for kq in range(1, nkb):
                            qlist.append((rl, r, kq))

                    def att_scores(n):
                        rl, r, kq = qlist[n]
                        bi = rl * nkb + kq
                        qi = rl * (nkb - 1) + (kq - 1)
                        a = n % (ALA + 1)
                        pS, bpS = PSs[a]
                        T.op("pe", lambda e: e.matmul(pS[:, 0:128], KT[us][:, (bi - 1) * 128: bi * 128], QT[us][:, qi * 128:(qi + 1) * 128],
                                                      start=True, stop=True), reads=[bKT[us], bKTh, bQT[us]], writes=[bpS])
                        T.op("pe", lambda e: e.matmul(pS[:, 128:256], KT[us][:, bi * 128:(bi + 1) * 128], QT[us][:, qi * 128:(qi + 1) * 128],
                                                      start=True, stop=True), reads=[bKT[us], bKTh, bQT[us]], writes=[bpS])
                        T.op("act", lambda e: e.activation(out=EX[a][:], in_=pS[:, 0:256], func=AF.Exp, scale=SCALE),
                             reads=[bpS], writes=[bEX[a]])
                        mk = masks[:, 1, :] if kq == 1 else masks[:, 0, :]
                        T.op("dve", lambda e: e.tensor_tensor(out=PP[a][:], in0=EX[a][:], in1=mk, op=ALU.mult),
                             reads=[bEX[a], b_masks], writes=[bPP[a]])

                    def att_pv(n):
                        rl, r, kq = qlist[n]
                        bi = rl * nkb + kq
                        a = n % (ALA + 1)
                        pO, bpO = PSo[a]
                        T.op("pe", lambda e: e.matmul(pO[:, 0:128], VV[us][:, bi - 1, :], PP[a][:, 0:128], start=True, stop=False),
                             reads=[bVV[us], bVVh, bPP[a]], writes=[bpO])
                        T.op("pe", lambda e: e.matmul(pO[:, 0:128], VV[us][:, bi, :], PP[a][:, 128:256], start=False, stop=True),
                             reads=[bVV[us], bVVh, bPP[a]], writes=[bpO])
                        T.op("pe", lambda e: e.matmul(pO[:, 128:256], ones[:], PP[a][:, 0:128], start=True, stop=False),
                             reads=[b_ones, bPP[a]], writes=[bpO])
                        T.op("pe", lambda e: e.matmul(pO[:, 128:256], ones[:], PP[a][:, 128:256], start=False, stop=True),
                             reads=[b_ones, bPP[a]], writes=[bpO])
                        t0 = (kq - 1) * 128 * d + r
                        dst = OL[:, :, t0:t0 + 127 * d + 1:d]
                        src_ = pO[:, 0:256].rearrange("p (a b) -> p a b", a=2)
                        if first_in_head:
                            T.op("dve", lambda e: e.tensor_copy(out=dst, in_=src_), reads=[bpO], writes=[bOL])
                        else:
                            T.op("dve", lambda e: e.tensor_tensor(out=dst, in0=src_, in1=dst, op=ALU.add), reads=[bpO, bOL], writes=[bOL])

                    NQ = len(qlist)
                    for n in range(0 if _DBG_NOATT[0] else NQ + ALA):
                        if n < NQ:
                            att_scores(n)
                        if n >= ALA:
                            att_pv(n - ALA)
                    if g == 0:
                        first_in_head = False
                    if g == 2 and p == 1:
                        first_in_head = True
                        if _DEBUG[0] and h == 0:
                            dbgOL = dout("dbgOL", [128, 2, TOK])
                            T.dma("sp", dbgOL, OL[:], reads=[bOL])
                        T.op("dve", lambda e: e.reciprocal(out=OL[:, 1, :], in_=OL[:, 1, :]), reads=[bOL], writes=[bOL])
                        gs = h % 2
                        blocks = [(tb * 512, 512, xnT, b_xnT, tb * 512) for tb in range(4)] + [(TOK, NSAMP, xnS, b_xnS, 2)]
                        pbs = [(PS[0], PSB[0]), (PS[1], PSB[1]), (PS[3], PSB[3]), (PS[4], PSB[4]), (PS[7], PSB[7])]
                        for kc in range(16):
                            for bi_, (c0, n, src_t, src_b, sc0) in enumerate(blocks):
                                pb, bpb = pbs[bi_]
                                T.op("pe", lambda e: e.matmul(pb[:, 0:n], wg[gs][:, kc, :], src_t[:, kc, sc0:sc0 + n],
                                                              start=(kc == 0), stop=(kc == 15)),
                                     reads=[bwg[gs], src_b], writes=[bpb])
                        for bi_, (c0, n, src_t, src_b, sc0) in enumerate(blocks):
                            pb, bpb = pbs[bi_]
                            if bi_ < 4:
                                a = bi_ % 2
                                T.op("act", lambda e: e.activation(out=sgt[a][:], in_=pb[:, 0:512], func=AF.Silu), reads=[bpb], writes=[bsgt[a]])
                                if _DEBUG[0] and h == 0 and bi_ == 0:
                                    dbgsg = dout("dbgsg", [128, 512])
                                    T.dma("sp", dbgsg, sgt[a][:], reads=[bsgt[a]])
                                T.op("dve", lambda e: e.tensor_tensor(out=ot[a][:], in0=OL[:, 0, c0:c0 + 512], in1=OL[:, 1, c0:c0 + 512], op=ALU.mult),
                                     reads=[bOL], writes=[bot[a]])
                                T.op("dve", lambda e: e.tensor_tensor(out=agT[:, h, c0:c0 + 512], in0=ot[a][:], in1=sgt[a][:], op=ALU.mult),
                                     reads=[bot[a], bsgt[a]], writes=[b_agT])
                            else:
                                T.op("act", lambda e: e.activation(out=sgS[:, h, :], in_=pb[:, 0:NSAMP], func=AF.Silu), reads=[bpb], writes=[bsgS])

                emit_cc(len(cc_chunks))
                s1w.close()
                T.barrier()
                _stage("S1")
                with ExitStack() as ss_:
                    kt_ = [ss_.enter_context(nc.sbuf_tensor("s_skt%d" % i, [128, 1024], BF16)) for i in range(2)]; bkt_ = T.bufs_n("skt", 2)
                    vt_ = [ss_.enter_context(nc.sbuf_tensor("s_svt%d" % i, [128, 1024], BF16)) for i in range(2)]; bvt_ = T.bufs_n("svt", 2)
                    ktT = [ss_.enter_context(nc.sbuf_tensor("s_sktT%d" % i, [128, 1024], BF16)) for i in range(2)]; bktT = T.bufs_n("sktT", 2)
                    pe_ = [ss_.enter_context(nc.sbuf_tensor("s_spe%d" % i, [128, NH, NSAMP], BF16)) for i in range(2)]; bpe_ = T.bufs_n("spe", 2)
                    pp_ = [ss_.enter_context(nc.sbuf_tensor("s_spp%d" % i, [128, NH, NSAMP], BF16)) for i in range(2)]; bpp_ = T.bufs_n("spp", 2)
                    osb = ss_.enter_context(nc.sbuf_tensor("s_osb", [NSAMP, 1024], F32)); bosb = T.buf("osb")
                    lsb = ss_.enter_context(nc.sbuf_tensor("s_lsb", [NSAMP, NH], F32)); blsb = T.buf("lsb")
                    accO = [PS[5], PS[6]]; baccO = [PSB[5], PSB[6]]
                    accL, baccL = PS[7], PSB[7]
                    trp = PS[2][:].bitcast(BF16); btrp = PSB[2]
                    tiles = []
                    for b in range(4):
                        tiles.append((0, b, 0, 0))
                        for g in (1, 2):
                            for t in range(4):
                                tiles.append((g, b, t, 1 + (g - 1) * 4 + t))
                    for ti, (g, b, t, mi) in enumerate(tiles):
                        d = DILS[g]
                        s = ti % 2
                        T.dma("pool", kt_[s][:], ck[g][b, t:t + 127 * d + 1:d, 0, :], writes=[bkt_[s]])
                        T.dma("pool", vt_[s][:], ck[g][b, t:t + 127 * d + 1:d, 1, :], writes=[bvt_[s]])
                        for h in range(NH):
                            T.op("pe", lambda e: e.transpose(out=trp[:, h * 128:(h + 1) * 128], in_=kt_[s][:, h * 128:(h + 1) * 128],
                                                             identity=identb[:]), reads=[bkt_[s], b_identb], writes=[btrp])
                        T.op("act", lambda e: e.activation(out=ktT[s][:], in_=trp[:, 0:1024], func=AF.Copy), reads=[btrp], writes=[bktT[s]])
                        pS, bpS = PS[3 + s], PSB[3 + s]
                        for h in range(NH):
                            T.op("pe", lambda e: e.matmul(pS[:, h * NSAMP:(h + 1) * NSAMP], ktT[s][:, h * 128:(h + 1) * 128], QsT[:, g, h, :],
                                                          start=True, stop=True), reads=[bktT[s], bQsT], writes=[bpS])
                        T.op("act", lambda e: e.activation(out=pe_[s][:], in_=pS[:, 0:NH * NSAMP].rearrange("p (h t) -> p h t", h=NH),
                                                           func=AF.Exp, scale=SCALE), reads=[bpS], writes=[bpe_[s]])
                        T.op("dve", lambda e: e.tensor_tensor(out=pp_[s][:], in0=pe_[s][:], in1=bmid(smask[:, b * 9 + mi, :], NH), op=ALU.mult),
                             reads=[bpe_[s], b_smask], writes=[bpp_[s]])
                        for h in range(NH):
                            T.op("pe", lambda e: e.matmul(accO[h // 4][:NSAMP, (h % 4) * 128:(h % 4 + 1) * 128], pp_[s][:, h, :],
                                                          vt_[s][:, h * 128:(h + 1) * 128], start=(ti == 0 and h % 4 == 0), stop=False),
                                 reads=[bpp_[s], bvt_[s]], writes=[baccO[h // 4]])
                            T.op("pe", lambda e: e.matmul(accL[:NSAMP, h:h + 1], pp_[s][:, h, :], ones[:, 0:1], start=(ti == 0 and h == 0), stop=False),
                                 reads=[bpp_[s], b_ones], writes=[baccL])
                    ne_ = ss_.enter_context(nc.sbuf_tensor("s_sne", [NSAMP, NH, NSAMP], BF16)); bne_ = T.buf("sne")
                    np_ = ss_.enter_context(nc.sbuf_tensor("s_snp", [NSAMP, NH, NSAMP], BF16)); bnp_ = T.buf("snp")
                    for g in range(3):
                        pS, bpS = PS[3 + g % 2], PSB[3 + g % 2]
                        for h in range(NH):
                            T.op("pe", lambda e: e.matmul(pS[:NSAMP, h * NSAMP:(h + 1) * NSAMP], KsT[:, g, h, :], QsT[:, g, h, :],
                                                          start=True, stop=True), reads=[bKsT, bQsT], writes=[bpS])
                        T.op("act", lambda e: e.activation(out=ne_[:], in_=pS[:NSAMP, 0:NH * NSAMP].rearrange("p (h t) -> p h t", h=NH),
                                                           func=AF.Exp, scale=SCALE), reads=[bpS], writes=[bne_])
                        T.op("dve", lambda e: e.tensor_tensor(out=np_[:], in0=ne_[:], in1=bmid(nmask[:, g, :], NH), op=ALU.mult),
                             reads=[bne_, b_nmask], writes=[bnp_])
                        for h in range(NH):
                            last = (g == 2)
                            T.op("pe", lambda e: e.matmul(accO[h // 4][:NSAMP, (h % 4) * 128:(h % 4 + 1) * 128], np_[:, h, :],
                                                          Vs[:, g, h, :], start=False, stop=last),
                                 reads=[bnp_, bVs], writes=[baccO[h // 4]])
                            T.op("pe", lambda e: e.matmul(accL[:NSAMP, h:h + 1], np_[:, h, :], ones[:NSAMP, 0:1], start=False, stop=last),
                                 reads=[bnp_, b_ones], writes=[baccL])
                    T.op("dve", lambda e: e.reciprocal(out=lsb[:], in_=accL[:NSAMP, 0:NH]), reads=[baccL], writes=[blsb])
                    for hh in range(2):
                        la = lsb[:, hh * 4:(hh + 1) * 4]
                        lb_ = bass.AP(la.tensor, la.offset, [list(la.ap[0]), [1, 4], [0, 128]])
                        T.op("dve", lambda e: e.tensor_tensor(out=osb[:, hh * 512:(hh + 1) * 512].rearrange("p (h c) -> p h c", h=4),
                                                              in0=accO[hh][:NSAMP, :].rearrange("p (h c) -> p h c", h=4),
                                                              in1=lb_, op=ALU.mult),
                             reads=[baccO[hh], blsb], writes=[bosb])
                    pT, bpT = PS[0], PSB[0]
                    for h in range(NH):
                        T.op("pe", lambda e: e.transpose(out=pT[:, h * NSAMP:(h + 1) * NSAMP], in_=osb[:, h * 128:(h + 1) * 128],
                                                         identity=identf[:NSAMP, :NSAMP]), reads=[bosb, b_identf], writes=[bpT])
                    T.op("dve", lambda e: e.tensor_tensor(out=agT[:, :, TOK:TOK + NSAMP],
                                                          in0=pT[:, 0:NH * NSAMP].rearrange("p (h t) -> p h t", h=NH),
                                                          in1=sgS[:], op=ALU.mult), reads=[bpT, bsgS], writes=[b_agT])

                _stage("S1s")
                ags = (dout if _DEBUG[0] else dscr)("ags", [128, NH, TOK + NSAMP], BF16); b_ags = T.buf("ags")
                T.dma("sp", ags, agT[:], reads=[b_agT], writes=[b_ags])
            sE.close()

            T.barrier()
            with ExitStack() as s2:
                def sb2(name, shape, dt=F32):
                    return s2.enter_context(nc.sbuf_tensor("s_s_" + name, list(shape), dt))

                cyT = sb2("cyT", [128, 16, TOK + NSAMP], BF16); b_cyT = T.buf("cyT")
                wsl = [sb2("wsl%d" % i, [128, 16, 128], BF16) for i in range(6)]; bwsl = T.bufs_n("wsl", 6)
                wrr = [0]

                def load_slab(src, nkc=16):
                    i = wrr[0] % 6
                    wrr[0] += 1
                    T.dma("pool", wsl[i][:, 0:nkc, :], src, writes=[bwsl[i]])
                    return wsl[i], bwsl[i]

                OWN = [(tb * 512, 512) for tb in range(4)]

                PASSES = [[0, 1, 4], [2, 3]]

                def proj_fm(slab, bslab, nkc, act_own, b_own, act_s, b_s, s0, sn, which):
                    outs = {bi_: nextbank() for bi_ in which}
                    for kc in range(nkc):
                        for bi_ in which:
                            pb, bpb = outs[bi_]
                            if bi_ < 4:
                                c0, n = OWN[bi_]
                                T.op("pe", lambda e: e.matmul(pb[:, 0:n], slab[:, kc, :], act_own[:, kc, c0:c0 + n],
                                                              start=(kc == 0), stop=(kc == nkc - 1)),
                                     reads=[bslab, b_own], writes=[bpb])
                            else:
                                T.op("pe", lambda e: e.matmul(pb[:, 0:sn], slab[:, kc, :], act_s[:, kc, s0:s0 + sn],
                                                              start=(kc == 0), stop=(kc == nkc - 1)),
                                     reads=[bslab, b_s], writes=[bpb])
                    return outs

                with ExitStack() as sc:
                    def sbc(name, shape, dt=F32):
                        return sc.enter_context(nc.sbuf_tensor("s_s_" + name, list(shape), dt))
                    uext = [sbc("uext%d" % i, [128, TOK + 2]) for i in range(2)]; buext = T.bufs_n("uext", 2)
                    usx = [sbc("usx%d" % i, [128, 4, 6]) for i in range(2)]; busx = T.bufs_n("usx", 2)
                    hS = [sbc("hS%d" % i, [128, 512]) for i in range(4)]; bhS = T.bufs_n("hS", 4)
                    hs_s = sbc("hs_s", [128, 18]); bhs_s = T.buf("hs_s")
                    us_s = sbc("us_s", [128, 18]); bus_s = T.buf("us_s")
                    acc = [sbc("acc%d" % i, [128, 512]) for i in range(2)]; bacc = T.bufs_n("acc", 2)
                    yv = [sbc("yv%d" % i, [128, 512]) for i in range(2)]; byv = T.bufs_n("yv", 2)
                    sg2 = [sbc("sg2%d" % i, [128, 512]) for i in range(2)]; bsg2 = T.bufs_n("sg2", 2)
                    accs = sbc("accs", [128, 4, 4]); baccs = T.buf("accs")
                    ys = sbc("ys", [128, 4, 4]); bys = T.buf("ys")
                    sgs2 = sbc("sgs2", [128, 4, 4]); bsgs2 = T.buf("sgs2")
                    ncvS = sbc("ncvS", [128, 16, 2]); bncvS = T.buf("ncvS")
                    sncvS = sbc("sncvS", [128, 16, 4, 2]); bsncvS = T.buf("sncvS")
                    for j in range(16):
                        ue, bue = uext[j % 2], buext[j % 2]
                        ux, bux = usx[j % 2], busx[j % 2]
                        sl_h = load_slab(wcv[j, 0]); sl_c = load_slab(wcv[j, 1])
                        sl_b = load_slab(wcv[j, 2]); sl_g = load_slab(wcv[j, 3])
                        for ps_ in PASSES:
                            ph = proj_fm(sl_h[0], sl_h[1], 16, xnT, b_xnT, xnS, b_xnS, 0, 18, ps_)
                            pc = proj_fm(sl_c[0], sl_c[1], 16, xnT, b_xnT, xnS, b_xnS, 0, 18, ps_)
                            for bi_ in ps_:
                                pb, bpb = ph[bi_]
                                pb2, bpb2 = pc[bi_]
                                if bi_ < 4:
                                    c0, n = OWN[bi_]
                                    T.op("act", lambda e: e.activation(out=hS[bi_][:], in_=pb[:, 0:512], func=AF.Copy), reads=[bpb], writes=[bhS[bi_]])
                                    T.op("dve", lambda e: e.tensor_tensor(out=ue[:, 2 + c0:2 + c0 + n], in0=pb2[:, 0:n], in1=hS[bi_][:], op=ALU.mult),
                                         reads=[bpb2, bhS[bi_]], writes=[bue])
                                else:
                                    T.op("act", lambda e: e.activation(out=hs_s[:], in_=pb[:, 0:18], func=AF.Copy), reads=[bpb], writes=[bhs_s])
                                    T.op("dve", lambda e: e.tensor_tensor(out=us_s[:], in0=pb2[:, 0:18], in1=hs_s[:], op=ALU.mult),
                                         reads=[bpb2, bhs_s], writes=[bus_s])
                                    T.op("dve", lambda e: e.tensor_copy(out=ue[:, 0:2], in_=us_s[:, 0:2]), reads=[bus_s], writes=[bue])
                                    T.op("dve", lambda e: e.tensor_copy(out=ux[:, :, 2:6], in_=us_s[:, 2:18].rearrange("p (b t) -> p b t", b=4)),
                                         reads=[bus_s], writes=[bux])
                                    T.op("dve", lambda e: e.tensor_copy(out=ux[:, :, 0:2], in_=scTs[:, j, :, :]), reads=[b_scT], writes=[bux])
                        T.op("dve", lambda e: e.tensor_copy(out=ncvS[:, j, :], in_=ue[:, TOK:TOK + 2]), reads=[bue], writes=[bncvS])
                        T.op("dve", lambda e: e.tensor_copy(out=sncvS[:, j, :, :], in_=ux[:, :, 4:6]), reads=[bux], writes=[bsncvS])
                        for ps_ in PASSES:
                            pbb = proj_fm(sl_b[0], sl_b[1], 16, xnT, b_xnT, xnS, b_xnS, 0, 18, ps_)
                            pgg = proj_fm(sl_g[0], sl_g[1], 16, xnT, b_xnT, xnS, b_xnS, 0, 18, ps_)
                            for bi_ in ps_:
                                pb, bpb = pbb[bi_]
                                pg, bpg = pgg[bi_]
                                if bi_ < 4:
                                    c0, n = OWN[bi_]
                                    a = bi_ % 2
                                    T.op("act", lambda e: e.activation(out=acc[a][:], in_=ue[:, 2 + c0:2 + c0 + n], func=AF.Copy, scale=cw[:, j, 2:3]),
                                         reads=[bue, b_cw], writes=[bacc[a]])
                                    T.op("dve", lambda e: e.scalar_tensor_tensor(out=acc[a][:], in0=ue[:, 1 + c0:1 + c0 + n], scalar=cw[:, j, 1:2],
                                                                                 in1=acc[a][:], op0=ALU.mult, op1=ALU.add),
                                         reads=[bue, b_cw, bacc[a]], writes=[bacc[a]])
                                    T.op("dve", lambda e: e.scalar_tensor_tensor(out=acc[a][:], in0=ue[:, c0:c0 + n], scalar=cw[:, j, 0:1],
                                                                                 in1=acc[a][:], op0=ALU.mult, op1=ALU.add),
                                         reads=[bue, b_cw, bacc[a]], writes=[bacc[a]])
                                    T.op("dve", lambda e: e.tensor_tensor(out=yv[a][:], in0=pb[:, 0:n], in1=acc[a][:], op=ALU.mult),
                                         reads=[bpb, bacc[a]], writes=[byv[a]])
                                    T.op("act", lambda e: e.activation(out=sg2[a][:], in_=pg[:, 0:n], func=AF.Silu), reads=[bpg], writes=[bsg2[a]])
                                    T.op("dve", lambda e: e.tensor_tensor(out=cyT[:, j, c0:c0 + n], in0=yv[a][:], in1=sg2[a][:], op=ALU.mult),
                                         reads=[byv[a], bsg2[a]], writes=[b_cyT])
                                else:
                                    T.op("act", lambda e: e.activation(out=accs[:], in_=ux[:, :, 2:6], func=AF.Copy, scale=cw[:, j, 2:3]),
                                         reads=[bux, b_cw], writes=[baccs])
                                    T.op("dve", lambda e: e.scalar_tensor_tensor(out=accs[:], in0=ux[:, :, 1:5], scalar=cw[:, j, 1:2], in1=accs[:],
                                                                                 op0=ALU.mult, op1=ALU.add), reads=[bux, b_cw, baccs], writes=[baccs])
                                    T.op("dve", lambda e: e.scalar_tensor_tensor(out=accs[:], in0=ux[:, :, 0:4], scalar=cw[:, j, 0:1], in1=accs[:],
                                                                                 op0=ALU.mult, op1=ALU.add), reads=[bux, b_cw, baccs], writes=[baccs])
                                    T.op("dve", lambda e: e.tensor_tensor(out=ys[:], in0=pb[:, 2:18].rearrange("p (b t) -> p b t", b=4), in1=accs[:], op=ALU.mult),
                                         reads=[bpb, baccs], writes=[bys])
                                    T.op("act", lambda e: e.activation(out=sgs2[:], in_=pg[:, 2:18].rearrange("p (b t) -> p b t", b=4), func=AF.Silu),
                                         reads=[bpg], writes=[bsgs2])
                                    T.op("dve", lambda e: e.tensor_tensor(out=cyT[:, j, TOK:TOK + NSAMP].rearrange("p (b t) -> p b t", b=4), in0=ys[:], in1=sgs2[:],
                                                                          op=ALU.mult), reads=[bys, bsgs2], writes=[b_cyT])
                    T.dma("sp", ncv_o, ncvS[:], reads=[bncvS])
                    T.dma("sp", sncv_o, sncvS[:], reads=[bsncvS])

                T.barrier()
                _stage("S2")
                _bt2 = T.buf("t2s"); b_t2s = [_bt2] * 16
                with ExitStack() as sc:
                    def sbc(name, shape, dt=F32):
                        return sc.enter_context(nc.sbuf_tensor("s_s_" + name, list(shape), dt))
                    sgm = [sbc("sgm%d" % i, [128, 512]) for i in range(2)]; bsgm = T.bufs_n("sgm", 2)
                    t2o = [sbc("t2o%d" % i, [128, TOK + NSAMP], BF16) for i in range(2)]; bt2o = T.bufs_n("t2o", 2)
                    for i in range(16):
                        sl_m = load_slab(wml[i, 1]); sl_p = load_slab(wcp[i])
                        to, bto = t2o[i % 2], bt2o[i % 2]
                        BL = OWN + [(TOK, NSAMP)]
                        for ps_ in PASSES:
                            pm = proj_fm(sl_m[0], sl_m[1], 16, xnT, b_xnT, xnS, b_xnS, 2, NSAMP, ps_)
                            pp2 = proj_fm(sl_p[0], sl_p[1], 16, cyT, b_cyT, cyT, b_cyT, TOK, NSAMP, ps_)
                            for bi_ in ps_:
                                c0, n = BL[bi_]
                                a = bi_ % 2
                                T.op("act", lambda e: e.activation(out=sgm[a][:, 0:n], in_=pm[bi_][0][:, 0:n], func=AF.Sigmoid),
                                     reads=[pm[bi_][1]], writes=[bsgm[a]])
                                T.op("dve", lambda e: e.tensor_tensor(out=to[:, c0:c0 + n], in0=pp2[bi_][0][:, 0:n], in1=sgm[a][:, 0:n], op=ALU.mult),
                                     reads=[pp2[bi_][1], bsgm[a]], writes=[bto])
                        T.dma("sp", t2s[i], to[:], reads=[bto], writes=[b_t2s[i]])

            T.barrier()
            _stage("S3b")
            with ExitStack() as s3:
                def sb3(name, shape, dt=F32):
                    return s3.enter_context(nc.sbuf_tensor("s_s_" + name, list(shape), dt))
                mT = sb3("mT", [128, 16, TOK + NSAMP], BF16); b_mT = T.buf("mT")
                agT = sb3("agT2", [128, NH, TOK + NSAMP], BF16); b_agT = T.buf("agT2")
                T.dma("pool", agT[:], ags, reads=[b_ags], writes=[b_agT])
                s3t = ExitStack()

                def sb3t(name, shape, dt=F32):
                    return s3t.enter_context(nc.sbuf_tensor("s_t_" + name, list(shape), dt))
                wsl = [sb3t("wsm%d" % i, [128, 16, 128], BF16) for i in range(4)]; bwsl = T.bufs_n("wsm", 4)
                wrr = [0]

                def load_slab3(src, nkc=16):
                    i = wrr[0] % 4
                    wrr[0] += 1
                    T.dma("pool", wsl[i][:, 0:nkc, :], src, writes=[bwsl[i]])
                    return wsl[i], bwsl[i]

                OWN = [(tb * 512, 512) for tb in range(4)]
                PASSES = [[0, 1, 4], [2, 3]]

                def proj_fm3(slab, bslab, nkc, act_own, b_own, act_s, b_s, s0, sn, which):
                    outs = {bi_: nextbank() for bi_ in which}
                    for kc in range(nkc):
                        for bi_ in which:
                            pb, bpb = outs[bi_]
                            if bi_ < 4:
                                c0, n = OWN[bi_]
                                T.op("pe", lambda e: e.matmul(pb[:, 0:n], slab[:, kc, :], act_own[:, kc, c0:c0 + n],
                                                              start=(kc == 0), stop=(kc == nkc - 1)),
                                     reads=[bslab, b_own], writes=[bpb])
                            else:
                                T.op("pe", lambda e: e.matmul(pb[:, 0:sn], slab[:, kc, :], act_s[:, kc, s0:s0 + sn],
                                                              start=(kc == 0), stop=(kc == nkc - 1)),
                                     reads=[bslab, b_s], writes=[bpb])
                    return outs

                sgm = [sb3t("sgn%d" % i, [128, 512]) for i in range(2)]; bsgm = T.bufs_n("sgn", 2)
                t1 = [sb3t("t1%d" % i, [128, 512]) for i in range(2)]; bt1 = T.bufs_n("t1", 2)
                t2i = [sb3t("t2i%d" % i, [128, TOK + NSAMP], BF16) for i in range(2)]; bt2i = T.bufs_n("t2i", 2)
                for i in range(16):
                    sl_m = load_slab3(wml[i, 0]); sl_a = load_slab3(watt[i], 8)
                    T.dma("pool", t2i[i % 2][:], t2s[i], reads=[b_t2s[i]], writes=[bt2i[i % 2]])
                    BL = OWN + [(TOK, NSAMP)]
                    for ps_ in PASSES:
                        pm = proj_fm3(sl_m[0], sl_m[1], 16, xnT, b_xnT, xnS, b_xnS, 2, NSAMP, ps_)
                        pa = proj_fm3(sl_a[0], sl_a[1], 8, agT, b_agT, agT, b_agT, TOK, NSAMP, ps_)
                        for bi_ in ps_:
                            c0, n = BL[bi_]
                            a = bi_ % 2
                            T.op("act", lambda e: e.activation(out=sgm[a][:, 0:n], in_=pm[bi_][0][:, 0:n], func=AF.Sigmoid),
                                 reads=[pm[bi_][1]], writes=[bsgm[a]])
                            T.op("dve", lambda e: e.tensor_tensor(out=t1[a][:, 0:n], in0=pa[bi_][0][:, 0:n], in1=sgm[a][:, 0:n], op=ALU.mult),
                                 reads=[pa[bi_][1], bsgm[a]], writes=[bt1[a]])
                            T.op("dve", lambda e: e.tensor_tensor(out=mT[:, i, c0:c0 + n], in0=t1[a][:, 0:n], in1=t2i[i % 2][:, c0:c0 + n], op=ALU.add),
                                 reads=[bt1[a], bt2i[i % 2]], writes=[b_mT])
                s3t.close()
                T.barrier()
                _stage("S3a")
                wo = [sb3("wo%d" % i, [128, 16, 512], BF16) for i in range(2)]; bwo = T.bufs_n("wo", 2)
                xs = [sb3("xs%d" % i, [128, 512]) for i in range(2)]; bxs = T.bufs_n("xs", 2)
                yo = [sb3("yo%d" % i, [128, 512]) for i in range(2)]; byo = T.bufs_n("yo", 2)
                b_y = T.buf("y_o")
                n4 = 0
                for cb in range(4):
                    T.dma("pool", wo[cb % 2][:], wout[cb], writes=[bwo[cb % 2]])
                    for tt in range(17):
                        rows = 128 if tt < 16 else NSAMP
                        a = n4 % 2
                        n4 += 1
                        T.dma("pool", xs[a][:rows, :], x[2048 + tt * 128: 2048 + tt * 128 + rows, cb * 512:(cb + 1) * 512], writes=[bxs[a]])
                        pb, bpb = nextbank()
                        for kc in range(16):
                            T.op("pe", lambda e: e.matmul(pb[:rows, :], mT[:, kc, tt * 128: tt * 128 + rows], wo[cb % 2][:, kc, :],
                                                          start=(kc == 0), stop=(kc == 15)), reads=[b_mT, bwo[cb % 2]], writes=[bpb])
                        T.op("dve", lambda e: e.tensor_tensor(out=yo[a][:rows, :], in0=pb[:rows, :], in1=xs[a][:rows, :], op=ALU.add),
                             reads=[bpb, bxs[a]], writes=[byo[a]])
                        T.dma("sp", y_o[tt * 128: tt * 128 + rows, cb * 512:(cb + 1) * 512], yo[a][:rows, :], reads=[byo[a]])

        except _Stop:
            if sE is not None:
                sE.close()
        T.finish("sp")
    return nc


def _slabs(w, cols, nkc=16):
    return np.ascontiguousarray(w[:, cols].reshape(nkc, 128, len(cols)).transpose(1, 0, 2))


def _rope_tables(c0):
    half = 16
    inv = (np.float32(500000.0) ** (-np.arange(half, dtype=np.float32) * np.float32(2.0 / 32))).astype(np.float32)
    tab = np.zeros((128, 70, 48), np.float32)
    i = np.arange(128)
    for g, d in enumerate(DILS):
        nkb = 16 // d + 1
        for r in range(d):
            for kb in range(nkb):
                pos = (c0 - 128 * d + (kb * 128 + i) * d + r).astype(np.float32)
                ang = pos[:, None] * inv[None, :]
                c, s = np.cos(ang).astype(np.float32), np.sin(ang).astype(np.float32)
                t = ROPE_BASE[g] + r * nkb + kb
                tab[:, t, 0:16] = c; tab[:, t, 16:32] = s; tab[:, t, 32:48] = c
    pos = (PAST_LEN + (np.arange(NSAMP) % 4)).astype(np.float32)
    ang = pos[:, None] * inv[None, :]
    tab[:NSAMP, ROPE_SAMPLE, 0:16] = np.cos(ang); tab[:NSAMP, ROPE_SAMPLE, 16:32] = np.sin(ang); tab[:NSAMP, ROPE_SAMPLE, 32:48] = np.cos(ang)
    return tab


def _const_masks(halo_valid):
    k = np.arange(128)[:, None]
    q = np.arange(128)[None, :]
    prev = (k >= q).astype(np.float32)
    cur = (k <= q).astype(np.float32)
    masks = np.zeros((128, 2, 256), np.float32)
    masks[:, 0, :128] = prev; masks[:, 0, 128:] = cur
    masks[:, 1, :128] = prev * halo_valid; masks[:, 1, 128:] = cur
    smask = np.zeros((128, 36, 16), np.float32)
    m = np.arange(128)
    for b in range(4):
        for t in range(4):
            smask[:, b * 9 + 0, b * 4 + t] = (m >= t)
            for gi in range(2):
                smask[:, b * 9 + 1 + gi * 4 + t, b * 4 + t] = 1.0
    nmask = np.zeros((16, 3, 16), np.float32)
    for b in range(4):
        for tk in range(4):
            for tq in range(4):
                nmask[b * 4 + tk, 0, b * 4 + tq] = float(tk <= tq)
                nmask[b * 4 + tk, 1, b * 4 + tq] = float(tk == tq)
                nmask[b * 4 + tk, 2, b * 4 + tq] = float(tk == tq)
    return masks, smask, nmask


_NC_CACHE = {}


def _prepare(x_prompt, x_sample, cache_kv_w128, cache_kv_w512, cache_kv_w2048, state_conv,
             norm_w, w_in, q_norm_w, k_norm_w, conv_w, w_att_proj, w_conv_proj, w_out):
    f = np.float32
    x_prompt = np.asarray(x_prompt, f); x_sample = np.asarray(x_sample, f)
    caches = [np.asarray(c, f)[0] for c in (cache_kv_w128, cache_kv_w512, cache_kv_w2048)]
    state_conv = np.asarray(state_conv, f)[0]
    w_in = np.asarray(w_in, f)[0]; w_att = np.asarray(w_att_proj, f)[0]
    w_cp = np.asarray(w_conv_proj, f)[0]; w_o = np.asarray(w_out, f)[0]
    norm_w = np.asarray(norm_w, f)[0]; qn = np.asarray(q_norm_w, f)[0]; kn = np.asarray(k_norm_w, f)[0]
    conv_w = np.asarray(conv_w, f)[0]

    ar = np.arange(128)
    wqkv = np.empty((NH, 3, 128, 16, 384), f)
    for h in range(NH):
        for g in range(3):
            cols = np.concatenate([g * 3072 + s * 1024 + h * 128 + ar for s in range(3)])
            wqkv[h, g] = _slabs(w_in, cols)
    wag = np.stack([_slabs(w_in, OFF_AGATE + h * 128 + ar) for h in range(NH)])
    wcv = np.empty((16, 4, 128, 16, 128), f)
    for j in range(16):
        wcv[j, 0] = _slabs(w_in, OFF_CONV + j * 128 + ar)
        wcv[j, 1] = _slabs(w_in, OFF_CONV + 4096 + j * 128 + ar)
        wcv[j, 2] = _slabs(w_in, OFF_CONV + 2048 + j * 128 + ar)
        wcv[j, 3] = _slabs(w_in, OFF_CGATE + j * 128 + ar)
    wml = np.empty((16, 2, 128, 16, 128), f)
    for i in range(16):
        wml[i, 0] = _slabs(w_in, OFF_MERGE + i * 128 + ar)
        wml[i, 1] = _slabs(w_in, OFF_MERGE + 2048 + i * 128 + ar)
    watt = np.stack([_slabs(w_att, i * 128 + ar, 8) for i in range(16)])
    wcp = np.stack([_slabs(w_cp, i * 128 + ar) for i in range(16)])
    wout = np.stack([_slabs(w_o, cb * 512 + np.arange(512)) for cb in range(4)])
    nw = np.ascontiguousarray(np.broadcast_to(norm_w[None, :], (128, D_MODEL)))
    qkw = np.ascontiguousarray(np.broadcast_to(np.stack([qn, kn], axis=1)[None], (128, 3, 2, 128)))
    cw = np.ascontiguousarray(conv_w.reshape(3, 16, 128).transpose(2, 1, 0))
    ident = np.eye(128, dtype=f)

    in_maps = []
    for c in range(NCORES):
        b, q = c // 4, c % 4
        c0 = q * TOK
        xe = np.zeros((4096 + NSAMP, D_MODEL), f)
        if q > 0:
            xe[0:2048] = x_prompt[b, c0 - 2048:c0]
        xe[2048:4096] = x_prompt[b, c0:c0 + TOK]
        xe[4096:] = x_sample[4 * c:4 * c + 4].reshape(NSAMP, D_MODEL)
        masks, smask, nmask = _const_masks(1.0 if q > 0 else 0.0)
        sc = state_conv[4 * c:4 * c + 4]
        scT = np.ascontiguousarray(sc.reshape(4, 2, 16, 128).transpose(3, 2, 0, 1))
        m = {"x": xe, "scT": scT, "wqkv": wqkv, "wag": wag, "wcv": wcv, "wml": wml, "watt": watt, "wcp": wcp,
             "wout": wout, "nw": nw, "qkw": qkw, "cw": cw, "rope": _rope_tables(c0), "masks": masks,
             "smask": smask, "nmask": nmask, "ident": ident}
        for g in range(3):
            m["ck%d" % g] = np.ascontiguousarray(caches[g][4 * c:4 * c + 4].reshape(4, CACHE_LEN[g], 2, 1024))
        in_maps.append(m)

    return in_maps


def _assemble(R):
    f = np.float32
    y_p = np.empty((2, SEQ, D_MODEL), f)
    y_s = np.empty((32, 4, D_MODEL), f)
    for c in range(NCORES):
        b, q = c // 4, c % 4
        y_p[b, q * TOK:(q + 1) * TOK] = R[c]["y"][:TOK]
        y_s[4 * c:4 * c + 4] = R[c]["y"][TOK:].reshape(4, 4, D_MODEL)
    kvp = []
    for g in range(3):
        L = CACHE_LEN[g]
        kvp.append(np.stack([R[3]["kv%d" % g], R[7]["kv%d" % g]]).reshape(1, 2, L, 2, NH, HD))
    ncp = np.stack([R[3]["ncv"], R[7]["ncv"]])
    ncp = np.ascontiguousarray(ncp.transpose(0, 3, 2, 1)).reshape(1, 2, 2, D_MODEL)
    kvs = []
    for g in range(3):
        L = CACHE_LEN[g]
        kvs.append(np.concatenate([R[c]["skv%d" % g] for c in range(NCORES)], axis=0).reshape(1, 32, L, 2, NH, HD))
    ncs = np.concatenate([np.ascontiguousarray(R[c]["sncv"].transpose(2, 3, 1, 0)).reshape(4, 2, D_MODEL) for c in range(NCORES)],
                         axis=0).reshape(1, 32, 2, D_MODEL)
    return (y_p, y_s, kvp[0], kvp[1], kvp[2], ncp, kvs[0], kvs[1], kvs[2], ncs)


def kernel(**inputs):
    in_maps = _prepare(**inputs)
    if "nc" not in _NC_CACHE:
        _NC_CACHE["nc"] = build_program()
    res = run_bass_kernel_spmd(_NC_CACHE["nc"], in_maps, core_ids=list(range(NCORES)))
    return _assemble(res.results)
```
